# Optimizing a Trainium2 kernel written in Bass

```python
import jax, jax.numpy as jnp
from jax import lax
import numpy as np

D_MODEL = 1024
BATCH = 8
SEQ = 4096
DEPTH = 1

GRID_W = 64
EPS = 1e-6
NEG_INF = -1e30

NA_HEADS = 8
NA_HEAD_DIM = 64
NA_WIN_H = 8
NA_WIN_W = 16
NA_QBLOCK_W = 16
NA_BAND_W = NA_QBLOCK_W + NA_WIN_W
NA_WIDTH = NA_HEADS * NA_HEAD_DIM

MLA_HEADS = 8
MLA_QK_NOPE = 64
MLA_QK_ROPE = 32
MLA_V_DIM = 64
MLA_Q_LORA = 256
MLA_KV_LORA = 128
MLA_QBLOCK = 128
MLA_WIDTH = MLA_HEADS * MLA_V_DIM
ROPE_THETA = 10000.0

MIX_WIDTH = NA_WIDTH + MLA_WIDTH
D_IN = 3 * NA_WIDTH + MLA_Q_LORA + MLA_KV_LORA + MLA_QK_ROPE
D_FF = 2816

kernel_name = "hybrid_na_mla_macaron_block"


def rms_norm(x, g):
    xf = x.astype(jnp.float32)
    y = xf * lax.rsqrt(jnp.mean(xf * xf, axis=-1, keepdims=True) + EPS)
    return (y * g.astype(jnp.float32)).astype(x.dtype)


def swiglu(h, w_gu, w_down):
    gate, up = jnp.split(h @ w_gu, 2, axis=-1)
    return (jax.nn.silu(gate) * up) @ w_down


def grid_rope_tables(seq):
    t = jnp.arange(seq)
    row = (t // GRID_W).astype(jnp.float32)
    col = (t % GRID_W).astype(jnp.float32)
    n_freq = MLA_QK_ROPE // 4
    inv_freq = 1.0 / (ROPE_THETA ** (jnp.arange(n_freq, dtype=jnp.float32) / n_freq))
    ang = jnp.concatenate([row[:, None] * inv_freq[None, :], col[:, None] * inv_freq[None, :]], axis=-1)
    return jnp.cos(ang), jnp.sin(ang)


def apply_rope(x, cos, sin):
    xf = x.astype(jnp.float32)
    x1, x2 = jnp.split(xf, 2, axis=-1)
    return jnp.concatenate([x1 * cos - x2 * sin, x2 * cos + x1 * sin], axis=-1).astype(x.dtype)


def neighbourhood_attention(q, k, v, rpb):
    B, S, H, D = q.shape
    R = S // GRID_W
    KH = min(NA_WIN_H, R)
    NCB = GRID_W // NA_QBLOCK_W
    rows = jnp.arange(R)
    row_start = jnp.clip(rows - KH // 2, 0, R - KH)
    key_rows = row_start[:, None] + jnp.arange(KH)
    c0 = jnp.arange(NCB) * NA_QBLOCK_W
    band_start = jnp.clip(c0 - NA_WIN_W // 2, 0, GRID_W - NA_BAND_W)
    key_cols = band_start[:, None] + jnp.arange(NA_BAND_W)
    q_cols = c0[:, None] + jnp.arange(NA_QBLOCK_W)
    win_start = jnp.clip(q_cols - NA_WIN_W // 2, 0, GRID_W - NA_WIN_W)
    kc = key_cols[:, None, :]
    col_mask = (kc >= win_start[..., None]) & (kc < win_start[..., None] + NA_WIN_W)
    row_off = key_rows - rows[:, None] + (NA_WIN_H - 1)
    col_off = jnp.clip(kc - q_cols[:, :, None] + (NA_WIN_W - 1), 0, 2 * NA_WIN_W - 2)
    bias = rpb.astype(jnp.float32)[:, row_off[:, None, None, :, None], col_off[None, :, :, None, :]]
    bias = bias.transpose(1, 2, 0, 3, 4, 5)
    bias = jnp.where(col_mask[:, None, :, None, :], bias, NEG_INF)
    qg = q.reshape(B, R, NCB, NA_QBLOCK_W, H, D)
    ridx = key_rows[:, None, :, None]
    cidx = key_cols[None, :, None, :]
    kg = k.reshape(B, R, GRID_W, H, D)[:, ridx, cidx]
    vg = v.reshape(B, R, GRID_W, H, D)[:, ridx, cidx]
    s = jnp.einsum('brnqhd,brnijhd->brnhqij', qg, kg).astype(jnp.float32) * (D ** -0.5)
    s = s + bias[None]
    sh = s.shape
    p = jax.nn.softmax(s.reshape(sh[:-2] + (sh[-2] * sh[-1],)), axis=-1).reshape(sh)
    o = jnp.einsum('brnhqij,brnijhd->brnqhd', p.astype(v.dtype), vg)
    return o.reshape(B, S, H * D)


def latent_attention(c_q, c_kv, k_rope, q_norm_g, w_uq, kv_norm_g, w_ukv, cos, sin):
    B, S, _ = c_q.shape
    H = MLA_HEADS
    q = (rms_norm(c_q, q_norm_g) @ w_uq).reshape(B, S, H, MLA_QK_NOPE + MLA_QK_ROPE)
    q_nope, q_rot = jnp.split(q, [MLA_QK_NOPE], axis=-1)
    kv = (rms_norm(c_kv, kv_norm_g) @ w_ukv).reshape(B, S, H, MLA_QK_NOPE + MLA_V_DIM)
    k_nope, v = jnp.split(kv, [MLA_QK_NOPE], axis=-1)
    q_rot = apply_rope(q_rot, cos[:, None, :], sin[:, None, :])
    k_rot = apply_rope(k_rope, cos, sin)
    scale = (MLA_QK_NOPE + MLA_QK_ROPE) ** -0.5
    nqb = S // MLA_QBLOCK

    def block(args):
        qn, qr = args
        s = (jnp.einsum('bqhd,bkhd->bhqk', qn, k_nope)
             + jnp.einsum('bqhd,bkd->bhqk', qr, k_rot)).astype(jnp.float32) * scale
        p = jax.nn.softmax(s, axis=-1)
        return jnp.einsum('bhqk,bkhd->bqhd', p.astype(v.dtype), v)

    qn_b = q_nope.reshape(B, nqb, MLA_QBLOCK, H, MLA_QK_NOPE).swapaxes(0, 1)
    qr_b = q_rot.reshape(B, nqb, MLA_QBLOCK, H, MLA_QK_ROPE).swapaxes(0, 1)
    o = lax.map(block, (qn_b, qr_b))
    return o.swapaxes(0, 1).reshape(B, S, H * MLA_V_DIM)


def setup_inputs(seed: int = 0) -> dict:
    key = jax.random.key(seed)
    ks = iter(jax.random.split(key, 32))

    def w(shape, fan_in):
        return jax.random.normal(next(ks), shape, jnp.float32) * (fan_in ** -0.5)

    def gain(n):
        return 1.0 + 0.05 * jax.random.normal(next(ks), (DEPTH, n), jnp.float32)

    L = DEPTH
    return {
        "x": jax.random.normal(next(ks), (BATCH, SEQ, D_MODEL), jnp.float32),
        "ffn1_pre_g": gain(D_MODEL),
        "ffn1_w_gu": w((L, D_MODEL, 2 * D_FF), D_MODEL),
        "ffn1_w_down": w((L, D_FF, D_MODEL), D_FF),
        "ffn1_post_g": gain(D_MODEL),
        "mix_pre_g": gain(D_MODEL),
        "w_in": w((L, D_MODEL, D_IN), D_MODEL),
        "na_rpb": 0.1 * jax.random.normal(next(ks), (L, NA_HEADS, 2 * NA_WIN_H - 1, 2 * NA_WIN_W - 1), jnp.float32),
        "mla_q_norm_g": gain(MLA_Q_LORA),
        "mla_w_uq": w((L, MLA_Q_LORA, MLA_HEADS * (MLA_QK_NOPE + MLA_QK_ROPE)), MLA_Q_LORA),
        "mla_kv_norm_g": gain(MLA_KV_LORA),
        "mla_w_ukv": w((L, MLA_KV_LORA, MLA_HEADS * (MLA_QK_NOPE + MLA_V_DIM)), MLA_KV_LORA),
        "na_out_norm_g": gain(NA_WIDTH),
        "mla_out_norm_g": gain(MLA_WIDTH),
        "w_out": w((L, MIX_WIDTH, D_MODEL), MIX_WIDTH),
        "mix_post_g": gain(D_MODEL),
        "ffn2_pre_g": gain(D_MODEL),
        "ffn2_w_gu": w((L, D_MODEL, 2 * D_FF), D_MODEL),
        "ffn2_w_down": w((L, D_FF, D_MODEL), D_FF),
        "ffn2_post_g": gain(D_MODEL),
    }


def reference(x, ffn1_pre_g, ffn1_w_gu, ffn1_w_down, ffn1_post_g, mix_pre_g, w_in, na_rpb,
              mla_q_norm_g, mla_w_uq, mla_kv_norm_g, mla_w_ukv, na_out_norm_g, mla_out_norm_g,
              w_out, mix_post_g, ffn2_pre_g, ffn2_w_gu, ffn2_w_down, ffn2_post_g):
    B, S, _ = x.shape
    cos, sin = grid_rope_tables(S)
    split_at = [NA_WIDTH, 2 * NA_WIDTH, 3 * NA_WIDTH,
                3 * NA_WIDTH + MLA_Q_LORA, 3 * NA_WIDTH + MLA_Q_LORA + MLA_KV_LORA]
    h = x
    for l in range(DEPTH):
        f = swiglu(rms_norm(h, ffn1_pre_g[l]), ffn1_w_gu[l], ffn1_w_down[l])
        h = h + 0.5 * rms_norm(f, ffn1_post_g[l])
        z = rms_norm(h, mix_pre_g[l]) @ w_in[l]
        q_na, k_na, v_na, c_q, c_kv, k_rope = jnp.split(z, split_at, axis=-1)
        hs = (B, S, NA_HEADS, NA_HEAD_DIM)
        o_na = neighbourhood_attention(q_na.reshape(hs), k_na.reshape(hs), v_na.reshape(hs), na_rpb[l])
        o_mla = latent_attention(c_q, c_kv, k_rope, mla_q_norm_g[l], mla_w_uq[l],
                                 mla_kv_norm_g[l], mla_w_ukv[l], cos, sin)
        mixed = jnp.concatenate([rms_norm(o_na, na_out_norm_g[l]),
                                 rms_norm(o_mla, mla_out_norm_g[l])], axis=-1) @ w_out[l]
        h = h + rms_norm(mixed, mix_post_g[l])
        f = swiglu(rms_norm(h, ffn2_pre_g[l]), ffn2_w_gu[l], ffn2_w_down[l])
        h = h + 0.5 * rms_norm(f, ffn2_post_g[l])
    return h
```

```python
import numpy as np
from contextlib import ExitStack
import concourse.bass as bass
import concourse.mybir as mybir
from concourse.bass_utils import run_bass_kernel_spmd

F32 = mybir.dt.float32
BF16 = mybir.dt.bfloat16
AF = mybir.ActivationFunctionType
ALU = mybir.AluOpType

S = 4096
D = 1024
DFF = 2816
NFC = DFF // 128
KC = D // 128
BLK = 512
NBLK = S // BLK
EPS = 1e-6
NEG = -1e30


class Tracker:
    def __init__(self, nc, es):
        self.nc = nc
        self.es = es
        self.eng = dict(pe=nc.tensor, act=nc.scalar, dve=nc.vector, pool=nc.gpsimd, sp=nc.sync)
        self.sem = {k: es.enter_context(nc.semaphore("s_" + k)) for k in ("pe", "act", "dve", "pool")}
        self.cnt = {k: 0 for k in self.sem}
        self.pending = {k: False for k in self.sem}
        self.seen = {k: {} for k in self.eng}
        self.res = {}
        self.dsem = {}
        self.dcnt = {}
        self.semobj = {}

    def _sid(self, s):
        self.semobj[id(s)] = s
        return id(s)

    def _deps(self, reads, writes):
        deps = []
        for k in reads:
            r = self.res.get(k)
            if r and r[0]:
                deps.append(r[0])
        for k in writes:
            r = self.res.get(k)
            if r:
                if r[0]:
                    deps.append(r[0])
                deps.extend(r[1])
        return deps

    def _wait(self, e, deps):
        best = {}
        for (s, v) in deps:
            sid = self._sid(s)
            if e == "pe" and s is self.sem["pe"]:
                continue
            if v > best.get(sid, 0):
                best[sid] = v
        for sid, v in best.items():
            if self.seen[e].get(sid, 0) >= v:
                continue
            self.eng[e].wait_ge(self.semobj[sid], v)
            self.seen[e][sid] = v

    def _record(self, tok, reads, writes):
        for k in reads:
            r = self.res.setdefault(k, [None, []])
            r[1].append(tok)
            if len(r[1]) > 32:
                r[1] = self._compact(r[1])
        for k in writes:
            self.res[k] = [tok, []]

    @staticmethod
    def _compact(lst):
        best = {}
        for (s, v) in lst:
            if v > best.get(id(s), (None, 0))[1]:
                best[id(s)] = (s, v)
        return list(best.values())

    def op(self, e, fn, reads=(), writes=(), sig=True):
        self._wait(e, self._deps(reads, writes))
        ins = fn(self.eng[e])
        if sig:
            self.cnt[e] += 1
            ins.then_inc(self.sem[e], 1)
            self.pending[e] = False
            tok = (self.sem[e], self.cnt[e])
        else:
            self.pending[e] = True
            tok = (self.sem[e], self.cnt[e] + 1)
        self._record(tok, reads, writes)
        return ins

    def dma(self, q, out, in_, reads=(), writes=(), key=None):
        if key is None:
            key = (tuple(writes) + tuple(reads))[0]
        if key not in self.dsem:
            self.dsem[key] = self.es.enter_context(self.nc.semaphore("d%d" % len(self.dsem)))
            self.dcnt[key] = 0
        self._wait(q, self._deps(reads, writes))
        ins = self.eng[q].dma_start(out=out, in_=in_)
        self.dcnt[key] += 16
        ins.then_inc(self.dsem[key], 16)
        self._record((self.dsem[key], self.dcnt[key]), reads, writes)
        return ins

    def barrier(self, engines=("pe", "act", "dve", "pool", "sp")):
        for e in self.sem:
            assert not self.pending[e], e
        toks = [(self.sem[e], self.cnt[e]) for e in self.sem if self.cnt[e] > 0]
        toks += [(self.dsem[k], self.dcnt[k]) for k in self.dsem]
        for e in engines:
            best = {}
            for (s, v) in toks:
                best[self._sid(s)] = v
            for sid, v in best.items():
                if self.seen[e].get(sid, 0) >= v:
                    continue
                self.eng[e].wait_ge(self.semobj[sid], v)
                self.seen[e][sid] = v
        self.res = {}


def _sb(nc, es, name, shape, dt):
    return es.enter_context(nc.sbuf_tensor(name, list(shape), dt))


def _ps(nc, es, name, shape, dt):
    return es.enter_context(nc.psum_tensor(name, list(shape), dt))


def ffn_alloc_weights(nc, es, ph):
    Wgu = _sb(nc, es, ph + "Wgu", [128, KC, 2 * DFF], BF16)
    Wdn = _sb(nc, es, ph + "Wdn", [128, NFC, D], BF16)
    return Wgu, Wdn


def ffn_load_weights(T, Wgu, Wdn, w_gu, w_dn):
    wgu_v = w_gu.rearrange("(k p) f -> p k f", p=128)
    wdn_v = w_dn.rearrange("(c p) d -> p c d", p=128)
    for j in range(6):
        c0, c1 = j * 512, min((j + 1) * 512, DFF)
        for half in range(2):
            o = half * DFF
            T.dma("pool", Wgu[:, :, o + c0:o + c1], wgu_v[:, :, o + c0:o + c1],
                  writes=[("Wgu", half, j)], key=("Wgu", half, j))
    for j in range(2):
        T.dma("pool", Wdn[:, j * 11:(j + 1) * 11, :], wdn_v[:, j * 11:(j + 1) * 11, :],
              writes=[("Wdn", j)], key=("Wdn", j))


def ffn_phase(nc, T, ph, src, dst, w_gu, w_dn, gcols, gc0, grow, ident, eps1, eps4, W=None):
    with ExitStack() as es:
        if W is None:
            Wgu, Wdn = ffn_alloc_weights(nc, es, ph)
            ffn_load_weights(T, Wgu, Wdn, w_gu, w_dn)
        else:
            Wgu, Wdn = W
        Sx = [_sb(nc, es, ph + "Sx%d" % i, [128, D], F32) for i in range(2)]
        Rx = [_sb(nc, es, ph + "Rx%d" % i, [128, D], F32) for i in range(2)]
        Tt = [_sb(nc, es, ph + "Tt%d" % i, [128, D], F32) for i in range(2)]
        xn = [_sb(nc, es, ph + "xn%d" % i, [128, D], BF16) for i in range(4)]
        xnT = _sb(nc, es, ph + "xnT", [128, KC, BLK], BF16)
        hT = _sb(nc, es, ph + "hT", [128, NFC, BLK], BF16)
        sg = [_sb(nc, es, ph + "sg%d" % i, [128, BLK], BF16) for i in range(2)]
        gpost = _sb(nc, es, ph + "gpost", [128, D], F32)
        junk = _sb(nc, es, ph + "junk", [128, D], BF16)
        st = _sb(nc, es, ph + "st", [128, 64], F32)
        pT2 = [_ps(nc, es, ph + "pT%d" % i, [128, KC, 128], BF16) for i in range(2)]
        pG = [_ps(nc, es, ph + "pG%d" % i, [128, BLK], F32) for i in range(4)]
        pD = [_ps(nc, es, ph + "pD%d" % i, [128, BLK], F32) for i in range(2)]

        T.dma("sp", gpost[:], grow.partition_broadcast(128), writes=["gpost"])

        def wgu_key(c):
            return c * 128 // 512

        def norm_stage(b):
            for i in range(4):
                t = b * 4 + i
                sl = t % 2
                T.dma("sp", Sx[sl][:], src[t * 128:(t + 1) * 128, :], writes=[("Sx", sl)])
                col = (b % 2) * 8 + i
                T.op("act", lambda e: e.activation(out=junk[:], in_=Sx[sl][:], func=AF.Square,
                                                   scale=float(D ** -0.5), accum_out=st[:, col:col + 1]),
                     reads=[("Sx", sl)], writes=["junk", ("st", col)])
                T.op("act", lambda e: e.activation(out=st[:, col + 4:col + 5], in_=st[:, col:col + 1], func=AF.Sqrt,
                                                   bias=eps1[:, 0:1], scale=1.0),
                     reads=[("st", col)], writes=[("st", col + 4)])
                T.op("dve", lambda e: e.reciprocal(out=st[:, col + 4:col + 5], in_=st[:, col + 4:col + 5]),
                     reads=[("st", col + 4)], writes=[("st", col + 4)])
                T.op("dve", lambda e: e.tensor_scalar(out=xn[i][:], in0=Sx[sl][:], scalar1=st[:, col + 4:col + 5],
                                                      scalar2=None, op0=ALU.mult),
                     reads=[("Sx", sl), ("st", col + 4)], writes=[("xn", i)])

        def transpose_stage(b):
            for i in range(4):
                pT = pT2[i % 2]
                for k in range(KC):
                    T.op("pe", lambda e: e.transpose(pT[:, k, :], xn[i][:, k * 128:(k + 1) * 128], ident[:]),
                         reads=[("xn", i)], writes=[("pT", i % 2)], sig=(k == KC - 1))
                T.op("dve", lambda e: e.tensor_tensor(
                    out=xnT[:, :, i * 128:(i + 1) * 128], in0=pT[:, :, :],
                    in1=gcols[:, gc0:gc0 + KC].unsqueeze(2).to_broadcast([128, KC, 128]), op=ALU.mult),
                    writes=[("pT", i % 2), ("xnT", i)])

        def gu_chunk(b, c):
            pg, pu = pG[(c % 2) * 2], pG[(c % 2) * 2 + 1]
            for half, pp in ((0, pg), (1, pu)):
                o = half * DFF + c * 128
                for k in range(KC):
                    T.op("pe", lambda e: e.matmul(pp[:], lhsT=Wgu[:, k, o:o + 128], rhs=xnT[:, k, :],
                                                  start=(k == 0), stop=(k == KC - 1)),
                         reads=[("Wgu", half, wgu_key(c))] + [("xnT", i) for i in range(4)],
                         writes=[("pG", (c % 2) * 2 + half)], sig=(k == KC - 1))
            s = sg[c % 2]
            T.op("act", lambda e: e.activation(out=s[:], in_=pg[:], func=AF.Silu),
                 reads=[("pG", (c % 2) * 2)], writes=[("sg", c % 2)])
            T.op("dve", lambda e: e.tensor_tensor(out=hT[:, c, :], in0=s[:], in1=pu[:], op=ALU.mult),
                 reads=[("sg", c % 2), ("pG", (c % 2) * 2 + 1)], writes=[("hT", c)])

        def down_tile(b, i):
            t = b * 4 + i
            sl = t % 2
            T.dma("sp", Rx[sl][:], src[t * 128:(t + 1) * 128, :], writes=[("Rx", sl)])
            col = 16 + (t % 2) * 8
            for n in range(2):
                pd = pD[n]
                for c in range(NFC):
                    T.op("pe", lambda e: e.matmul(pd[:], lhsT=hT[:, c, i * 128:(i + 1) * 128],
                                                  rhs=Wdn[:, c, n * 512:(n + 1) * 512],
                                                  start=(c == 0), stop=(c == NFC - 1)),
                         reads=[("hT", c), ("Wdn", c // 11)], writes=[("pD", n)], sig=(c == NFC - 1))
                T.op("act", lambda e: e.activation(out=junk[:, 0:512], in_=pd[:], func=AF.Square,
                                                   scale=float(D ** -0.5), accum_out=st[:, col + n:col + n + 1]),
                     writes=[("pD", n), "junk", ("st", col + n)])
                T.op("dve", lambda e: e.tensor_tensor(out=Tt[sl][:, n * 512:(n + 1) * 512], in0=pd[:],
                                                      in1=gpost[:, n * 512:(n + 1) * 512], op=ALU.mult),
                     reads=["gpost"], writes=[("pD", n), ("Tt", sl, n)])
            T.op("dve", lambda e: e.tensor_tensor(out=st[:, col + 2:col + 3], in0=st[:, col:col + 1],
                                                  in1=st[:, col + 1:col + 2], op=ALU.add),
                 reads=[("st", col), ("st", col + 1)], writes=[("st", col + 2)])
            T.op("act", lambda e: e.activation(out=st[:, col + 3:col + 4], in_=st[:, col + 2:col + 3], func=AF.Sqrt,
                                               bias=eps4[:, 0:1], scale=4.0),
                 reads=[("st", col + 2)], writes=[("st", col + 3)])
            T.op("dve", lambda e: e.reciprocal(out=st[:, col + 3:col + 4], in_=st[:, col + 3:col + 4]),
                 reads=[("st", col + 3)], writes=[("st", col + 3)])
            T.op("dve", lambda e: e.scalar_tensor_tensor(out=Rx[sl][:], in0=Tt[sl][:], scalar=st[:, col + 3:col + 4],
                                                         in1=Rx[sl][:], op0=ALU.mult, op1=ALU.add),
                 reads=[("Tt", sl, 0), ("Tt", sl, 1), ("st", col + 3), ("Rx", sl)], writes=[("Rx", sl)])
            T.dma("sp", dst[t * 128:(t + 1) * 128, :], Rx[sl][:], reads=[("Rx", sl)], key=("Rxo", sl))

        norm_stage(0)
        transpose_stage(0)
        for b in range(NBLK):
            for c in range(NFC):
                gu_chunk(b, c)
                if c == 8 and b + 1 < NBLK:
                    norm_stage(b + 1)
            if b + 1 < NBLK:
                transpose_stage(b + 1)
            for i in range(4):
                down_tile(b, i)
        T.barrier()


TAB_W = 22 * 64 + 6 * 512 + 6 * 512
FIRST0 = 22 * 64
LAST0 = FIRST0 + 6 * 512
TQ = TAB_W // 4
NA_SUBRANGE = False


def attn_stream(T, items, pS2, PT2, PM2, scale, after_back, bias_ident=None):
    def rng(ch):
        return ch.get("cr", (0, 512))

    def front_pe(k):
        it = items[k]
        s = k % 2
        n = len(it["chunks"])
        for c, ch in enumerate(it["chunks"]):
            c0, c1 = rng(ch)
            ps = pS2[s][:, c * 512 + c0:c * 512 + c1]
            if bias_ident is None:
                T.op("pe", lambda e: ch["qk"](e, ps, True, c0, c1), reads=it["qk_reads"],
                     writes=[("pS", s)], sig=(c == n - 1))
            else:
                T.op("pe", lambda e: ch["qk"](e, ps, False, c0, c1), reads=it["qk_reads"],
                     writes=[("pS", s)], sig=False)
                T.op("pe", lambda e: e.matmul(ps, lhsT=bias_ident[:, :], rhs=ch["mask"][:, c0:c1],
                                              start=False, stop=True),
                     reads=it["mask_reads"], writes=[("pS", s)], sig=(c == n - 1))

    def front_act(k):
        it = items[k]
        s = k % 2
        s3 = k % 3
        n = len(it["chunks"])
        if all(rng(ch) == (0, 512) for ch in it["chunks"]):
            T.op("act", lambda e: e.activation(out=PT2[s3][:, 0:n * 512], in_=pS2[s][:, 0:n * 512], func=AF.Exp,
                                               scale=float(scale)), writes=[("pS", s), ("PT", s3)])
        else:
            for c, ch in enumerate(it["chunks"]):
                c0, c1 = rng(ch)
                T.op("act", lambda e: e.activation(out=PT2[s3][:, c * 512 + c0:c * 512 + c1],
                                                   in_=pS2[s][:, c * 512 + c0:c * 512 + c1], func=AF.Exp,
                                                   scale=float(scale)), writes=[("pS", s), ("PT", s3)])

    for k in range(min(2, len(items))):
        front_pe(k)
        front_act(k)
    deferred = []
    for k in range(len(items)):
        if k + 2 < len(items):
            front_pe(k + 2)
            front_act(k + 2)
        for fn in deferred:
            fn()
        deferred = []
        it = items[k]
        s3 = k % 3
        n = len(it["chunks"])
        for c, ch in enumerate(it["chunks"]):
            c0, c1 = rng(ch)
            T.op("pe", lambda e: e.matmul(it["pO"][0:65, c0:c1], lhsT=ch["v"], rhs=PT2[s3][:, c * 512 + c0:c * 512 + c1],
                                          start=(it["first"] and c == 0), stop=(it["last"] and c == n - 1),
                                          skip_group_check=True),
                 reads=[("PT", s3)] + it["v_reads"], writes=[it["pOkey"]], sig=True)
        after_back(k, it, deferred)
    for fn in deferred:
        fn()


def attn_finalize(T, pO, pOkey, pF, pFkey, oT, otok, rc, identF, par, dst_ap, deferred):
    T.op("dve", lambda e: e.tensor_copy(out=oT[par][0:65, :], in_=pO[0:65, :]), writes=[pOkey, ("oT", par)])

    def late():
        for i in range(4):
            T.op("pe", lambda e: e.transpose(pF[:, i, 0:65], oT[par][0:65, i * 128:(i + 1) * 128], identF[0:65, 0:65]),
                 reads=[("oT", par)], writes=[pFkey], sig=(i == 3))
        rcv = rc[:, par * 4:par * 4 + 4]
        T.op("dve", lambda e: e.reciprocal(out=rcv.unsqueeze(2), in_=pF[:, :, 64:65]), writes=[pFkey, ("rc", par)])
        T.op("dve", lambda e: e.tensor_tensor(out=otok[par][:, :, :], in0=pF[:, :, 0:64],
                                              in1=rcv.unsqueeze(2).to_broadcast([128, 4, 64]), op=ALU.mult),
             reads=[("rc", par)], writes=[pFkey, ("otok", par)])
        T.dma("sp", dst_ap, otok[par][:, :, :], reads=[("otok", par)], key=("otok_o", par))
    deferred.append(late)


def mixer_phase(nc, T, C):
    gcols, ident, identF, eps1, ones_bf = C["gcols"], C["ident"], C["identF"], C["eps1"], C["ones_bf"]
    h_scr, oscr = C["h_scr"], C["oscr"]
    with ExitStack() as es0:
        cqnT = _sb(nc, es0, "cqnT", [128, 2, S], BF16)
        ckvnT = _sb(nc, es0, "ckvnT", [128, S], BF16)
        krotT = _sb(nc, es0, "krotT", [96, S], BF16)
        rope = None
        with ExitStack() as esA:
            qT_na = [_sb(nc, esA, "qTna%d" % i, [128, S], BF16) for i in range(4)]
            kT_na = [_sb(nc, esA, "kTna%d" % i, [128, S], BF16) for i in range(4)]
            V_na = _sb(nc, esA, "Vna", [128, 32, 8, 66], BF16)
            T.op("pool", lambda e: e.memset(V_na[:, :, :, 64:66], 1.0), writes=["Vna1"])
            with ExitStack() as es:
                p2_proj(nc, T, C, es, cqnT, ckvnT, krotT, rope, qT_na, kT_na, V_na)
            T.barrier()
            with ExitStack() as es:
                p3_na(nc, T, C, es, qT_na, kT_na, V_na)
            T.barrier()
        with ExitStack() as es:
            p4_mla(nc, T, C, es, cqnT, ckvnT, krotT, rope)
        T.barrier()


B2 = 256
NB2 = S // B2
TP2 = B2 // 128


def p2_proj(nc, T, C, es, cqnT, ckvnT, krotT, rope, qT_na, kT_na, V_na):
    gcols, ident, eps1, ones_bf, h_scr = C["gcols"], C["ident"], C["eps1"], C["ones_bf"], C["h_scr"]
    Wna = _sb(nc, es, "Wna", [128, KC, 1536], BF16)
    Wlat = _sb(nc, es, "Wlat", [128, KC, 576], BF16)
    Sx = [_sb(nc, es, "p2Sx%d" % i, [128, D], F32) for i in range(2)]
    xn = [_sb(nc, es, "p2xn%d" % i, [128, D], BF16) for i in range(2)]
    xnT2 = [_sb(nc, es, "p2xnT%d" % i, [128, KC, B2], BF16) for i in range(2)]
    st = _sb(nc, es, "p2st", [128, 64], F32)
    sq = _sb(nc, es, "p2sq", [128, 2, 512], BF16)
    rs = _sb(nc, es, "p2rs", [128, B2], F32)
    tmp = _sb(nc, es, "p2tmp", [96, 2, B2], F32)
    ropeb = [_sb(nc, es, "p2rope%d" % i, [96, 2, B2], F32) for i in range(3)]
    pT2 = [_ps(nc, es, "p2pT%d" % i, [128, KC, 128], BF16) for i in range(2)]
    pA = [_ps(nc, es, "p2pA%d" % i, [128, BLK], F32) for i in range(2)]
    pC = [_ps(nc, es, "p2pC%d" % i, [128, BLK], F32) for i in range(2)]
    pSm = _ps(nc, es, "p2pS", [128, BLK], F32)
    pK1 = _ps(nc, es, "p2pK1", [128, BLK], F32)
    pK = [pC[1], pC[0]]
    pKkey = [("pC", 1), ("pC", 0)]
    CQB = [(pC[0], ("pC", 0)), (pC[1], ("pC", 1))]
    CKVB = [(pK1, ("pK", 1))]
    junk = sq[:, :, :].rearrange("p a b -> p (a b)")

    wna_v = C["w_na"].rearrange("(k p) f -> p k f", p=128)
    for j in range(3):
        T.dma("pool", Wna[:, :, j * 512:(j + 1) * 512], wna_v[:, :, j * 512:(j + 1) * 512],
              writes=[("Wna", j)], key=("Wgu", 0, j))
    T.dma("pool", Wlat[:, :, :], C["w_lat"].rearrange("(k p) f -> p k f", p=128), writes=["Wlat"], key=("Wgu", 0, 3))

    def norm_stage(b):
        blk = slice(b * B2, (b + 1) * B2)
        T.dma("sp", ropeb[b % 3][64:96, 0, :], C["cosT"][:, blk], writes=[("ropeb", b % 3, 0)])
        T.dma("sp", ropeb[b % 3][64:96, 1, :], C["sinT"][:, blk], writes=[("ropeb", b % 3, 1)])
        for i in range(TP2):
            t = b * TP2 + i
            sl = t % 2
            T.dma("sp", Sx[sl][:], h_scr[t * 128:(t + 1) * 128, :], writes=[("Sx", sl)])
            col = (b % 2) * 8 + i
            T.op("act", lambda e: e.activation(out=junk, in_=Sx[sl][:], func=AF.Square,
                                               scale=float(D ** -0.5), accum_out=st[:, col:col + 1]),
                 reads=[("Sx", sl)], writes=[("sq", 0), ("sq", 1), ("st", col)])
            T.op("act", lambda e: e.activation(out=st[:, col + 4:col + 5], in_=st[:, col:col + 1], func=AF.Sqrt,
                                               bias=eps1[:, 0:1], scale=1.0),
                 reads=[("st", col)], writes=[("st", col + 4)])
            T.op("dve", lambda e: e.reciprocal(out=st[:, col + 4:col + 5], in_=st[:, col + 4:col + 5]),
                 reads=[("st", col + 4)], writes=[("st", col + 4)])
            T.op("dve", lambda e: e.tensor_scalar(out=xn[i][:], in0=Sx[sl][:], scalar1=st[:, col + 4:col + 5],
                                                  scalar2=None, op0=ALU.mult),
                 reads=[("Sx", sl), ("st", col + 4)], writes=[("xn", i)])

    def transpose_stage(b):
        xnT = xnT2[b % 2]
        for i in range(TP2):
            pT = pT2[i % 2]
            for k in range(KC):
                T.op("pe", lambda e: e.transpose(pT[:, k, :], xn[i][:, k * 128:(k + 1) * 128], ident[:]),
                     reads=[("xn", i)], writes=[("pT", i % 2)], sig=(k == KC - 1))
            T.op("dve", lambda e: e.tensor_tensor(
                out=xnT[:, :, i * 128:(i + 1) * 128], in0=pT[:, :, :],
                in1=gcols[:, 8:16].unsqueeze(2).to_broadcast([128, KC, 128]), op=ALU.mult),
                writes=[("pT", i % 2), ("xnT", b % 2, i)])

    nev = [0]

    def evac(out, in_, reads, writes):
        nev[0] += 1
        if nev[0] % 2:
            T.op("act", lambda e: e.copy(out=out, in_=in_), reads=reads, writes=writes)
        else:
            T.op("dve", lambda e: e.tensor_copy(out=out, in_=in_), reads=reads, writes=writes)

    def lat_mm(b, nch, c0, banks):
        xnT = xnT2[b % 2]
        XR = [("xnT", b % 2, i) for i in range(TP2)]
        for ch in range(nch):
            pb, pkey = banks[ch]
            for k in range(KC):
                T.op("pe", lambda e: e.matmul(pb[:, 0:B2], lhsT=Wlat[:, k, c0 + ch * 128:c0 + (ch + 1) * 128],
                                              rhs=xnT[:, k, :], start=(k == 0), stop=(k == KC - 1)),
                     reads=["Wlat"] + XR, writes=[pkey], sig=(k == KC - 1))
            T.op("act", lambda e: e.activation(out=sq[:, ch, 0:B2], in_=pb[:, 0:B2], func=AF.Square),
                 writes=[pkey, ("sq", ch)])

    def lat_sum(b, nch, nfeat):
        for ch in range(nch):
            T.op("pe", lambda e: e.matmul(pSm[:, 0:B2], lhsT=ones_bf[:, :], rhs=sq[:, ch, 0:B2],
                                          start=(ch == 0), stop=(ch == nch - 1)),
                 reads=[("sq", ch)], writes=["pSm"], sig=(ch == nch - 1))
        T.op("act", lambda e: e.activation(out=rs[:], in_=pSm[:, 0:B2], func=AF.Sqrt, bias=eps1[:, 0:1],
                                           scale=1.0 / nfeat), writes=["pSm", "rs"])
        T.op("dve", lambda e: e.reciprocal(out=rs[:], in_=rs[:]), writes=["rs"])

    def lat_scale(b, nch, c0, gcol0, dstf, banks):
        blk = slice(b * B2, (b + 1) * B2)
        for ch in range(nch):
            pb, pkey = banks[ch]
            T.op("dve", lambda e: e.scalar_tensor_tensor(out=dstf(ch)[:, blk], in0=pb[:, 0:B2],
                                                         scalar=gcols[:, gcol0 + ch:gcol0 + ch + 1], in1=rs[:],
                                                         op0=ALU.mult, op1=ALU.mult),
                 reads=["rs"], writes=[pkey, ("lat", c0, ch)])

    def na_qk(b, fcs):
        blk = slice(b * B2, (b + 1) * B2)
        xnT = xnT2[b % 2]
        XR = [("xnT", b % 2, i) for i in range(TP2)]
        for fc in fcs:
            pa = pA[fc % 2]
            for k in range(KC):
                T.op("pe", lambda e: e.matmul(pa[:, 0:B2], lhsT=Wna[:, k, fc * 128:(fc + 1) * 128], rhs=xnT[:, k, :],
                                              start=(k == 0), stop=(k == KC - 1)),
                     reads=[("Wna", fc // 4)] + XR, writes=[("pA", fc % 2)], sig=(k == KC - 1))
            dstT = (qT_na if fc < 4 else kT_na)[fc % 4]
            evac(dstT[:, blk], pa[:, 0:B2], [], [("pA", fc % 2), ("qk", fc)])

    def na_v(b):
        xnT = xnT2[b % 2]
        for i in range(TP2):
            pa = pA[i % 2]
            for k in range(KC):
                T.op("pe", lambda e: e.matmul(pa[:], lhsT=xnT[:, k, i * 128:(i + 1) * 128], rhs=Wna[:, k, 1024:1536],
                                              start=(k == 0), stop=(k == KC - 1)),
                     reads=[("Wna", 2), ("xnT", b % 2, i)], writes=[("pA", i % 2)], sig=(k == KC - 1))
            evac(V_na[:, b * TP2 + i, :, 0:64], pa[:, :].rearrange("p (h d) -> p h d", h=8), [],
                 [("pA", i % 2), ("vna", i)])

    def k_rope(b):
        blk = slice(b * B2, (b + 1) * B2)
        xnT = xnT2[b % 2]
        XR = [("xnT", b % 2, i) for i in range(TP2)]
        for v in range(2):
            for k in range(KC):
                T.op("pe", lambda e: e.matmul(pK[v][0:96, 0:B2], lhsT=Wlat[:, k, 384 + v * 96:480 + v * 96],
                                              rhs=xnT[:, k, :], start=(k == 0), stop=(k == KC - 1)),
                     reads=["Wlat"] + XR, writes=[pKkey[v]], sig=(k == KC - 1))
            T.op("dve", lambda e: e.tensor_tensor(out=tmp[64:96, v, :], in0=pK[v][64:96, 0:B2],
                                                  in1=ropeb[b % 3][64:96, v, :], op=ALU.mult),
                 reads=[("ropeb", b % 3, v)], writes=[pKkey[v], ("tmp", v)])
        T.op("dve", lambda e: e.tensor_tensor(out=krotT[64:96, blk], in0=tmp[64:96, 0, :], in1=tmp[64:96, 1, :],
                                              op=ALU.add),
             reads=[("tmp", 0), ("tmp", 1)], writes=[("krot", b)])

    def proj_block(b):
        cq = lambda ch: cqnT[:, ch, :]
        ckv = lambda ch: ckvnT
        lat_mm(b, 2, 0, CQB)
        na_qk(b, range(0, 4))
        lat_sum(b, 2, 256.0)
        if b + 2 < NB2:
            norm_stage(b + 2)
        lat_mm(b, 1, 256, CKVB)
        na_qk(b, range(4, 6))
        lat_scale(b, 2, 0, 24, cq, CQB)
        na_qk(b, range(6, 8))
        na_v(b)
        lat_sum(b, 1, 128.0)
        k_rope(b)
        lat_scale(b, 1, 256, 26, ckv, CKVB)

    norm_stage(0)
    transpose_stage(0)
    norm_stage(1)
    transpose_stage(1)
    for b in range(NB2):
        proj_block(b)
        if b + 2 < NB2:
            transpose_stage(b + 2)


def p3_na(nc, T, C, es, qT_na, kT_na, V_na):
    identF, oscr, natab = C["identF"], C["oscr"], C["natab"]
    Eb = [_sb(nc, es, "Eb%d" % i, [128, TAB_W], BF16) for i in range(2)]
    stg = [_sb(nc, es, "ebstg%d" % i, [128, TQ], F32) for i in range(2)]
    PT2 = [_sb(nc, es, "p3PT%d" % i, [128, 2 * BLK], BF16) for i in range(3)]
    oT = [_sb(nc, es, "p3oT%d" % i, [65, BLK], F32) for i in range(2)]
    otok = [_sb(nc, es, "p3otok%d" % i, [128, 4, 64], F32) for i in range(2)]
    rc = _sb(nc, es, "p3rc", [128, 8], F32)
    qz = [_sb(nc, es, "p3qz%d" % i, [128, BLK], BF16) for i in range(2)]
    pS2 = [_ps(nc, es, "p3pS%d" % i, [128, 2 * BLK], F32) for i in range(2)]
    pO = [_ps(nc, es, "p3pO%d" % i, [128, BLK], F32) for i in range(2)]

    def q_prep(gi):
        h, g = divmod(gi, 8)
        fc, hp = h // 2, h % 2
        rows = slice(hp * 64, hp * 64 + 64)
        other = slice((1 - hp) * 64, (1 - hp) * 64 + 64)
        par = gi % 2
        if g < 2:
            T.op("pool", lambda e: e.memset(qz[par][other, :], 0.0), writes=[("qz", par)])
        T.op("pool", lambda e: e.tensor_copy(out=qz[par][rows, :], in_=qT_na[fc][rows, g * BLK:(g + 1) * BLK]),
             writes=[("qz", par)])
    pF = _ps(nc, es, "p3pF", [128, 4, 66], F32)

    def eb_dma(h, q):
        T.dma("sp", stg[q % 2][:], natab[h, :, q * TQ:(q + 1) * TQ], writes=[("stg", q % 2)])

    def eb_exp(h, q):
        T.op("act", lambda e: e.activation(out=Eb[h % 2][:, q * TQ:(q + 1) * TQ], in_=stg[q % 2][:], func=AF.Copy,
                                           scale=8.0),
             reads=[("stg", q % 2)], writes=[("Eb", h % 2, q)])

    def load_eb(h):
        for q in range(4):
            eb_dma(h, q)
            eb_exp(h, q)

    items = []
    gi = 0
    for h in range(8):
        fc, hp = h // 2, h % 2
        rows = slice(hp * 64, hp * 64 + 64)
        eb = Eb[h % 2]
        for g in range(8):
            if g == 0:
                chunks = [(2 * j, FIRST0 + j * 512) for j in range(6)]
            elif g == 7:
                chunks = [(52 + 2 * j, LAST0 + j * 512) for j in range(6)]
            else:
                chunks = [(8 * g - 4 + 2 * j, (14 - 2 * j) * 64) for j in range(8)]
            qblk = slice(g * BLK, (g + 1) * BLK)
            chs = []
            for (kr0, o) in chunks:
                ct = kr0 // 2

                def qk(e, ps, stop, c0, c1, ct=ct, fc=fc, par=gi % 2):
                    return e.matmul(ps, lhsT=kT_na[fc][:, ct * 128:(ct + 1) * 128], rhs=qz[par][:, c0:c1],
                                    start=True, stop=stop)
                chd = dict(qk=qk, v=V_na[:, ct, h, 0:65], mask=eb[:, o:o + 512])
                if NA_SUBRANGE and 1 <= g <= 6:
                    j = len(chs)
                    i0, i1 = max(0, 2 * j - 7), min(7, 2 * j + 1)
                    chd["cr"] = (i0 * 64, (i1 + 1) * 64)
                chs.append(chd)
            n_it = len(chs) // 2
            for a in range(n_it):
                items.append(dict(chunks=chs[2 * a:2 * a + 2], first=(a == 0), last=(a == n_it - 1),
                                  pO=pO[gi % 2], pOkey=("pO", gi % 2), qk_reads=[("qz", gi % 2)], v_reads=[], a=a,
                                  mask_reads=[("Eb", h % 2, q) for q in range(4)], h=h, g=g, gi=gi,
                                  newhead=(g == 0 and a == 0)))
            gi += 1

    load_eb(0)
    q_prep(0)

    def after_back(k, it, deferred):
        if it["gi"] == 0 and it["a"] == 0:
            deferred.append(lambda: (eb_dma(1, 0), eb_dma(1, 1)))
        if it["gi"] == 1 and it["a"] == 0:
            deferred.append(lambda: (eb_exp(1, 0), eb_exp(1, 1), eb_dma(1, 2), eb_dma(1, 3)))
        if it["gi"] == 3 and it["a"] == 0:
            deferred.append(lambda: (eb_exp(1, 2), eb_exp(1, 3)))
        if it["a"] == 0 and it["gi"] + 1 < 64:
            q_prep(it["gi"] + 1)
        if it["last"]:
            h, g, gi_ = it["h"], it["g"], it["gi"]
            dst = oscr[g * BLK:(g + 1) * BLK, h * 64:(h + 1) * 64].rearrange("(i p) d -> p i d", p=128)
            attn_finalize(T, it["pO"], it["pOkey"], pF, "pF", oT, otok, rc, identF, gi_ % 2, dst, deferred)
            if g == 5 and h + 2 < 8:
                deferred.append(lambda: (eb_dma(h + 2, 0), eb_dma(h + 2, 1)))
            if g == 0 and 1 <= h and h + 1 < 8:
                deferred.append(lambda: (eb_exp(h + 1, 0), eb_exp(h + 1, 1), eb_dma(h + 1, 2), eb_dma(h + 1, 3)))
            if g == 3 and 1 <= h and h + 1 < 8:
                deferred.append(lambda: (eb_exp(h + 1, 2), eb_exp(h + 1, 3)))

    attn_stream(T, items, pS2, PT2, None, 0.125, after_back, bias_ident=C["ident"])


def p4_mla(nc, T, C, es, cqnT, ckvnT, krotT, rope):
    identF, oscr = C["identF"], C["oscr"]
    Wuq = _sb(nc, es, "Wuq", [128, 2, 768], BF16)
    Wuqs = _sb(nc, es, "Wuqs", [128, 2, 768], BF16)
    Wukv = _sb(nc, es, "Wukv", [128, 1024], BF16)
    V_all = _sb(nc, es, "Vall", [128, 32, 8, 66], BF16)
    kT = [_sb(nc, es, "p4kT%d" % i, [96, S], BF16) for i in range(2)]
    qTb = [_sb(nc, es, "p4qT%d" % i, [96, BLK], BF16) for i in range(2)]
    tmp = _sb(nc, es, "p4tmp", [96, 2, BLK], F32)
    ropeb = [_sb(nc, es, "p4rope%d" % i, [96, 2, BLK], F32) for i in range(2)]
    PT2 = [_sb(nc, es, "p4PT%d" % i, [128, 2 * BLK], BF16) for i in range(3)]
    oT = [_sb(nc, es, "p4oT%d" % i, [65, BLK], F32) for i in range(2)]
    otok = [_sb(nc, es, "p4otok%d" % i, [128, 4, 64], F32) for i in range(2)]
    rc = _sb(nc, es, "p4rc", [128, 8], F32)
    pS2 = [_ps(nc, es, "p4pS%d" % i, [128, 2 * BLK], F32) for i in range(2)]
    pO = [_ps(nc, es, "p4pO%d" % i, [128, BLK], F32) for i in range(2)]
    pX = [_ps(nc, es, "p4pX%d" % i, [128, BLK], F32) for i in range(2)]
    pF = pX[1][:, 0:264].rearrange("p (i d) -> p i d", i=4)

    T.dma("pool", Wuq[:, :, :], C["w_uq"].rearrange("(k p) f -> p k f", p=128), writes=["Wuq"], key=("Wgu", 0, 0))
    T.dma("pool", Wuqs[:, :, :], C["w_uqs"].rearrange("(k p) f -> p k f", p=128), writes=["Wuqs"], key=("Wgu", 0, 1))
    T.dma("pool", Wukv[:, :], C["w_ukv"][:, :], writes=["Wukv"], key=("Wgu", 0, 2))
    T.op("pool", lambda e: e.memset(V_all[:, :, :, 64:66], 1.0), writes=["Vall1"])

    wv = Wukv[:, :].rearrange("p (h t d) -> p h t d", h=8, t=2)
    for t in range(32):
        px = pX[t % 2]
        T.op("pe", lambda e: e.matmul(px[:], lhsT=ckvnT[:, t * 128:(t + 1) * 128], rhs=wv[:, :, 1, :],
                                      start=True, stop=True), reads=["Wukv"], writes=[("pX", t % 2)])
        if t % 2:
            T.op("act", lambda e: e.copy(out=V_all[:, t, :, 0:64], in_=px[:, :].rearrange("p (h d) -> p h d", h=8)),
                 writes=[("pX", t % 2), ("v_all", 1)])
        else:
            T.op("dve", lambda e: e.tensor_copy(out=V_all[:, t, :, 0:64], in_=px[:, :].rearrange("p (h d) -> p h d", h=8)),
                 writes=[("pX", t % 2), ("v_all", 0)])

    def k_gen(h):
        kt = kT[h % 2]
        for b in range(NBLK):
            px = pX[b % 2]
            blk = slice(b * BLK, (b + 1) * BLK)
            T.op("pe", lambda e: e.matmul(px[0:64, :], lhsT=Wukv[:, h * 128:h * 128 + 64], rhs=ckvnT[:, blk],
                                          start=True, stop=True), reads=["Wukv"], writes=[("pX", b % 2)])
            T.op("dve", lambda e: e.tensor_copy(out=kt[0:64, blk], in_=px[0:64, :]),
                 writes=[("pX", b % 2), ("kTn", h % 2)])
        T.op("dve", lambda e: e.tensor_copy(out=kt[64:96, :], in_=krotT[64:96, :]), writes=[("kTr", h % 2)])

    def q_gen(h, g, par):
        blk = slice(g * BLK, (g + 1) * BLK)
        T.dma("sp", ropeb[par][64:96, 0, :], C["cosT"][:, blk], writes=[("ropeb", par, 0)])
        T.dma("sp", ropeb[par][64:96, 1, :], C["sinT"][:, blk], writes=[("ropeb", par, 1)])
        for v, W in ((0, Wuq), (1, Wuqs)):
            for kc in range(2):
                T.op("pe", lambda e: e.matmul(pX[v][0:96, :], lhsT=W[:, kc, h * 96:(h + 1) * 96], rhs=cqnT[:, kc, blk],
                                              start=(kc == 0), stop=(kc == 1)),
                     reads=["Wuq", "Wuqs"], writes=[("pX", v)], sig=(kc == 1))
        T.op("dve", lambda e: e.tensor_copy(out=qTb[par][0:64, :], in_=pX[0][0:64, :]),
             writes=[("pX", 0), ("qTbn", par)])
        for v in range(2):
            T.op("dve", lambda e: e.tensor_tensor(out=tmp[64:96, v, :], in0=pX[v][64:96, :], in1=ropeb[par][64:96, v, :],
                                                  op=ALU.mult),
                 reads=[("ropeb", par, v)], writes=[("pX", v), ("tmp", v)])
        T.op("dve", lambda e: e.tensor_tensor(out=qTb[par][64:96, :], in0=tmp[64:96, 0, :], in1=tmp[64:96, 1, :],
                                              op=ALU.add),
             reads=[("tmp", 0), ("tmp", 1)], writes=[("qTb", par)])

    sc = float(96 ** -0.5)
    groups = [(h, g) for h in range(8) for g in range(8)]
    items = []
    for gi, (h, g) in enumerate(groups):
        par = gi % 2
        kt = kT[h % 2]
        for a in range(16):
            chs = []
            for c in range(2):
                j = 2 * a + c

                def qk(e, ps, stop, c0, c1, j=j, kt=kt, par=par):
                    return e.matmul(ps, lhsT=kt[0:96, j * 128:(j + 1) * 128], rhs=qTb[par][0:96, c0:c1],
                                    start=True, stop=stop)
                chs.append(dict(qk=qk, v=V_all[:, j, h, 0:65], mask=None))
            items.append(dict(chunks=chs, first=(a == 0), last=(a == 15), pO=pO[gi % 2], pOkey=("pO", gi % 2),
                              qk_reads=[("kTn", h % 2), ("kTr", h % 2), ("qTb", par), ("qTbn", par)],
                              v_reads=[("v_all", 0), ("v_all", 1), "Vall1"], mask_reads=[], h=h, g=g, gi=gi, a=a))

    k_gen(0)
    q_gen(0, 0, 0)

    def after_back(k, it, deferred):
        gi, a = it["gi"], it["a"]
        if a == 3 and it["g"] == 0 and it["h"] + 1 < 8:
            k_gen(it["h"] + 1)
        if a == 10 and gi + 1 < len(groups):
            nh, ng = groups[gi + 1]
            q_gen(nh, ng, (gi + 1) % 2)
        if it["last"]:
            h, g = it["h"], it["g"]
            dst = oscr[g * BLK:(g + 1) * BLK, 512 + h * 64:512 + (h + 1) * 64].rearrange("(i p) d -> p i d", p=128)
            attn_finalize(T, it["pO"], it["pOkey"], pF, ("pX", 1), oT, otok, rc, identF, gi % 2, dst, deferred)

    attn_stream(T, items, pS2, PT2, None, sc, after_back)


def p5a_wout(nc, T, C, prefetch=None):
    gcols, ident, eps1, h_scr, oscr = C["gcols"], C["ident"], C["eps1"], C["h_scr"], C["oscr"]
    with ExitStack() as es:
        Wout = _sb(nc, es, "Wout", [128, KC, D], BF16)
        Sx = [_sb(nc, es, "p5Sx%d" % i, [128, D], F32) for i in range(4)]
        Rx = [_sb(nc, es, "p5Rx%d" % i, [128, D], F32) for i in range(4)]
        Tt = [_sb(nc, es, "p5Tt%d" % i, [128, D], F32) for i in range(4)]
        xn = [_sb(nc, es, "p5xn%d" % i, [128, D], BF16) for i in range(4)]
        xnT2 = [_sb(nc, es, "p5xnT%d" % i, [128, KC, BLK], BF16) for i in range(2)]
        gpost = _sb(nc, es, "p5gpost", [128, D], F32)
        junk = _sb(nc, es, "p5junk", [128, D], BF16)
        st = _sb(nc, es, "p5st", [128, 64], F32)
        pT2 = [_ps(nc, es, "p5pT%d" % i, [128, KC, 128], BF16) for i in range(2)]
        pD = [_ps(nc, es, "p5pD%d" % i, [128, BLK], F32) for i in range(4)]
        wv = C["w_out"].rearrange("(k p) d -> p k d", p=128)
        for j in range(2):
            T.dma("pool", Wout[:, :, j * 512:(j + 1) * 512], wv[:, :, j * 512:(j + 1) * 512],
                  writes=[("Wout", j)], key=("Wgu", 0, j))
        T.dma("sp", gpost[:], C["grows"][1:2, :].partition_broadcast(128), writes=["gpost"])
        if prefetch is not None:
            prefetch()

        def load_sx(t):
            if t < S // 128:
                T.dma("sp", Sx[t % 4][:], oscr[t * 128:(t + 1) * 128, :], writes=[("Sx", t % 4)])

        def norm_a(b, i):
            t = b * 4 + i
            sl = t % 4
            load_sx(t + 2)
            col = (b % 2) * 16 + i * 4
            for hf in range(2):
                hs = slice(hf * 512, (hf + 1) * 512)
                T.op("act", lambda e: e.activation(out=junk[:, hs], in_=Sx[sl][:, hs], func=AF.Square,
                                                   scale=float(512 ** -0.5), accum_out=st[:, col + hf:col + hf + 1]),
                     reads=[("Sx", sl)], writes=[("junk", hf), ("st", col + hf)])
            T.op("act", lambda e: e.activation(out=st[:, col + 2:col + 4], in_=st[:, col:col + 2], func=AF.Sqrt,
                                               bias=eps1[:, 0:1], scale=1.0),
                 reads=[("st", col), ("st", col + 1)], writes=[("st", col + 2), ("st", col + 3)])
            T.op("dve", lambda e: e.reciprocal(out=st[:, col + 2:col + 4], in_=st[:, col + 2:col + 4]),
                 writes=[("st", col + 2), ("st", col + 3)])

        def norm_b(b, i):
            t = b * 4 + i
            sl = t % 4
            col = (b % 2) * 16 + i * 4
            for hf in range(2):
                hs = slice(hf * 512, (hf + 1) * 512)
                T.op("act", lambda e: e.activation(out=xn[i][:, hs], in_=Sx[sl][:, hs], func=AF.Copy,
                                                   scale=st[:, col + 2 + hf:col + 3 + hf]),
                     reads=[("Sx", sl), ("st", col + 2 + hf)], writes=[("xn", i, hf)])

        def norm_stage(b):
            for i in range(4):
                norm_a(b, i)
                norm_b(b, i)

        def transpose_stage(b):
            for i in range(4):
                pT = pT2[i % 2]
                for k in range(KC):
                    T.op("pe", lambda e: e.transpose(pT[:, k, :], xn[i][:, k * 128:(k + 1) * 128], ident[:]),
                         reads=[("xn", i, k // 4)], writes=[("pT", i % 2)], sig=(k == KC - 1))
                T.op("dve", lambda e: e.tensor_tensor(
                    out=xnT2[b % 2][:, :, i * 128:(i + 1) * 128], in0=pT[:, :, :],
                    in1=gcols[:, 27:35].unsqueeze(2).to_broadcast([128, KC, 128]), op=ALU.mult),
                    writes=[("pT", i % 2), ("xnT", b % 2, i)])

        def out_tile(b, i):
            t = b * 4 + i
            sl = t % 4
            T.dma("sp", Rx[sl][:], h_scr[t * 128:(t + 1) * 128, :], writes=[("Rx", sl)])
            col = 32 + (t % 4) * 4
            for n in range(2):
                pdi = (i % 2) * 2 + n
                pd = pD[pdi]
                for k in range(KC):
                    T.op("pe", lambda e: e.matmul(pd[:], lhsT=xnT2[b % 2][:, k, i * 128:(i + 1) * 128],
                                                  rhs=Wout[:, k, n * 512:(n + 1) * 512],
                                                  start=(k == 0), stop=(k == KC - 1)),
                         reads=[("xnT", b % 2, i), ("Wout", n)], writes=[("pD", pdi)], sig=(k == KC - 1))
                T.op("act", lambda e: e.activation(out=junk[:, 0:512], in_=pd[:], func=AF.Square,
                                                   scale=float(D ** -0.5), accum_out=st[:, col + n:col + n + 1]),
                     writes=[("pD", pdi), ("junk", 0), ("st", col + n)])
                T.op("dve", lambda e: e.tensor_tensor(out=Tt[sl][:, n * 512:(n + 1) * 512], in0=pd[:],
                                                      in1=gpost[:, n * 512:(n + 1) * 512], op=ALU.mult),
                     reads=["gpost"], writes=[("pD", pdi), ("Tt", sl, n)])
            T.op("dve", lambda e: e.tensor_tensor(out=st[:, col + 2:col + 3], in0=st[:, col:col + 1],
                                                  in1=st[:, col + 1:col + 2], op=ALU.add),
                 reads=[("st", col), ("st", col + 1)], writes=[("st", col + 2)])
            T.op("act", lambda e: e.activation(out=st[:, col + 3:col + 4], in_=st[:, col + 2:col + 3], func=AF.Sqrt,
                                               bias=eps1[:, 0:1], scale=1.0),
                 reads=[("st", col + 2)], writes=[("st", col + 3)])
            T.op("dve", lambda e: e.reciprocal(out=st[:, col + 3:col + 4], in_=st[:, col + 3:col + 4]),
                 reads=[("st", col + 3)], writes=[("st", col + 3)])
            T.op("dve", lambda e: e.scalar_tensor_tensor(out=Rx[sl][:], in0=Tt[sl][:], scalar=st[:, col + 3:col + 4],
                                                         in1=Rx[sl][:], op0=ALU.mult, op1=ALU.add),
                 reads=[("Tt", sl, 0), ("Tt", sl, 1), ("st", col + 3), ("Rx", sl)], writes=[("Rx", sl)])
            T.dma("sp", h_scr[t * 128:(t + 1) * 128, :], Rx[sl][:], reads=[("Rx", sl)], key=("Rxo", sl))

        load_sx(0)
        load_sx(1)
        norm_stage(0)
        transpose_stage(0)
        norm_stage(1)
        transpose_stage(1)
        for b in range(NBLK):
            for i in range(4):
                if b + 2 < NBLK:
                    norm_a(b + 2, i)
                out_tile(b, i)
                if b + 2 < NBLK:
                    norm_b(b + 2, i)
            if b + 2 < NBLK:
                transpose_stage(b + 2)
        T.barrier()


PH_ALL = ("f1", "mix", "wout", "f2")


def build_program(phases=PH_ALL, dbg_out=None, dbg_h_from_x=False):
    nc = bass.Bass("TRN2", target_bir_lowering=False)
    dram_in = lambda n, shp, dt=F32: nc.dram_tensor(n, list(shp), dt, kind="ExternalInput").ap()
    C = {}
    x = dram_in("x", [S, D])
    for nm, shp in (("w_gu1", [D, 2 * DFF]), ("w_dn1", [DFF, D]), ("w_gu2", [D, 2 * DFF]), ("w_dn2", [DFF, D]),
                    ("w_na", [D, 1536]), ("w_lat", [D, 576]), ("w_uq", [256, 768]), ("w_uqs", [256, 768]),
                    ("w_ukv", [128, 1024]), ("w_out", [D, D]), ("gcols_d", [128, 40]), ("grows", [4, D]),
                    ("ident_d", [128, 128]), ("cosT", [32, S]), ("sinT", [32, S]), ("natab", [8, 128, TAB_W])):
        C[nm] = dram_in(nm, shp)
    out = nc.dram_tensor("out", [S, D], F32, kind="ExternalOutput").ap()
    if dbg_out == "oscr":
        C["h_scr"] = nc.dram_tensor("h_scr", [S, D], F32).ap()
        C["oscr"] = out
    elif dbg_out == "h_scr":
        C["h_scr"] = out
        C["oscr"] = nc.dram_tensor("oscr", [S, D], F32).ap()
    else:
        C["h_scr"] = nc.dram_tensor("h_scr", [S, D], F32).ap()
        C["oscr"] = nc.dram_tensor("oscr", [S, D], F32).ap()

    if dbg_h_from_x:
        C["h_scr"] = x
    with ExitStack() as es:
        T = Tracker(nc, es)
        gcols = _sb(nc, es, "sb_gcols", [128, 40], F32)
        ident = _sb(nc, es, "sb_ident", [128, 128], BF16)
        identF = _sb(nc, es, "sb_identF", [128, 128], F32)
        ones_bf = _sb(nc, es, "sb_ones", [128, 128], BF16)
        eps1 = _sb(nc, es, "eps1", [128, 1], F32)
        eps4 = _sb(nc, es, "eps4", [128, 1], F32)
        T.dma("sp", gcols[:], C["gcols_d"][:, :], writes=["gcols"])
        T.dma("pool", ident[:], C["ident_d"][:, :], writes=["ident"])
        T.dma("sp", identF[:], C["ident_d"][:, :], writes=["identF"])
        T.op("dve", lambda e: e.memset(eps1[:], EPS), writes=["eps1"])
        T.op("dve", lambda e: e.memset(eps4[:], 4 * EPS), writes=["eps4"])
        T.op("dve", lambda e: e.memset(ones_bf[:], 1.0), writes=["ones"])
        T.barrier()
        C.update(gcols=gcols, ident=ident, identF=identF, ones_bf=ones_bf, eps1=eps1, eps4=eps4)
        h_scr = C["h_scr"]
        if "f1" in phases:
            ffn_phase(nc, T, "f1", x, h_scr, C["w_gu1"], C["w_dn1"], gcols, 0, C["grows"][0:1, :], ident, eps1, eps4)
        if "mix" in phases:
            mixer_phase(nc, T, C)
        if "wout" in phases:
            p5a_wout(nc, T, C)
        if "f2" in phases:
            ffn_phase(nc, T, "f2", h_scr, out, C["w_gu2"], C["w_dn2"], gcols, 16, C["grows"][2:3, :], ident, eps1, eps4)
        T.barrier()
    return nc


def _na_bias_tables(rpb):
    H = rpb.shape[0]
    kl = np.arange(128) // 64
    kc = np.arange(128) % 64
    qc = np.arange(64)
    ws = np.clip(qc - 8, 0, 48)
    colv = (kc[:, None] >= ws[None, :]) & (kc[:, None] < ws[None, :] + 16)
    coff = np.clip(kc[:, None] - qc[None, :] + 15, 0, 30)
    tab = np.full((H, 128, TAB_W), NEG, np.float32)
    for e in range(22):
        dr = (10 - e) + kl
        rowv = (dr >= -4) & (dr <= 3)
        roff = np.clip(dr + 7, 0, 14)
        vals = rpb[:, roff[:, None], coff]
        m = rowv[:, None] & colv
        tab[:, :, e * 64:(e + 1) * 64] = np.where(m[None], vals, np.float32(NEG))
    for base, r0, k0 in ((FIRST0, 0, 0), (LAST0, 56, 52)):
        for j in range(6):
            for i in range(8):
                r = r0 + i
                rs = min(max(r - 4, 0), 56)
                kr = k0 + 2 * j + kl
                rowv = (kr >= rs) & (kr < rs + 8)
                roff = np.clip(kr - r + 7, 0, 14)
                vals = rpb[:, roff[:, None], coff]
                m = rowv[:, None] & colv
                o = base + j * 512 + i * 64
                tab[:, :, o:o + 64] = np.where(m[None], vals, np.float32(NEG))
    return tab


def _rope_tables():
    t = np.arange(S)
    row = (t // 64).astype(np.float32)
    col = (t % 64).astype(np.float32)
    inv = (np.float32(1.0) / (np.float32(10000.0) ** (np.arange(8, dtype=np.float32) / np.float32(8)))).astype(np.float32)
    ang = np.concatenate([row[:, None] * inv[None, :], col[:, None] * inv[None, :]], axis=-1).astype(np.float32)
    cos, sin = np.cos(ang).astype(np.float32), np.sin(ang).astype(np.float32)
    cosT = np.concatenate([cos.T, cos.T], axis=0)
    sinT = np.concatenate([-sin.T, sin.T], axis=0)
    return np.ascontiguousarray(cosT), np.ascontiguousarray(sinT)


def host_prep(inp):
    p = {k: np.asarray(v) for k, v in inp.items()}
    L = 0
    f = np.float32
    gcols = np.zeros((128, 40), f)
    gcols[:, 0:8] = p["ffn1_pre_g"][L].reshape(8, 128).T
    gcols[:, 8:16] = p["mix_pre_g"][L].reshape(8, 128).T
    gcols[:, 16:24] = p["ffn2_pre_g"][L].reshape(8, 128).T
    gcols[:, 24:26] = p["mla_q_norm_g"][L].reshape(2, 128).T
    gcols[:, 26:27] = p["mla_kv_norm_g"][L].reshape(1, 128).T
    gcols[:, 27:35] = np.concatenate([p["na_out_norm_g"][L], p["mla_out_norm_g"][L]]).reshape(8, 128).T
    grows = np.stack([p["ffn1_post_g"][L], p["mix_post_g"][L], p["ffn2_post_g"][L], p["ffn2_post_g"][L]]).astype(f)
    w_in = p["w_in"][L]
    kr = w_in[:, 1920:1952]
    krs = np.concatenate([kr[:, 16:32], kr[:, 0:16]], axis=1)
    z64 = np.zeros((D, 64), f)
    w_lat = np.concatenate([w_in[:, 1536:1920], z64, kr, z64, krs], axis=1)
    w_uq = p["mla_w_uq"][L]
    w_uqs = w_uq.reshape(256, 8, 96).copy()
    w_uqs[:, :, 64:80] = w_uq.reshape(256, 8, 96)[:, :, 80:96]
    w_uqs[:, :, 80:96] = w_uq.reshape(256, 8, 96)[:, :, 64:80]
    cosT, sinT = _rope_tables()
    c = np.ascontiguousarray
    shared = dict(
        w_gu1=c(p["ffn1_w_gu"][L]), w_dn1=c(p["ffn1_w_down"][L]), w_gu2=c(p["ffn2_w_gu"][L]), w_dn2=c(p["ffn2_w_down"][L]),
        w_na=c(w_in[:, 0:1536]), w_lat=c(w_lat.astype(f)), w_uq=c(w_uq), w_uqs=c(w_uqs.reshape(256, 768)),
        w_ukv=c(p["mla_w_ukv"][L]), w_out=c(p["w_out"][L]), gcols_d=gcols, grows=grows,
        ident_d=np.eye(128, dtype=f), cosT=cosT, sinT=sinT, natab=_na_bias_tables(p["na_rpb"][L].astype(f)),
    )
    return p, shared


def kernel(**inputs):
    p, shared = host_prep(inputs)
    nc = build_program()
    n = 8
    in_maps = [dict(shared, x=np.ascontiguousarray(p["x"][c], dtype=np.float32)) for c in range(n)]
    res = run_bass_kernel_spmd(nc, in_maps, core_ids=list(range(n)))
    return np.stack([np.asarray(r["out"]) for r in res.results], axis=0).astype(np.float32)
```

```python
import numpy as np
from contextlib import ExitStack
import concourse.bass as bass
import concourse.mybir as mybir
from concourse.bass_utils import run_bass_kernel_spmd

F32 = mybir.dt.float32
BF16 = mybir.dt.bfloat16
AF = mybir.ActivationFunctionType
ALU = mybir.AluOpType

S = 4096
D = 1024
DFF = 2816
NFC = DFF // 128
KC = D // 128
BLK = 512
NBLK = S // BLK
EPS = 1e-6
NEG = -1e30


class Tracker:
    def __init__(self, nc, es):
        self.nc = nc
        self.es = es
        self.eng = dict(pe=nc.tensor, act=nc.scalar, dve=nc.vector, pool=nc.gpsimd, sp=nc.sync)
        self.sem = {k: es.enter_context(nc.semaphore("s_" + k)) for k in ("pe", "act", "dve", "pool")}
        self.cnt = {k: 0 for k in self.sem}
        self.pending = {k: False for k in self.sem}
        self.seen = {k: {} for k in self.eng}
        self.res = {}
        self.dsem = {}
        self.dcnt = {}
        self.semobj = {}

    def _sid(self, s):
        self.semobj[id(s)] = s
        return id(s)

    def _deps(self, reads, writes):
        deps = []
        for k in reads:
            r = self.res.get(k)
            if r and r[0]:
                deps.append(r[0])
        for k in writes:
            r = self.res.get(k)
            if r:
                if r[0]:
                    deps.append(r[0])
                deps.extend(r[1])
        return deps

    def _wait(self, e, deps):
        best = {}
        for (s, v) in deps:
            sid = self._sid(s)
            if e == "pe" and s is self.sem["pe"]:
                continue
            if v > best.get(sid, 0):
                best[sid] = v
        for sid, v in best.items():
            if self.seen[e].get(sid, 0) >= v:
                continue
            self.eng[e].wait_ge(self.semobj[sid], v)
            self.seen[e][sid] = v

    def _record(self, tok, reads, writes):
        for k in reads:
            r = self.res.setdefault(k, [None, []])
            r[1].append(tok)
            if len(r[1]) > 32:
                r[1] = self._compact(r[1])
        for k in writes:
            self.res[k] = [tok, []]

    @staticmethod
    def _compact(lst):
        best = {}
        for (s, v) in lst:
            if v > best.get(id(s), (None, 0))[1]:
                best[id(s)] = (s, v)
        return list(best.values())

    def op(self, e, fn, reads=(), writes=(), sig=True):
        self._wait(e, self._deps(reads, writes))
        ins = fn(self.eng[e])
        if sig:
            self.cnt[e] += 1
            ins.then_inc(self.sem[e], 1)
            self.pending[e] = False
            tok = (self.sem[e], self.cnt[e])
        else:
            self.pending[e] = True
            tok = (self.sem[e], self.cnt[e] + 1)
        self._record(tok, reads, writes)
        return ins

    def dma(self, q, out, in_, reads=(), writes=(), key=None):
        if key is None:
            key = (tuple(writes) + tuple(reads))[0]
        if key not in self.dsem:
            self.dsem[key] = self.es.enter_context(self.nc.semaphore("d%d" % len(self.dsem)))
            self.dcnt[key] = 0
        self._wait(q, self._deps(reads, writes))
        ins = self.eng[q].dma_start(out=out, in_=in_)
        self.dcnt[key] += 16
        ins.then_inc(self.dsem[key], 16)
        self._record((self.dsem[key], self.dcnt[key]), reads, writes)
        return ins

    def barrier(self, engines=("pe", "act", "dve", "pool", "sp")):
        for e in self.sem:
            assert not self.pending[e], e
        toks = [(self.sem[e], self.cnt[e]) for e in self.sem if self.cnt[e] > 0]
        toks += [(self.dsem[k], self.dcnt[k]) for k in self.dsem]
        for e in engines:
            best = {}
            for (s, v) in toks:
                best[self._sid(s)] = v
            for sid, v in best.items():
                if self.seen[e].get(sid, 0) >= v:
                    continue
                self.eng[e].wait_ge(self.semobj[sid], v)
                self.seen[e][sid] = v
        self.res = {}


def _sb(nc, es, name, shape, dt):
    return es.enter_context(nc.sbuf_tensor(name, list(shape), dt))


def _ps(nc, es, name, shape, dt):
    return es.enter_context(nc.psum_tensor(name, list(shape), dt))


def ffn_alloc_weights(nc, es, ph):
    Wgu = _sb(nc, es, ph + "Wgu", [128, KC, 2 * DFF], BF16)
    Wdn = _sb(nc, es, ph + "Wdn", [128, NFC, D], BF16)
    return Wgu, Wdn


def ffn_load_weights(T, Wgu, Wdn, w_gu, w_dn):
    wgu_v = w_gu.rearrange("(k p) f -> p k f", p=128)
    wdn_v = w_dn.rearrange("(c p) d -> p c d", p=128)
    for j in range(6):
        c0, c1 = j * 512, min((j + 1) * 512, DFF)
        for half in range(2):
            o = half * DFF
            T.dma("pool", Wgu[:, :, o + c0:o + c1], wgu_v[:, :, o + c0:o + c1],
                  writes=[("Wgu", half, j)], key=("Wgu", half, j))
    for j in range(2):
        T.dma("pool", Wdn[:, j * 11:(j + 1) * 11, :], wdn_v[:, j * 11:(j + 1) * 11, :],
              writes=[("Wdn", j)], key=("Wdn", j))


def ffn_phase(nc, T, ph, src, dst, w_gu, w_dn, gcols, gc0, grow, ident, eps1, eps4, W=None):
    with ExitStack() as es:
        if W is None:
            Wgu, Wdn = ffn_alloc_weights(nc, es, ph)
            ffn_load_weights(T, Wgu, Wdn, w_gu, w_dn)
        else:
            Wgu, Wdn = W
        Sx = [_sb(nc, es, ph + "Sx%d" % i, [128, D], F32) for i in range(2)]
        Rx = [_sb(nc, es, ph + "Rx%d" % i, [128, D], F32) for i in range(2)]
        Tt = [_sb(nc, es, ph + "Tt%d" % i, [128, D], F32) for i in range(2)]
        xn = [_sb(nc, es, ph + "xn%d" % i, [128, D], BF16) for i in range(4)]
        xnT = _sb(nc, es, ph + "xnT", [128, KC, BLK], BF16)
        hT = _sb(nc, es, ph + "hT", [128, NFC, BLK], BF16)
        sg = [_sb(nc, es, ph + "sg%d" % i, [128, BLK], BF16) for i in range(2)]
        gpost = _sb(nc, es, ph + "gpost", [128, D], F32)
        junk = _sb(nc, es, ph + "junk", [128, D], BF16)
        st = _sb(nc, es, ph + "st", [128, 64], F32)
        pT2 = [_ps(nc, es, ph + "pT%d" % i, [128, KC, 128], BF16) for i in range(2)]
        pG = [_ps(nc, es, ph + "pG%d" % i, [128, BLK], F32) for i in range(4)]
        pD = [_ps(nc, es, ph + "pD%d" % i, [128, BLK], F32) for i in range(2)]

        T.dma("sp", gpost[:], grow.partition_broadcast(128), writes=["gpost"])

        def wgu_key(c):
            return c * 128 // 512

        def norm_stage(b):
            for i in range(4):
                t = b * 4 + i
                sl = t % 2
                T.dma("sp", Sx[sl][:], src[t * 128:(t + 1) * 128, :], writes=[("Sx", sl)])
                col = (b % 2) * 8 + i
                T.op("act", lambda e: e.activation(out=junk[:], in_=Sx[sl][:], func=AF.Square,
                                                   scale=float(D ** -0.5), accum_out=st[:, col:col + 1]),
                     reads=[("Sx", sl)], writes=["junk", ("st", col)])
                T.op("act", lambda e: e.activation(out=st[:, col + 4:col + 5], in_=st[:, col:col + 1], func=AF.Sqrt,
                                                   bias=eps1[:, 0:1], scale=1.0),
                     reads=[("st", col)], writes=[("st", col + 4)])
                T.op("dve", lambda e: e.reciprocal(out=st[:, col + 4:col + 5], in_=st[:, col + 4:col + 5]),
                     reads=[("st", col + 4)], writes=[("st", col + 4)])
                T.op("dve", lambda e: e.tensor_scalar(out=xn[i][:], in0=Sx[sl][:], scalar1=st[:, col + 4:col + 5],
                                                      scalar2=None, op0=ALU.mult),
                     reads=[("Sx", sl), ("st", col + 4)], writes=[("xn", i)])

        def transpose_stage(b):
            for i in range(4):
                pT = pT2[i % 2]
                for k in range(KC):
                    T.op("pe", lambda e: e.transpose(pT[:, k, :], xn[i][:, k * 128:(k + 1) * 128], ident[:]),
                         reads=[("xn", i)], writes=[("pT", i % 2)], sig=(k == KC - 1))
                T.op("dve", lambda e: e.tensor_tensor(
                    out=xnT[:, :, i * 128:(i + 1) * 128], in0=pT[:, :, :],
                    in1=gcols[:, gc0:gc0 + KC].unsqueeze(2).to_broadcast([128, KC, 128]), op=ALU.mult),
                    writes=[("pT", i % 2), ("xnT", i)])

        def gu_chunk(b, c):
            pg, pu = pG[(c % 2) * 2], pG[(c % 2) * 2 + 1]
            for half, pp in ((0, pg), (1, pu)):
                o = half * DFF + c * 128
                for k in range(KC):
                    T.op("pe", lambda e: e.matmul(pp[:], lhsT=Wgu[:, k, o:o + 128], rhs=xnT[:, k, :],
                                                  start=(k == 0), stop=(k == KC - 1)),
                         reads=[("Wgu", half, wgu_key(c))] + [("xnT", i) for i in range(4)],
                         writes=[("pG", (c % 2) * 2 + half)], sig=(k == KC - 1))
            s = sg[c % 2]
            T.op("act", lambda e: e.activation(out=s[:], in_=pg[:], func=AF.Silu),
                 reads=[("pG", (c % 2) * 2)], writes=[("sg", c % 2)])
            T.op("dve", lambda e: e.tensor_tensor(out=hT[:, c, :], in0=s[:], in1=pu[:], op=ALU.mult),
                 reads=[("sg", c % 2), ("pG", (c % 2) * 2 + 1)], writes=[("hT", c)])

        def down_tile(b, i):
            t = b * 4 + i
            sl = t % 2
            T.dma("sp", Rx[sl][:], src[t * 128:(t + 1) * 128, :], writes=[("Rx", sl)])
            col = 16 + (t % 2) * 8
            for n in range(2):
                pd = pD[n]
                for c in range(NFC):
                    T.op("pe", lambda e: e.matmul(pd[:], lhsT=hT[:, c, i * 128:(i + 1) * 128],
                                                  rhs=Wdn[:, c, n * 512:(n + 1) * 512],
                                                  start=(c == 0), stop=(c == NFC - 1)),
                         reads=[("hT", c), ("Wdn", c // 11)], writes=[("pD", n)], sig=(c == NFC - 1))
                T.op("act", lambda e: e.activation(out=junk[:, 0:512], in_=pd[:], func=AF.Square,
                                                   scale=float(D ** -0.5), accum_out=st[:, col + n:col + n + 1]),
                     writes=[("pD", n), "junk", ("st", col + n)])
                T.op("dve", lambda e: e.tensor_tensor(out=Tt[sl][:, n * 512:(n + 1) * 512], in0=pd[:],
                                                      in1=gpost[:, n * 512:(n + 1) * 512], op=ALU.mult),
                     reads=["gpost"], writes=[("pD", n), ("Tt", sl, n)])
            T.op("dve", lambda e: e.tensor_tensor(out=st[:, col + 2:col + 3], in0=st[:, col:col + 1],
                                                  in1=st[:, col + 1:col + 2], op=ALU.add),
                 reads=[("st", col), ("st", col + 1)], writes=[("st", col + 2)])
            T.op("act", lambda e: e.activation(out=st[:, col + 3:col + 4], in_=st[:, col + 2:col + 3], func=AF.Sqrt,
                                               bias=eps4[:, 0:1], scale=4.0),
                 reads=[("st", col + 2)], writes=[("st", col + 3)])
            T.op("dve", lambda e: e.reciprocal(out=st[:, col + 3:col + 4], in_=st[:, col + 3:col + 4]),
                 reads=[("st", col + 3)], writes=[("st", col + 3)])
            T.op("dve", lambda e: e.scalar_tensor_tensor(out=Rx[sl][:], in0=Tt[sl][:], scalar=st[:, col + 3:col + 4],
                                                         in1=Rx[sl][:], op0=ALU.mult, op1=ALU.add),
                 reads=[("Tt", sl, 0), ("Tt", sl, 1), ("st", col + 3), ("Rx", sl)], writes=[("Rx", sl)])
            T.dma("sp", dst[t * 128:(t + 1) * 128, :], Rx[sl][:], reads=[("Rx", sl)], key=("Rxo", sl))

        norm_stage(0)
        transpose_stage(0)
        for b in range(NBLK):
            for c in range(NFC):
                gu_chunk(b, c)
                if c == 8 and b + 1 < NBLK:
                    norm_stage(b + 1)
            if b + 1 < NBLK:
                transpose_stage(b + 1)
            for i in range(4):
                down_tile(b, i)
        T.barrier()


TAB_W = 22 * 64 + 6 * 512 + 6 * 512
FIRST0 = 22 * 64
LAST0 = FIRST0 + 6 * 512
TQ = TAB_W // 4
NA_SUBRANGE = False


def attn_stream(T, items, pS2, PT2, PM2, scale, after_back, bias_ident=None):
    def rng(ch):
        return ch.get("cr", (0, 512))

    def front_pe(k):
        it = items[k]
        s = k % 2
        n = len(it["chunks"])
        for c, ch in enumerate(it["chunks"]):
            c0, c1 = rng(ch)
            ps = pS2[s][:, c * 512 + c0:c * 512 + c1]
            if bias_ident is None:
                T.op("pe", lambda e: ch["qk"](e, ps, True, c0, c1), reads=it["qk_reads"],
                     writes=[("pS", s)], sig=(c == n - 1))
            else:
                T.op("pe", lambda e: ch["qk"](e, ps, False, c0, c1), reads=it["qk_reads"],
                     writes=[("pS", s)], sig=False)
                T.op("pe", lambda e: e.matmul(ps, lhsT=bias_ident[:, :], rhs=ch["mask"][:, c0:c1],
                                              start=False, stop=True),
                     reads=it["mask_reads"], writes=[("pS", s)], sig=(c == n - 1))

    def front_act(k):
        it = items[k]
        s = k % 2
        s3 = k % 3
        n = len(it["chunks"])
        if all(rng(ch) == (0, 512) for ch in it["chunks"]):
            T.op("act", lambda e: e.activation(out=PT2[s3][:, 0:n * 512], in_=pS2[s][:, 0:n * 512], func=AF.Exp,
                                               scale=float(scale)), writes=[("pS", s), ("PT", s3)])
        else:
            for c, ch in enumerate(it["chunks"]):
                c0, c1 = rng(ch)
                T.op("act", lambda e: e.activation(out=PT2[s3][:, c * 512 + c0:c * 512 + c1],
                                                   in_=pS2[s][:, c * 512 + c0:c * 512 + c1], func=AF.Exp,
                                                   scale=float(scale)), writes=[("pS", s), ("PT", s3)])

    for k in range(min(2, len(items))):
        front_pe(k)
        front_act(k)
    deferred = []
    for k in range(len(items)):
        if k + 2 < len(items):
            front_pe(k + 2)
            front_act(k + 2)
        for fn in deferred:
            fn()
        deferred = []
        it = items[k]
        s3 = k % 3
        n = len(it["chunks"])
        for c, ch in enumerate(it["chunks"]):
            c0, c1 = rng(ch)
            T.op("pe", lambda e: e.matmul(it["pO"][0:65, c0:c1], lhsT=ch["v"], rhs=PT2[s3][:, c * 512 + c0:c * 512 + c1],
                                          start=(it["first"] and c == 0), stop=(it["last"] and c == n - 1),
                                          skip_group_check=True),
                 reads=[("PT", s3)] + it["v_reads"], writes=[it["pOkey"]], sig=True)
        after_back(k, it, deferred)
    for fn in deferred:
        fn()


def attn_finalize(T, pO, pOkey, pF, pFkey, oT, otok, rc, identF, par, dst_ap, deferred):
    T.op("dve", lambda e: e.tensor_copy(out=oT[par][0:65, :], in_=pO[0:65, :]), writes=[pOkey, ("oT", par)])

    def late():
        for i in range(4):
            T.op("pe", lambda e: e.transpose(pF[:, i, 0:65], oT[par][0:65, i * 128:(i + 1) * 128], identF[0:65, 0:65]),
                 reads=[("oT", par)], writes=[pFkey], sig=(i == 3))
        rcv = rc[:, par * 4:par * 4 + 4]
        T.op("dve", lambda e: e.reciprocal(out=rcv.unsqueeze(2), in_=pF[:, :, 64:65]), writes=[pFkey, ("rc", par)])
        T.op("dve", lambda e: e.tensor_tensor(out=otok[par][:, :, :], in0=pF[:, :, 0:64],
                                              in1=rcv.unsqueeze(2).to_broadcast([128, 4, 64]), op=ALU.mult),
             reads=[("rc", par)], writes=[pFkey, ("otok", par)])
        T.dma("sp", dst_ap, otok[par][:, :, :], reads=[("otok", par)], key=("otok_o", par))
    deferred.append(late)


def mixer_phase(nc, T, C):
    gcols, ident, identF, eps1, ones_bf = C["gcols"], C["ident"], C["identF"], C["eps1"], C["ones_bf"]
    h_scr, oscr = C["h_scr"], C["oscr"]
    with ExitStack() as es0:
        cqnT = _sb(nc, es0, "cqnT", [128, 2, S], BF16)
        ckvnT = _sb(nc, es0, "ckvnT", [128, S], BF16)
        krotT = _sb(nc, es0, "krotT", [96, S], BF16)
        rope = None
        with ExitStack() as esA:
            qT_na = [_sb(nc, esA, "qTna%d" % i, [128, S], BF16) for i in range(4)]
            kT_na = [_sb(nc, esA, "kTna%d" % i, [128, S], BF16) for i in range(4)]
            V_na = _sb(nc, esA, "Vna", [128, 32, 8, 66], BF16)
            T.op("pool", lambda e: e.memset(V_na[:, :, :, 64:66], 1.0), writes=["Vna1"])
            with ExitStack() as es:
                p2_proj(nc, T, C, es, cqnT, ckvnT, krotT, rope, qT_na, kT_na, V_na)
            T.barrier()
            with ExitStack() as es:
                p3_na(nc, T, C, es, qT_na, kT_na, V_na)
            T.barrier()
        with ExitStack() as es:
            p4_mla(nc, T, C, es, cqnT, ckvnT, krotT, rope)
        T.barrier()


B2 = 256
NB2 = S // B2
TP2 = B2 // 128


def p2_proj(nc, T, C, es, cqnT, ckvnT, krotT, rope, qT_na, kT_na, V_na):
    gcols, ident, eps1, ones_bf, h_scr = C["gcols"], C["ident"], C["eps1"], C["ones_bf"], C["h_scr"]
    Wna = _sb(nc, es, "Wna", [128, KC, 1536], BF16)
    Wlat = _sb(nc, es, "Wlat", [128, KC, 576], BF16)
    Sx = [_sb(nc, es, "p2Sx%d" % i, [128, D], F32) for i in range(2)]
    xn = [_sb(nc, es, "p2xn%d" % i, [128, D], BF16) for i in range(2)]
    xnT2 = [_sb(nc, es, "p2xnT%d" % i, [128, KC, B2], BF16) for i in range(2)]
    st = _sb(nc, es, "p2st", [128, 64], F32)
    sq = _sb(nc, es, "p2sq", [128, 2, 512], BF16)
    rs = _sb(nc, es, "p2rs", [128, B2], F32)
    tmp = _sb(nc, es, "p2tmp", [96, 2, B2], F32)
    ropeb = [_sb(nc, es, "p2rope%d" % i, [96, 2, B2], F32) for i in range(3)]
    pT2 = [_ps(nc, es, "p2pT%d" % i, [128, KC, 128], BF16) for i in range(2)]
    pA = [_ps(nc, es, "p2pA%d" % i, [128, BLK], F32) for i in range(2)]
    pC = [_ps(nc, es, "p2pC%d" % i, [128, BLK], F32) for i in range(2)]
    pSm = _ps(nc, es, "p2pS", [128, BLK], F32)
    pK1 = _ps(nc, es, "p2pK1", [128, BLK], F32)
    pK = [pC[1], pC[0]]
    pKkey = [("pC", 1), ("pC", 0)]
    CQB = [(pC[0], ("pC", 0)), (pC[1], ("pC", 1))]
    CKVB = [(pK1, ("pK", 1))]
    junk = sq[:, :, :].rearrange("p a b -> p (a b)")

    wna_v = C["w_na"].rearrange("(k p) f -> p k f", p=128)
    for j in range(3):
        T.dma("pool", Wna[:, :, j * 512:(j + 1) * 512], wna_v[:, :, j * 512:(j + 1) * 512],
              writes=[("Wna", j)], key=("Wgu", 0, j))
    T.dma("pool", Wlat[:, :, :], C["w_lat"].rearrange("(k p) f -> p k f", p=128), writes=["Wlat"], key=("Wgu", 0, 3))

    def norm_stage(b):
        blk = slice(b * B2, (b + 1) * B2)
        T.dma("sp", ropeb[b % 3][64:96, 0, :], C["cosT"][:, blk], writes=[("ropeb", b % 3, 0)])
        T.dma("sp", ropeb[b % 3][64:96, 1, :], C["sinT"][:, blk], writes=[("ropeb", b % 3, 1)])
        for i in range(TP2):
            t = b * TP2 + i
            sl = t % 2
            T.dma("sp", Sx[sl][:], h_scr[t * 128:(t + 1) * 128, :], writes=[("Sx", sl)])
            col = (b % 2) * 8 + i
            T.op("act", lambda e: e.activation(out=junk, in_=Sx[sl][:], func=AF.Square,
                                               scale=float(D ** -0.5), accum_out=st[:, col:col + 1]),
                 reads=[("Sx", sl)], writes=[("sq", 0), ("sq", 1), ("st", col)])
            T.op("act", lambda e: e.activation(out=st[:, col + 4:col + 5], in_=st[:, col:col + 1], func=AF.Sqrt,
                                               bias=eps1[:, 0:1], scale=1.0),
                 reads=[("st", col)], writes=[("st", col + 4)])
            T.op("dve", lambda e: e.reciprocal(out=st[:, col + 4:col + 5], in_=st[:, col + 4:col + 5]),
                 reads=[("st", col + 4)], writes=[("st", col + 4)])
            T.op("dve", lambda e: e.tensor_scalar(out=xn[i][:], in0=Sx[sl][:], scalar1=st[:, col + 4:col + 5],
                                                  scalar2=None, op0=ALU.mult),
                 reads=[("Sx", sl), ("st", col + 4)], writes=[("xn", i)])

    def transpose_stage(b):
        xnT = xnT2[b % 2]
        for i in range(TP2):
            pT = pT2[i % 2]
            for k in range(KC):
                T.op("pe", lambda e: e.transpose(pT[:, k, :], xn[i][:, k * 128:(k + 1) * 128], ident[:]),
                     reads=[("xn", i)], writes=[("pT", i % 2)], sig=(k == KC - 1))
            T.op("dve", lambda e: e.tensor_tensor(
                out=xnT[:, :, i * 128:(i + 1) * 128], in0=pT[:, :, :],
                in1=gcols[:, 8:16].unsqueeze(2).to_broadcast([128, KC, 128]), op=ALU.mult),
                writes=[("pT", i % 2), ("xnT", b % 2, i)])

    nev = [0]

    def evac(out, in_, reads, writes):
        nev[0] += 1
        if nev[0] % 2:
            T.op("act", lambda e: e.copy(out=out, in_=in_), reads=reads, writes=writes)
        else:
            T.op("dve", lambda e: e.tensor_copy(out=out, in_=in_), reads=reads, writes=writes)

    def lat_mm(b, nch, c0, banks):
        xnT = xnT2[b % 2]
        XR = [("xnT", b % 2, i) for i in range(TP2)]
        for ch in range(nch):
            pb, pkey = banks[ch]
            for k in range(KC):
                T.op("pe", lambda e: e.matmul(pb[:, 0:B2], lhsT=Wlat[:, k, c0 + ch * 128:c0 + (ch + 1) * 128],
                                              rhs=xnT[:, k, :], start=(k == 0), stop=(k == KC - 1)),
                     reads=["Wlat"] + XR, writes=[pkey], sig=(k == KC - 1))
            T.op("act", lambda e: e.activation(out=sq[:, ch, 0:B2], in_=pb[:, 0:B2], func=AF.Square),
                 writes=[pkey, ("sq", ch)])

    def lat_sum(b, nch, nfeat):
        for ch in range(nch):
            T.op("pe", lambda e: e.matmul(pSm[:, 0:B2], lhsT=ones_bf[:, :], rhs=sq[:, ch, 0:B2],
                                          start=(ch == 0), stop=(ch == nch - 1)),
                 reads=[("sq", ch)], writes=["pSm"], sig=(ch == nch - 1))
        T.op("act", lambda e: e.activation(out=rs[:], in_=pSm[:, 0:B2], func=AF.Sqrt, bias=eps1[:, 0:1],
                                           scale=1.0 / nfeat), writes=["pSm", "rs"])
        T.op("dve", lambda e: e.reciprocal(out=rs[:], in_=rs[:]), writes=["rs"])

    def lat_scale(b, nch, c0, gcol0, dstf, banks):
        blk = slice(b * B2, (b + 1) * B2)
        for ch in range(nch):
            pb, pkey = banks[ch]
            T.op("dve", lambda e: e.scalar_tensor_tensor(out=dstf(ch)[:, blk], in0=pb[:, 0:B2],
                                                         scalar=gcols[:, gcol0 + ch:gcol0 + ch + 1], in1=rs[:],
                                                         op0=ALU.mult, op1=ALU.mult),
                 reads=["rs"], writes=[pkey, ("lat", c0, ch)])

    def na_qk(b, fcs):
        blk = slice(b * B2, (b + 1) * B2)
        xnT = xnT2[b % 2]
        XR = [("xnT", b % 2, i) for i in range(TP2)]
        for fc in fcs:
            pa = pA[fc % 2]
            for k in range(KC):
                T.op("pe", lambda e: e.matmul(pa[:, 0:B2], lhsT=Wna[:, k, fc * 128:(fc + 1) * 128], rhs=xnT[:, k, :],
                                              start=(k == 0), stop=(k == KC - 1)),
                     reads=[("Wna", fc // 4)] + XR, writes=[("pA", fc % 2)], sig=(k == KC - 1))
            dstT = (qT_na if fc < 4 else kT_na)[fc % 4]
            evac(dstT[:, blk], pa[:, 0:B2], [], [("pA", fc % 2), ("qk", fc)])

    def na_v(b):
        xnT = xnT2[b % 2]
        for i in range(TP2):
            pa = pA[i % 2]
            for k in range(KC):
                T.op("pe", lambda e: e.matmul(pa[:], lhsT=xnT[:, k, i * 128:(i + 1) * 128], rhs=Wna[:, k, 1024:1536],
                                              start=(k == 0), stop=(k == KC - 1)),
                     reads=[("Wna", 2), ("xnT", b % 2, i)], writes=[("pA", i % 2)], sig=(k == KC - 1))
            evac(V_na[:, b * TP2 + i, :, 0:64], pa[:, :].rearrange("p (h d) -> p h d", h=8), [],
                 [("pA", i % 2), ("vna", i)])

    def k_rope(b):
        blk = slice(b * B2, (b + 1) * B2)
        xnT = xnT2[b % 2]
        XR = [("xnT", b % 2, i) for i in range(TP2)]
        for v in range(2):
            for k in range(KC):
                T.op("pe", lambda e: e.matmul(pK[v][0:96, 0:B2], lhsT=Wlat[:, k, 384 + v * 96:480 + v * 96],
                                              rhs=xnT[:, k, :], start=(k == 0), stop=(k == KC - 1)),
                     reads=["Wlat"] + XR, writes=[pKkey[v]], sig=(k == KC - 1))
            T.op("dve", lambda e: e.tensor_tensor(out=tmp[64:96, v, :], in0=pK[v][64:96, 0:B2],
                                                  in1=ropeb[b % 3][64:96, v, :], op=ALU.mult),
                 reads=[("ropeb", b % 3, v)], writes=[pKkey[v], ("tmp", v)])
        T.op("dve", lambda e: e.tensor_tensor(out=krotT[64:96, blk], in0=tmp[64:96, 0, :], in1=tmp[64:96, 1, :],
                                              op=ALU.add),
             reads=[("tmp", 0), ("tmp", 1)], writes=[("krot", b)])

    def proj_block(b):
        cq = lambda ch: cqnT[:, ch, :]
        ckv = lambda ch: ckvnT
        lat_mm(b, 2, 0, CQB)
        na_qk(b, range(0, 4))
        lat_sum(b, 2, 256.0)
        if b + 2 < NB2:
            norm_stage(b + 2)
        lat_mm(b, 1, 256, CKVB)
        na_qk(b, range(4, 6))
        lat_scale(b, 2, 0, 24, cq, CQB)
        na_qk(b, range(6, 8))
        na_v(b)
        lat_sum(b, 1, 128.0)
        k_rope(b)
        lat_scale(b, 1, 256, 26, ckv, CKVB)

    norm_stage(0)
    transpose_stage(0)
    norm_stage(1)
    transpose_stage(1)
    for b in range(NB2):
        proj_block(b)
        if b + 2 < NB2:
            transpose_stage(b + 2)


def p3_na(nc, T, C, es, qT_na, kT_na, V_na):
    identF, oscr, natab = C["identF"], C["oscr"], C["natab"]
    Eb = [_sb(nc, es, "Eb%d" % i, [128, TAB_W], BF16) for i in range(2)]
    stg = [_sb(nc, es, "ebstg%d" % i, [128, TQ], F32) for i in range(2)]
    PT2 = [_sb(nc, es, "p3PT%d" % i, [128, 2 * BLK], BF16) for i in range(3)]
    oT = [_sb(nc, es, "p3oT%d" % i, [65, BLK], F32) for i in range(2)]
    otok = [_sb(nc, es, "p3otok%d" % i, [128, 4, 64], F32) for i in range(2)]
    rc = _sb(nc, es, "p3rc", [128, 8], F32)
    qz = [_sb(nc, es, "p3qz%d" % i, [128, BLK], BF16) for i in range(2)]
    pS2 = [_ps(nc, es, "p3pS%d" % i, [128, 2 * BLK], F32) for i in range(2)]
    pO = [_ps(nc, es, "p3pO%d" % i, [128, BLK], F32) for i in range(2)]

    def q_prep(gi):
        h, g = divmod(gi, 8)
        fc, hp = h // 2, h % 2
        rows = slice(hp * 64, hp * 64 + 64)
        other = slice((1 - hp) * 64, (1 - hp) * 64 + 64)
        par = gi % 2
        if g < 2:
            T.op("pool", lambda e: e.memset(qz[par][other, :], 0.0), writes=[("qz", par)])
        T.op("pool", lambda e: e.tensor_copy(out=qz[par][rows, :], in_=qT_na[fc][rows, g * BLK:(g + 1) * BLK]),
             writes=[("qz", par)])
    pF = _ps(nc, es, "p3pF", [128, 4, 66], F32)

    def eb_dma(h, q):
        T.dma("sp", stg[q % 2][:], natab[h, :, q * TQ:(q + 1) * TQ], writes=[("stg", q % 2)])

    def eb_exp(h, q):
        T.op("act", lambda e: e.activation(out=Eb[h % 2][:, q * TQ:(q + 1) * TQ], in_=stg[q % 2][:], func=AF.Copy,
                                           scale=8.0),
             reads=[("stg", q % 2)], writes=[("Eb", h % 2, q)])

    def load_eb(h):
        for q in range(4):
            eb_dma(h, q)
            eb_exp(h, q)

    items = []
    gi = 0
    for h in range(8):
        fc, hp = h // 2, h % 2
        rows = slice(hp * 64, hp * 64 + 64)
        eb = Eb[h % 2]
        for g in range(8):
            if g == 0:
                chunks = [(2 * j, FIRST0 + j * 512) for j in range(6)]
            elif g == 7:
                chunks = [(52 + 2 * j, LAST0 + j * 512) for j in range(6)]
            else:
                chunks = [(8 * g - 4 + 2 * j, (14 - 2 * j) * 64) for j in range(8)]
            qblk = slice(g * BLK, (g + 1) * BLK)
            chs = []
            for (kr0, o) in chunks:
                ct = kr0 // 2

                def qk(e, ps, stop, c0, c1, ct=ct, fc=fc, par=gi % 2):
                    return e.matmul(ps, lhsT=kT_na[fc][:, ct * 128:(ct + 1) * 128], rhs=qz[par][:, c0:c1],
                                    start=True, stop=stop)
                chd = dict(qk=qk, v=V_na[:, ct, h, 0:65], mask=eb[:, o:o + 512])
                if NA_SUBRANGE and 1 <= g <= 6:
                    j = len(chs)
                    i0, i1 = max(0, 2 * j - 7), min(7, 2 * j + 1)
                    chd["cr"] = (i0 * 64, (i1 + 1) * 64)
                chs.append(chd)
            n_it = len(chs) // 2
            for a in range(n_it):
                items.append(dict(chunks=chs[2 * a:2 * a + 2], first=(a == 0), last=(a == n_it - 1),
                                  pO=pO[gi % 2], pOkey=("pO", gi % 2), qk_reads=[("qz", gi % 2)], v_reads=[], a=a,
                                  mask_reads=[("Eb", h % 2, q) for q in range(4)], h=h, g=g, gi=gi,
                                  newhead=(g == 0 and a == 0)))
            gi += 1

    load_eb(0)
    q_prep(0)

    def after_back(k, it, deferred):
        if it["gi"] == 0 and it["a"] == 0:
            deferred.append(lambda: (eb_dma(1, 0), eb_dma(1, 1)))
        if it["gi"] == 1 and it["a"] == 0:
            deferred.append(lambda: (eb_exp(1, 0), eb_exp(1, 1), eb_dma(1, 2), eb_dma(1, 3)))
        if it["gi"] == 3 and it["a"] == 0:
            deferred.append(lambda: (eb_exp(1, 2), eb_exp(1, 3)))
        if it["a"] == 0 and it["gi"] + 1 < 64:
            q_prep(it["gi"] + 1)
        if it["last"]:
            h, g, gi_ = it["h"], it["g"], it["gi"]
            dst = oscr[g * BLK:(g + 1) * BLK, h * 64:(h + 1) * 64].rearrange("(i p) d -> p i d", p=128)
            attn_finalize(T, it["pO"], it["pOkey"], pF, "pF", oT, otok, rc, identF, gi_ % 2, dst, deferred)
            if g == 5 and h + 2 < 8:
                deferred.append(lambda: (eb_dma(h + 2, 0), eb_dma(h + 2, 1)))
            if g == 0 and 1 <= h and h + 1 < 8:
                deferred.append(lambda: (eb_exp(h + 1, 0), eb_exp(h + 1, 1), eb_dma(h + 1, 2), eb_dma(h + 1, 3)))
            if g == 3 and 1 <= h and h + 1 < 8:
                deferred.append(lambda: (eb_exp(h + 1, 2), eb_exp(h + 1, 3)))

    attn_stream(T, items, pS2, PT2, None, 0.125, after_back, bias_ident=C["ident"])


def p4_mla(nc, T, C, es, cqnT, ckvnT, krotT, rope):
    identF, oscr = C["identF"], C["oscr"]
    Wuq = _sb(nc, es, "Wuq", [128, 2, 768], BF16)
    Wuqs = _sb(nc, es, "Wuqs", [128, 2, 768], BF16)
    Wukv = _sb(nc, es, "Wukv", [128, 1024], BF16)
    V_all = _sb(nc, es, "Vall", [128, 32, 8, 66], BF16)
    kT = [_sb(nc, es, "p4kT%d" % i, [96, S], BF16) for i in range(2)]
    qTb = [_sb(nc, es, "p4qT%d" % i, [96, BLK], BF16) for i in range(2)]
    tmp = _sb(nc, es, "p4tmp", [96, 2, BLK], F32)
    ropeb = [_sb(nc, es, "p4rope%d" % i, [96, 2, BLK], F32) for i in range(2)]
    PT2 = [_sb(nc, es, "p4PT%d" % i, [128, 2 * BLK], BF16) for i in range(3)]
    oT = [_sb(nc, es, "p4oT%d" % i, [65, BLK], F32) for i in range(2)]
    otok = [_sb(nc, es, "p4otok%d" % i, [128, 4, 64], F32) for i in range(2)]
    rc = _sb(nc, es, "p4rc", [128, 8], F32)
    pS2 = [_ps(nc, es, "p4pS%d" % i, [128, 2 * BLK], F32) for i in range(2)]
    pO = [_ps(nc, es, "p4pO%d" % i, [128, BLK], F32) for i in range(2)]
    pX = [_ps(nc, es, "p4pX%d" % i, [128, BLK], F32) for i in range(2)]
    pF = pX[1][:, 0:264].rearrange("p (i d) -> p i d", i=4)

    T.dma("pool", Wuq[:, :, :], C["w_uq"].rearrange("(k p) f -> p k f", p=128), writes=["Wuq"], key=("Wgu", 0, 0))
    T.dma("pool", Wuqs[:, :, :], C["w_uqs"].rearrange("(k p) f -> p k f", p=128), writes=["Wuqs"], key=("Wgu", 0, 1))
    T.dma("pool", Wukv[:, :], C["w_ukv"][:, :], writes=["Wukv"], key=("Wgu", 0, 2))
    T.op("pool", lambda e: e.memset(V_all[:, :, :, 64:66], 1.0), writes=["Vall1"])

    wv = Wukv[:, :].rearrange("p (h t d) -> p h t d", h=8, t=2)
    for t in range(32):
        px = pX[t % 2]
        T.op("pe", lambda e: e.matmul(px[:], lhsT=ckvnT[:, t * 128:(t + 1) * 128], rhs=wv[:, :, 1, :],
                                      start=True, stop=True), reads=["Wukv"], writes=[("pX", t % 2)])
        if t % 2:
            T.op("act", lambda e: e.copy(out=V_all[:, t, :, 0:64], in_=px[:, :].rearrange("p (h d) -> p h d", h=8)),
                 writes=[("pX", t % 2), ("v_all", 1)])
        else:
            T.op("dve", lambda e: e.tensor_copy(out=V_all[:, t, :, 0:64], in_=px[:, :].rearrange("p (h d) -> p h d", h=8)),
                 writes=[("pX", t % 2), ("v_all", 0)])

    def k_gen(h):
        kt = kT[h % 2]
        for b in range(NBLK):
            px = pX[b % 2]
            blk = slice(b * BLK, (b + 1) * BLK)
            T.op("pe", lambda e: e.matmul(px[0:64, :], lhsT=Wukv[:, h * 128:h * 128 + 64], rhs=ckvnT[:, blk],
                                          start=True, stop=True), reads=["Wukv"], writes=[("pX", b % 2)])
            T.op("dve", lambda e: e.tensor_copy(out=kt[0:64, blk], in_=px[0:64, :]),
                 writes=[("pX", b % 2), ("kTn", h % 2)])
        T.op("dve", lambda e: e.tensor_copy(out=kt[64:96, :], in_=krotT[64:96, :]), writes=[("kTr", h % 2)])

    def q_gen(h, g, par):
        blk = slice(g * BLK, (g + 1) * BLK)
        T.dma("sp", ropeb[par][64:96, 0, :], C["cosT"][:, blk], writes=[("ropeb", par, 0)])
        T.dma("sp", ropeb[par][64:96, 1, :], C["sinT"][:, blk], writes=[("ropeb", par, 1)])
        for v, W in ((0, Wuq), (1, Wuqs)):
            for kc in range(2):
                T.op("pe", lambda e: e.matmul(pX[v][0:96, :], lhsT=W[:, kc, h * 96:(h + 1) * 96], rhs=cqnT[:, kc, blk],
                                              start=(kc == 0), stop=(kc == 1)),
                     reads=["Wuq", "Wuqs"], writes=[("pX", v)], sig=(kc == 1))
        T.op("dve", lambda e: e.tensor_copy(out=qTb[par][0:64, :], in_=pX[0][0:64, :]),
             writes=[("pX", 0), ("qTbn", par)])
        for v in range(2):
            T.op("dve", lambda e: e.tensor_tensor(out=tmp[64:96, v, :], in0=pX[v][64:96, :], in1=ropeb[par][64:96, v, :],
                                                  op=ALU.mult),
                 reads=[("ropeb", par, v)], writes=[("pX", v), ("tmp", v)])
        T.op("dve", lambda e: e.tensor_tensor(out=qTb[par][64:96, :], in0=tmp[64:96, 0, :], in1=tmp[64:96, 1, :],
                                              op=ALU.add),
             reads=[("tmp", 0), ("tmp", 1)], writes=[("qTb", par)])

    sc = float(96 ** -0.5)
    groups = [(h, g) for h in range(8) for g in range(8)]
    items = []
    for gi, (h, g) in enumerate(groups):
        par = gi % 2
        kt = kT[h % 2]
        for a in range(16):
            chs = []
            for c in range(2):
                j = 2 * a + c

                def qk(e, ps, stop, c0, c1, j=j, kt=kt, par=par):
                    return e.matmul(ps, lhsT=kt[0:96, j * 128:(j + 1) * 128], rhs=qTb[par][0:96, c0:c1],
                                    start=True, stop=stop)
                chs.append(dict(qk=qk, v=V_all[:, j, h, 0:65], mask=None))
            items.append(dict(chunks=chs, first=(a == 0), last=(a == 15), pO=pO[gi % 2], pOkey=("pO", gi % 2),
                              qk_reads=[("kTn", h % 2), ("kTr", h % 2), ("qTb", par), ("qTbn", par)],
                              v_reads=[("v_all", 0), ("v_all", 1), "Vall1"], mask_reads=[], h=h, g=g, gi=gi, a=a))

    k_gen(0)
    q_gen(0, 0, 0)

    def after_back(k, it, deferred):
        gi, a = it["gi"], it["a"]
        if a == 3 and it["g"] == 0 and it["h"] + 1 < 8:
            k_gen(it["h"] + 1)
        if a == 10 and gi + 1 < len(groups):
            nh, ng = groups[gi + 1]
            q_gen(nh, ng, (gi + 1) % 2)
        if it["last"]:
            h, g = it["h"], it["g"]
            dst = oscr[g * BLK:(g + 1) * BLK, 512 + h * 64:512 + (h + 1) * 64].rearrange("(i p) d -> p i d", p=128)
            attn_finalize(T, it["pO"], it["pOkey"], pF, ("pX", 1), oT, otok, rc, identF, gi % 2, dst, deferred)

    attn_stream(T, items, pS2, PT2, None, sc, after_back)


def p5a_wout(nc, T, C, prefetch=None):
    gcols, ident, eps1, h_scr, oscr = C["gcols"], C["ident"], C["eps1"], C["h_scr"], C["oscr"]
    with ExitStack() as es:
        Wout = _sb(nc, es, "Wout", [128, KC, D], BF16)
        Sx = [_sb(nc, es, "p5Sx%d" % i, [128, D], F32) for i in range(4)]
        Rx = [_sb(nc, es, "p5Rx%d" % i, [128, D], F32) for i in range(4)]
        Tt = [_sb(nc, es, "p5Tt%d" % i, [128, D], F32) for i in range(4)]
        xn = [_sb(nc, es, "p5xn%d" % i, [128, D], BF16) for i in range(4)]
        xnT2 = [_sb(nc, es, "p5xnT%d" % i, [128, KC, BLK], BF16) for i in range(2)]
        gpost = _sb(nc, es, "p5gpost", [128, D], F32)
        junk = _sb(nc, es, "p5junk", [128, D], BF16)
        st = _sb(nc, es, "p5st", [128, 64], F32)
        pT2 = [_ps(nc, es, "p5pT%d" % i, [128, KC, 128], BF16) for i in range(2)]
        pD = [_ps(nc, es, "p5pD%d" % i, [128, BLK], F32) for i in range(4)]
        wv = C["w_out"].rearrange("(k p) d -> p k d", p=128)
        for j in range(2):
            T.dma("pool", Wout[:, :, j * 512:(j + 1) * 512], wv[:, :, j * 512:(j + 1) * 512],
                  writes=[("Wout", j)], key=("Wgu", 0, j))
        T.dma("sp", gpost[:], C["grows"][1:2, :].partition_broadcast(128), writes=["gpost"])
        if prefetch is not None:
            prefetch()

        def load_sx(t):
            if t < S // 128:
                T.dma("sp", Sx[t % 4][:], oscr[t * 128:(t + 1) * 128, :], writes=[("Sx", t % 4)])

        def norm_a(b, i):
            t = b * 4 + i
            sl = t % 4
            load_sx(t + 2)
            col = (b % 2) * 16 + i * 4
            for hf in range(2):
                hs = slice(hf * 512, (hf + 1) * 512)
                T.op("act", lambda e: e.activation(out=junk[:, hs], in_=Sx[sl][:, hs], func=AF.Square,
                                                   scale=float(512 ** -0.5), accum_out=st[:, col + hf:col + hf + 1]),
                     reads=[("Sx", sl)], writes=[("junk", hf), ("st", col + hf)])
            T.op("act", lambda e: e.activation(out=st[:, col + 2:col + 4], in_=st[:, col:col + 2], func=AF.Sqrt,
                                               bias=eps1[:, 0:1], scale=1.0),
                 reads=[("st", col), ("st", col + 1)], writes=[("st", col + 2), ("st", col + 3)])
            T.op("dve", lambda e: e.reciprocal(out=st[:, col + 2:col + 4], in_=st[:, col + 2:col + 4]),
                 writes=[("st", col + 2), ("st", col + 3)])

        def norm_b(b, i):
            t = b * 4 + i
            sl = t % 4
            col = (b % 2) * 16 + i * 4
            for hf in range(2):
                hs = slice(hf * 512, (hf + 1) * 512)
                T.op("act", lambda e: e.activation(out=xn[i][:, hs], in_=Sx[sl][:, hs], func=AF.Copy,
                                                   scale=st[:, col + 2 + hf:col + 3 + hf]),
                     reads=[("Sx", sl), ("st", col + 2 + hf)], writes=[("xn", i, hf)])

        def norm_stage(b):
            for i in range(4):
                norm_a(b, i)
                norm_b(b, i)

        def transpose_stage(b):
            for i in range(4):
                pT = pT2[i % 2]
                for k in range(KC):
                    T.op("pe", lambda e: e.transpose(pT[:, k, :], xn[i][:, k * 128:(k + 1) * 128], ident[:]),
                         reads=[("xn", i, k // 4)], writes=[("pT", i % 2)], sig=(k == KC - 1))
                T.op("dve", lambda e: e.tensor_tensor(
                    out=xnT2[b % 2][:, :, i * 128:(i + 1) * 128], in0=pT[:, :, :],
                    in1=gcols[:, 27:35].unsqueeze(2).to_broadcast([128, KC, 128]), op=ALU.mult),
                    writes=[("pT", i % 2), ("xnT", b % 2, i)])

        def out_tile(b, i):
            t = b * 4 + i
            sl = t % 4
            T.dma("sp", Rx[sl][:], h_scr[t * 128:(t + 1) * 128, :], writes=[("Rx", sl)])
            col = 32 + (t % 4) * 4
            for n in range(2):
                pdi = (i % 2) * 2 + n
                pd = pD[pdi]
                for k in range(KC):
                    T.op("pe", lambda e: e.matmul(pd[:], lhsT=xnT2[b % 2][:, k, i * 128:(i + 1) * 128],
                                                  rhs=Wout[:, k, n * 512:(n + 1) * 512],
                                                  start=(k == 0), stop=(k == KC - 1)),
                         reads=[("xnT", b % 2, i), ("Wout", n)], writes=[("pD", pdi)], sig=(k == KC - 1))
                T.op("act", lambda e: e.activation(out=junk[:, 0:512], in_=pd[:], func=AF.Square,
                                                   scale=float(D ** -0.5), accum_out=st[:, col + n:col + n + 1]),
                     writes=[("pD", pdi), ("junk", 0), ("st", col + n)])
                T.op("dve", lambda e: e.tensor_tensor(out=Tt[sl][:, n * 512:(n + 1) * 512], in0=pd[:],
                                                      in1=gpost[:, n * 512:(n + 1) * 512], op=ALU.mult),
                     reads=["gpost"], writes=[("pD", pdi), ("Tt", sl, n)])
            T.op("dve", lambda e: e.tensor_tensor(out=st[:, col + 2:col + 3], in0=st[:, col:col + 1],
                                                  in1=st[:, col + 1:col + 2], op=ALU.add),
                 reads=[("st", col), ("st", col + 1)], writes=[("st", col + 2)])
            T.op("act", lambda e: e.activation(out=st[:, col + 3:col + 4], in_=st[:, col + 2:col + 3], func=AF.Sqrt,
                                               bias=eps1[:, 0:1], scale=1.0),
                 reads=[("st", col + 2)], writes=[("st", col + 3)])
            T.op("dve", lambda e: e.reciprocal(out=st[:, col + 3:col + 4], in_=st[:, col + 3:col + 4]),
                 reads=[("st", col + 3)], writes=[("st", col + 3)])
            T.op("dve", lambda e: e.scalar_tensor_tensor(out=Rx[sl][:], in0=Tt[sl][:], scalar=st[:, col + 3:col + 4],
                                                         in1=Rx[sl][:], op0=ALU.mult, op1=ALU.add),
                 reads=[("Tt", sl, 0), ("Tt", sl, 1), ("st", col + 3), ("Rx", sl)], writes=[("Rx", sl)])
            T.dma("sp", h_scr[t * 128:(t + 1) * 128, :], Rx[sl][:], reads=[("Rx", sl)], key=("Rxo", sl))

        load_sx(0)
        load_sx(1)
        norm_stage(0)
        transpose_stage(0)
        norm_stage(1)
        transpose_stage(1)
        for b in range(NBLK):
            for i in range(4):
                if b + 2 < NBLK:
                    norm_a(b + 2, i)
                out_tile(b, i)
                if b + 2 < NBLK:
                    norm_b(b + 2, i)
            if b + 2 < NBLK:
                transpose_stage(b + 2)
        T.barrier()


PH_ALL = ("f1", "mix", "wout", "f2")


def build_program(phases=PH_ALL, dbg_out=None, dbg_h_from_x=False):
    nc = bass.Bass("TRN2", target_bir_lowering=False)
    dram_in = lambda n, shp, dt=F32: nc.dram_tensor(n, list(shp), dt, kind="ExternalInput").ap()
    C = {}
    x = dram_in("x", [S, D])
    for nm, shp in (("w_gu1", [D, 2 * DFF]), ("w_dn1", [DFF, D]), ("w_gu2", [D, 2 * DFF]), ("w_dn2", [DFF, D]),
                    ("w_na", [D, 1536]), ("w_lat", [D, 576]), ("w_uq", [256, 768]), ("w_uqs", [256, 768]),
                    ("w_ukv", [128, 1024]), ("w_out", [D, D]), ("gcols_d", [128, 40]), ("grows", [4, D]),
                    ("ident_d", [128, 128]), ("cosT", [32, S]), ("sinT", [32, S]), ("natab", [8, 128, TAB_W])):
        C[nm] = dram_in(nm, shp)
    out = nc.dram_tensor("out", [S, D], F32, kind="ExternalOutput").ap()
    if dbg_out == "oscr":
        C["h_scr"] = nc.dram_tensor("h_scr", [S, D], F32).ap()
        C["oscr"] = out
    elif dbg_out == "h_scr":
        C["h_scr"] = out
        C["oscr"] = nc.dram_tensor("oscr", [S, D], F32).ap()
    else:
        C["h_scr"] = nc.dram_tensor("h_scr", [S, D], F32).ap()
        C["oscr"] = nc.dram_tensor("oscr", [S, D], F32).ap()

    if dbg_h_from_x:
        C["h_scr"] = x
    with ExitStack() as es:
        T = Tracker(nc, es)
        gcols = _sb(nc, es, "sb_gcols", [128, 40], F32)
        ident = _sb(nc, es, "sb_ident", [128, 128], BF16)
        identF = _sb(nc, es, "sb_identF", [128, 128], F32)
        ones_bf = _sb(nc, es, "sb_ones", [128, 128], BF16)
        eps1 = _sb(nc, es, "eps1", [128, 1], F32)
        eps4 = _sb(nc, es, "eps4", [128, 1], F32)
        T.dma("sp", gcols[:], C["gcols_d"][:, :], writes=["gcols"])
        T.dma("pool", ident[:], C["ident_d"][:, :], writes=["ident"])
        T.dma("sp", identF[:], C["ident_d"][:, :], writes=["identF"])
        T.op("dve", lambda e: e.memset(eps1[:], EPS), writes=["eps1"])
        T.op("dve", lambda e: e.memset(eps4[:], 4 * EPS), writes=["eps4"])
        T.op("dve", lambda e: e.memset(ones_bf[:], 1.0), writes=["ones"])
        T.barrier()
        C.update(gcols=gcols, ident=ident, identF=identF, ones_bf=ones_bf, eps1=eps1, eps4=eps4)
        h_scr = C["h_scr"]
        if "f1" in phases:
            ffn_phase(nc, T, "f1", x, h_scr, C["w_gu1"], C["w_dn1"], gcols, 0, C["grows"][0:1, :], ident, eps1, eps4)
        if "mix" in phases:
            mixer_phase(nc, T, C)
        if "wout" in phases:
            p5a_wout(nc, T, C)
        if "f2" in phases:
            ffn_phase(nc, T, "f2", h_scr, out, C["w_gu2"], C["w_dn2"], gcols, 16, C["grows"][2:3, :], ident, eps1, eps4)
        T.barrier()
    return nc


def _na_bias_tables(rpb):
    H = rpb.shape[0]
    kl = np.arange(128) // 64
    kc = np.arange(128) % 64
    qc = np.arange(64)
    ws = np.clip(qc - 8, 0, 48)
    colv = (kc[:, None] >= ws[None, :]) & (kc[:, None] < ws[None, :] + 16)
    coff = np.clip(kc[:, None] - qc[None, :] + 15, 0, 30)
    tab = np.full((H, 128, TAB_W), NEG, np.float32)
    for e in range(22):
        dr = (10 - e) + kl
        rowv = (dr >= -4) & (dr <= 3)
        roff = np.clip(dr + 7, 0, 14)
        vals = rpb[:, roff[:, None], coff]
        m = rowv[:, None] & colv
        tab[:, :, e * 64:(e + 1) * 64] = np.where(m[None], vals, np.float32(NEG))
    for base, r0, k0 in ((FIRST0, 0, 0), (LAST0, 56, 52)):
        for j in range(6):
            for i in range(8):
                r = r0 + i
                rs = min(max(r - 4, 0), 56)
                kr = k0 + 2 * j + kl
                rowv = (kr >= rs) & (kr < rs + 8)
                roff = np.clip(kr - r + 7, 0, 14)
                vals = rpb[:, roff[:, None], coff]
                m = rowv[:, None] & colv
                o = base + j * 512 + i * 64
                tab[:, :, o:o + 64] = np.where(m[None], vals, np.float32(NEG))
    return tab


def _rope_tables():
    t = np.arange(S)
    row = (t // 64).astype(np.float64)
    col = (t % 64).astype(np.float64)
    inv = 1.0 / (10000.0 ** (np.arange(8, dtype=np.float64) / 8.0))
    ang = np.concatenate([row[:, None] * inv[None, :], col[:, None] * inv[None, :]], axis=-1)
    cos, sin = np.cos(ang), np.sin(ang)
    cosT = np.concatenate([cos.T, cos.T], axis=0).astype(np.float32)
    sinT = np.concatenate([-sin.T, sin.T], axis=0).astype(np.float32)
    return np.ascontiguousarray(cosT), np.ascontiguousarray(sinT)


def host_prep(inp):
    p = {k: np.asarray(v) for k, v in inp.items()}
    L = 0
    f = np.float32
    gcols = np.zeros((128, 40), f)
    gcols[:, 0:8] = p["ffn1_pre_g"][L].reshape(8, 128).T
    gcols[:, 8:16] = p["mix_pre_g"][L].reshape(8, 128).T
    gcols[:, 16:24] = p["ffn2_pre_g"][L].reshape(8, 128).T
    gcols[:, 24:26] = p["mla_q_norm_g"][L].reshape(2, 128).T
    gcols[:, 26:27] = p["mla_kv_norm_g"][L].reshape(1, 128).T
    gcols[:, 27:35] = np.concatenate([p["na_out_norm_g"][L], p["mla_out_norm_g"][L]]).reshape(8, 128).T
    grows = np.stack([p["ffn1_post_g"][L], p["mix_post_g"][L], p["ffn2_post_g"][L], p["ffn2_post_g"][L]]).astype(f)
    w_in = p["w_in"][L]
    kr = w_in[:, 1920:1952]
    krs = np.concatenate([kr[:, 16:32], kr[:, 0:16]], axis=1)
    z64 = np.zeros((D, 64), f)
    w_lat = np.concatenate([w_in[:, 1536:1920], z64, kr, z64, krs], axis=1)
    w_uq = p["mla_w_uq"][L]
    w_uqs = w_uq.reshape(256, 8, 96).copy()
    w_uqs[:, :, 64:80] = w_uq.reshape(256, 8, 96)[:, :, 80:96]
    w_uqs[:, :, 80:96] = w_uq.reshape(256, 8, 96)[:, :, 64:80]
    cosT, sinT = _rope_tables()
    c = np.ascontiguousarray
    shared = dict(
        w_gu1=c(p["ffn1_w_gu"][L]), w_dn1=c(p["ffn1_w_down"][L]), w_gu2=c(p["ffn2_w_gu"][L]), w_dn2=c(p["ffn2_w_down"][L]),
        w_na=c(w_in[:, 0:1536]), w_lat=c(w_lat.astype(f)), w_uq=c(w_uq), w_uqs=c(w_uqs.reshape(256, 768)),
        w_ukv=c(p["mla_w_ukv"][L]), w_out=c(p["w_out"][L]), gcols_d=gcols, grows=grows,
        ident_d=np.eye(128, dtype=f), cosT=cosT, sinT=sinT, natab=_na_bias_tables(p["na_rpb"][L].astype(f)),
    )
    return p, shared


def kernel(**inputs):
    p, shared = host_prep(inputs)
    nc = build_program()
    n = 8
    in_maps = [dict(shared, x=np.ascontiguousarray(p["x"][c], dtype=np.float32)) for c in range(n)]
    res = run_bass_kernel_spmd(nc, in_maps, core_ids=list(range(n)))
    return np.stack([np.asarray(r["out"]) for r in res.results], axis=0).astype(np.float32)
```

```python
import numpy as np
from contextlib import ExitStack
import concourse.bass as bass
import concourse.mybir as mybir
from concourse.bass_utils import run_bass_kernel_spmd

F32 = mybir.dt.float32
BF16 = mybir.dt.bfloat16
AF = mybir.ActivationFunctionType
ALU = mybir.AluOpType

S = 4096
D = 1024
DFF = 2816
NFC = DFF // 128
KC = D // 128
BLK = 512
NBLK = S // BLK
EPS = 1e-6
NEG = -1e30


class Tracker:
    def __init__(self, nc, es):
        self.nc = nc
        self.es = es
        self.eng = dict(pe=nc.tensor, act=nc.scalar, dve=nc.vector, pool=nc.gpsimd, sp=nc.sync)
        self.sem = {k: es.enter_context(nc.semaphore("s_" + k)) for k in ("pe", "act", "dve", "pool")}
        self.cnt = {k: 0 for k in self.sem}
        self.pending = {k: False for k in self.sem}
        self.seen = {k: {} for k in self.eng}
        self.res = {}
        self.dsem = {}
        self.dcnt = {}
        self.semobj = {}

    def _sid(self, s):
        self.semobj[id(s)] = s
        return id(s)

    def _deps(self, reads, writes):
        deps = []
        for k in reads:
            r = self.res.get(k)
            if r and r[0]:
                deps.append(r[0])
        for k in writes:
            r = self.res.get(k)
            if r:
                if r[0]:
                    deps.append(r[0])
                deps.extend(r[1])
        return deps

    def _wait(self, e, deps):
        best = {}
        for (s, v) in deps:
            sid = self._sid(s)
            if e == "pe" and s is self.sem["pe"]:
                continue
            if v > best.get(sid, 0):
                best[sid] = v
        for sid, v in best.items():
            if self.seen[e].get(sid, 0) >= v:
                continue
            self.eng[e].wait_ge(self.semobj[sid], v)
            self.seen[e][sid] = v

    def _record(self, tok, reads, writes):
        for k in reads:
            r = self.res.setdefault(k, [None, []])
            r[1].append(tok)
            if len(r[1]) > 32:
                r[1] = self._compact(r[1])
        for k in writes:
            self.res[k] = [tok, []]

    @staticmethod
    def _compact(lst):
        best = {}
        for (s, v) in lst:
            if v > best.get(id(s), (None, 0))[1]:
                best[id(s)] = (s, v)
        return list(best.values())

    def op(self, e, fn, reads=(), writes=(), sig=True):
        self._wait(e, self._deps(reads, writes))
        ins = fn(self.eng[e])
        if sig:
            self.cnt[e] += 1
            ins.then_inc(self.sem[e], 1)
            self.pending[e] = False
            tok = (self.sem[e], self.cnt[e])
        else:
            self.pending[e] = True
            tok = (self.sem[e], self.cnt[e] + 1)
        self._record(tok, reads, writes)
        return ins

    def dma(self, q, out, in_, reads=(), writes=(), key=None):
        if key is None:
            key = (tuple(writes) + tuple(reads))[0]
        if key not in self.dsem:
            self.dsem[key] = self.es.enter_context(self.nc.semaphore("d%d" % len(self.dsem)))
            self.dcnt[key] = 0
        self._wait(q, self._deps(reads, writes))
        ins = self.eng[q].dma_start(out=out, in_=in_)
        self.dcnt[key] += 16
        ins.then_inc(self.dsem[key], 16)
        self._record((self.dsem[key], self.dcnt[key]), reads, writes)
        return ins

    def barrier(self, engines=("pe", "act", "dve", "pool", "sp")):
        for e in self.sem:
            assert not self.pending[e], e
        toks = [(self.sem[e], self.cnt[e]) for e in self.sem if self.cnt[e] > 0]
        toks += [(self.dsem[k], self.dcnt[k]) for k in self.dsem]
        for e in engines:
            best = {}
            for (s, v) in toks:
                best[self._sid(s)] = v
            for sid, v in best.items():
                if self.seen[e].get(sid, 0) >= v:
                    continue
                self.eng[e].wait_ge(self.semobj[sid], v)
                self.seen[e][sid] = v
        self.res = {}


def _sb(nc, es, name, shape, dt):
    return es.enter_context(nc.sbuf_tensor(name, list(shape), dt))


def _ps(nc, es, name, shape, dt):
    return es.enter_context(nc.psum_tensor(name, list(shape), dt))


def ffn_alloc_weights(nc, es, ph):
    Wgu = _sb(nc, es, ph + "Wgu", [128, KC, 2 * DFF], BF16)
    Wdn = _sb(nc, es, ph + "Wdn", [128, NFC, D], BF16)
    return Wgu, Wdn


def ffn_load_weights(T, Wgu, Wdn, w_gu, w_dn):
    wgu_v = w_gu.rearrange("(k p) f -> p k f", p=128)
    wdn_v = w_dn.rearrange("(c p) d -> p c d", p=128)
    for j in range(6):
        c0, c1 = j * 512, min((j + 1) * 512, DFF)
        for half in range(2):
            o = half * DFF
            T.dma("pool", Wgu[:, :, o + c0:o + c1], wgu_v[:, :, o + c0:o + c1],
                  writes=[("Wgu", half, j)], key=("Wgu", half, j))
    for j in range(2):
        T.dma("pool", Wdn[:, j * 11:(j + 1) * 11, :], wdn_v[:, j * 11:(j + 1) * 11, :],
              writes=[("Wdn", j)], key=("Wdn", j))


def ffn_phase(nc, T, ph, src, dst, w_gu, w_dn, gcols, gc0, grow, ident, eps1, eps4, W=None):
    with ExitStack() as es:
        if W is None:
            Wgu, Wdn = ffn_alloc_weights(nc, es, ph)
            ffn_load_weights(T, Wgu, Wdn, w_gu, w_dn)
        else:
            Wgu, Wdn = W
        Sx = [_sb(nc, es, ph + "Sx%d" % i, [128, D], F32) for i in range(2)]
        Rx = [_sb(nc, es, ph + "Rx%d" % i, [128, D], F32) for i in range(2)]
        Tt = [_sb(nc, es, ph + "Tt%d" % i, [128, D], F32) for i in range(2)]
        xn = [_sb(nc, es, ph + "xn%d" % i, [128, D], BF16) for i in range(4)]
        xnT = _sb(nc, es, ph + "xnT", [128, KC, BLK], BF16)
        hT = _sb(nc, es, ph + "hT", [128, NFC, BLK], BF16)
        sg = [_sb(nc, es, ph + "sg%d" % i, [128, BLK], BF16) for i in range(2)]
        gpost = _sb(nc, es, ph + "gpost", [128, D], F32)
        junk = _sb(nc, es, ph + "junk", [128, D], BF16)
        st = _sb(nc, es, ph + "st", [128, 64], F32)
        pT2 = [_ps(nc, es, ph + "pT%d" % i, [128, KC, 128], BF16) for i in range(2)]
        pG = [_ps(nc, es, ph + "pG%d" % i, [128, BLK], F32) for i in range(4)]
        pD = [_ps(nc, es, ph + "pD%d" % i, [128, BLK], F32) for i in range(2)]

        T.dma("sp", gpost[:], grow.partition_broadcast(128), writes=["gpost"])

        def wgu_key(c):
            return c * 128 // 512

        def norm_stage(b):
            for i in range(4):
                t = b * 4 + i
                sl = t % 2
                T.dma("sp", Sx[sl][:], src[t * 128:(t + 1) * 128, :], writes=[("Sx", sl)])
                col = (b % 2) * 8 + i
                T.op("act", lambda e: e.activation(out=junk[:], in_=Sx[sl][:], func=AF.Square,
                                                   scale=float(D ** -0.5), accum_out=st[:, col:col + 1]),
                     reads=[("Sx", sl)], writes=["junk", ("st", col)])
                T.op("act", lambda e: e.activation(out=st[:, col + 4:col + 5], in_=st[:, col:col + 1], func=AF.Sqrt,
                                                   bias=eps1[:, 0:1], scale=1.0),
                     reads=[("st", col)], writes=[("st", col + 4)])
                T.op("dve", lambda e: e.reciprocal(out=st[:, col + 4:col + 5], in_=st[:, col + 4:col + 5]),
                     reads=[("st", col + 4)], writes=[("st", col + 4)])
                T.op("dve", lambda e: e.tensor_scalar(out=xn[i][:], in0=Sx[sl][:], scalar1=st[:, col + 4:col + 5],
                                                      scalar2=None, op0=ALU.mult),
                     reads=[("Sx", sl), ("st", col + 4)], writes=[("xn", i)])

        def transpose_stage(b):
            for i in range(4):
                pT = pT2[i % 2]
                for k in range(KC):
                    T.op("pe", lambda e: e.transpose(pT[:, k, :], xn[i][:, k * 128:(k + 1) * 128], ident[:]),
                         reads=[("xn", i)], writes=[("pT", i % 2)], sig=(k == KC - 1))
                T.op("dve", lambda e: e.tensor_tensor(
                    out=xnT[:, :, i * 128:(i + 1) * 128], in0=pT[:, :, :],
                    in1=gcols[:, gc0:gc0 + KC].unsqueeze(2).to_broadcast([128, KC, 128]), op=ALU.mult),
                    writes=[("pT", i % 2), ("xnT", i)])

        def gu_chunk(b, c):
            pg, pu = pG[(c % 2) * 2], pG[(c % 2) * 2 + 1]
            for half, pp in ((0, pg), (1, pu)):
                o = half * DFF + c * 128
                for k in range(KC):
                    T.op("pe", lambda e: e.matmul(pp[:], lhsT=Wgu[:, k, o:o + 128], rhs=xnT[:, k, :],
                                                  start=(k == 0), stop=(k == KC - 1)),
                         reads=[("Wgu", half, wgu_key(c))] + [("xnT", i) for i in range(4)],
                         writes=[("pG", (c % 2) * 2 + half)], sig=(k == KC - 1))
            s = sg[c % 2]
            T.op("act", lambda e: e.activation(out=s[:], in_=pg[:], func=AF.Silu),
                 reads=[("pG", (c % 2) * 2)], writes=[("sg", c % 2)])
            T.op("dve", lambda e: e.tensor_tensor(out=hT[:, c, :], in0=s[:], in1=pu[:], op=ALU.mult),
                 reads=[("sg", c % 2), ("pG", (c % 2) * 2 + 1)], writes=[("hT", c)])

        def down_tile(b, i):
            t = b * 4 + i
            sl = t % 2
            T.dma("sp", Rx[sl][:], src[t * 128:(t + 1) * 128, :], writes=[("Rx", sl)])
            col = 16 + (t % 2) * 8
            for n in range(2):
                pd = pD[n]
                for c in range(NFC):
                    T.op("pe", lambda e: e.matmul(pd[:], lhsT=hT[:, c, i * 128:(i + 1) * 128],
                                                  rhs=Wdn[:, c, n * 512:(n + 1) * 512],
                                                  start=(c == 0), stop=(c == NFC - 1)),
                         reads=[("hT", c), ("Wdn", c // 11)], writes=[("pD", n)], sig=(c == NFC - 1))
                T.op("act", lambda e: e.activation(out=junk[:, 0:512], in_=pd[:], func=AF.Square,
                                                   scale=float(D ** -0.5), accum_out=st[:, col + n:col + n + 1]),
                     writes=[("pD", n), "junk", ("st", col + n)])
                T.op("dve", lambda e: e.tensor_tensor(out=Tt[sl][:, n * 512:(n + 1) * 512], in0=pd[:],
                                                      in1=gpost[:, n * 512:(n + 1) * 512], op=ALU.mult),
                     reads=["gpost"], writes=[("pD", n), ("Tt", sl, n)])
            T.op("dve", lambda e: e.tensor_tensor(out=st[:, col + 2:col + 3], in0=st[:, col:col + 1],
                                                  in1=st[:, col + 1:col + 2], op=ALU.add),
                 reads=[("st", col), ("st", col + 1)], writes=[("st", col + 2)])
            T.op("act", lambda e: e.activation(out=st[:, col + 3:col + 4], in_=st[:, col + 2:col + 3], func=AF.Sqrt,
                                               bias=eps4[:, 0:1], scale=4.0),
                 reads=[("st", col + 2)], writes=[("st", col + 3)])
            T.op("dve", lambda e: e.reciprocal(out=st[:, col + 3:col + 4], in_=st[:, col + 3:col + 4]),
                 reads=[("st", col + 3)], writes=[("st", col + 3)])
            T.op("dve", lambda e: e.scalar_tensor_tensor(out=Rx[sl][:], in0=Tt[sl][:], scalar=st[:, col + 3:col + 4],
                                                         in1=Rx[sl][:], op0=ALU.mult, op1=ALU.add),
                 reads=[("Tt", sl, 0), ("Tt", sl, 1), ("st", col + 3), ("Rx", sl)], writes=[("Rx", sl)])
            T.dma("sp", dst[t * 128:(t + 1) * 128, :], Rx[sl][:], reads=[("Rx", sl)], key=("Rxo", sl))

        norm_stage(0)
        transpose_stage(0)
        for b in range(NBLK):
            for c in range(NFC):
                gu_chunk(b, c)
                if c == 8 and b + 1 < NBLK:
                    norm_stage(b + 1)
            if b + 1 < NBLK:
                transpose_stage(b + 1)
            for i in range(4):
                down_tile(b, i)
        T.barrier()


TAB_W = 22 * 64 + 6 * 512 + 6 * 512
FIRST0 = 22 * 64
LAST0 = FIRST0 + 6 * 512
TQ = TAB_W // 4
NA_SUBRANGE = False


def attn_stream(T, items, pS2, PT2, PM2, scale, after_back, bias_ident=None):
    def rng(ch):
        return ch.get("cr", (0, 512))

    def front_pe(k):
        it = items[k]
        s = k % 2
        n = len(it["chunks"])
        for c, ch in enumerate(it["chunks"]):
            c0, c1 = rng(ch)
            ps = pS2[s][:, c * 512 + c0:c * 512 + c1]
            if bias_ident is None:
                T.op("pe", lambda e: ch["qk"](e, ps, True, c0, c1), reads=it["qk_reads"],
                     writes=[("pS", s)], sig=(c == n - 1))
            else:
                T.op("pe", lambda e: ch["qk"](e, ps, False, c0, c1), reads=it["qk_reads"],
                     writes=[("pS", s)], sig=False)
                T.op("pe", lambda e: e.matmul(ps, lhsT=bias_ident[:, :], rhs=ch["mask"][:, c0:c1],
                                              start=False, stop=True),
                     reads=it["mask_reads"], writes=[("pS", s)], sig=(c == n - 1))

    def front_act(k):
        it = items[k]
        s = k % 2
        s3 = k % 3
        n = len(it["chunks"])
        if all(rng(ch) == (0, 512) for ch in it["chunks"]):
            T.op("act", lambda e: e.activation(out=PT2[s3][:, 0:n * 512], in_=pS2[s][:, 0:n * 512], func=AF.Exp,
                                               scale=float(scale)), writes=[("pS", s), ("PT", s3)])
        else:
            for c, ch in enumerate(it["chunks"]):
                c0, c1 = rng(ch)
                T.op("act", lambda e: e.activation(out=PT2[s3][:, c * 512 + c0:c * 512 + c1],
                                                   in_=pS2[s][:, c * 512 + c0:c * 512 + c1], func=AF.Exp,
                                                   scale=float(scale)), writes=[("pS", s), ("PT", s3)])

    for k in range(min(2, len(items))):
        front_pe(k)
        front_act(k)
    deferred = []
    for k in range(len(items)):
        if k + 2 < len(items):
            front_pe(k + 2)
            front_act(k + 2)
        for fn in deferred:
            fn()
        deferred = []
        it = items[k]
        s3 = k % 3
        n = len(it["chunks"])
        for c, ch in enumerate(it["chunks"]):
            c0, c1 = rng(ch)
            T.op("pe", lambda e: e.matmul(it["pO"][0:65, c0:c1], lhsT=ch["v"], rhs=PT2[s3][:, c * 512 + c0:c * 512 + c1],
                                          start=(it["first"] and c == 0), stop=(it["last"] and c == n - 1),
                                          skip_group_check=True),
                 reads=[("PT", s3)] + it["v_reads"], writes=[it["pOkey"]], sig=True)
        after_back(k, it, deferred)
    for fn in deferred:
        fn()


def attn_finalize(T, pO, pOkey, pF, pFkey, oT, otok, rc, identF, par, dst_ap, deferred):
    T.op("dve", lambda e: e.tensor_copy(out=oT[par][0:65, :], in_=pO[0:65, :]), writes=[pOkey, ("oT", par)])

    def late():
        for i in range(4):
            T.op("pe", lambda e: e.transpose(pF[:, i, 0:65], oT[par][0:65, i * 128:(i + 1) * 128], identF[0:65, 0:65]),
                 reads=[("oT", par)], writes=[pFkey], sig=(i == 3))
        rcv = rc[:, par * 4:par * 4 + 4]
        T.op("dve", lambda e: e.reciprocal(out=rcv.unsqueeze(2), in_=pF[:, :, 64:65]), writes=[pFkey, ("rc", par)])
        T.op("dve", lambda e: e.tensor_tensor(out=otok[par][:, :, :], in0=pF[:, :, 0:64],
                                              in1=rcv.unsqueeze(2).to_broadcast([128, 4, 64]), op=ALU.mult),
             reads=[("rc", par)], writes=[pFkey, ("otok", par)])
        T.dma("sp", dst_ap, otok[par][:, :, :], reads=[("otok", par)], key=("otok_o", par))
    deferred.append(late)


def mixer_phase(nc, T, C):
    gcols, ident, identF, eps1, ones_bf = C["gcols"], C["ident"], C["identF"], C["eps1"], C["ones_bf"]
    h_scr, oscr = C["h_scr"], C["oscr"]
    with ExitStack() as es0:
        cqnT = _sb(nc, es0, "cqnT", [128, 2, S], BF16)
        ckvnT = _sb(nc, es0, "ckvnT", [128, S], BF16)
        krotT = _sb(nc, es0, "krotT", [96, S], BF16)
        rope = None
        with ExitStack() as esA:
            qT_na = [_sb(nc, esA, "qTna%d" % i, [128, S], BF16) for i in range(4)]
            kT_na = [_sb(nc, esA, "kTna%d" % i, [128, S], BF16) for i in range(4)]
            V_na = _sb(nc, esA, "Vna", [128, 32, 8, 66], BF16)
            T.op("pool", lambda e: e.memset(V_na[:, :, :, 64:66], 1.0), writes=["Vna1"])
            with ExitStack() as es:
                p2_proj(nc, T, C, es, cqnT, ckvnT, krotT, rope, qT_na, kT_na, V_na)
            T.barrier()
            with ExitStack() as es:
                p3_na(nc, T, C, es, qT_na, kT_na, V_na)
            T.barrier()
        with ExitStack() as es:
            p4_mla(nc, T, C, es, cqnT, ckvnT, krotT, rope)
        T.barrier()


B2 = 256
NB2 = S // B2
TP2 = B2 // 128


def p2_proj(nc, T, C, es, cqnT, ckvnT, krotT, rope, qT_na, kT_na, V_na):
    gcols, ident, eps1, ones_bf, h_scr = C["gcols"], C["ident"], C["eps1"], C["ones_bf"], C["h_scr"]
    Wna = _sb(nc, es, "Wna", [128, KC, 1536], BF16)
    Wlat = _sb(nc, es, "Wlat", [128, KC, 576], BF16)
    Sx = [_sb(nc, es, "p2Sx%d" % i, [128, D], F32) for i in range(2)]
    xn = [_sb(nc, es, "p2xn%d" % i, [128, D], BF16) for i in range(2)]
    xnT2 = [_sb(nc, es, "p2xnT%d" % i, [128, KC, B2], BF16) for i in range(2)]
    st = _sb(nc, es, "p2st", [128, 64], F32)
    sq = _sb(nc, es, "p2sq", [128, 2, 512], BF16)
    rs = _sb(nc, es, "p2rs", [128, B2], F32)
    tmp = _sb(nc, es, "p2tmp", [96, 2, B2], F32)
    ropeb = [_sb(nc, es, "p2rope%d" % i, [96, 2, B2], F32) for i in range(3)]
    pT2 = [_ps(nc, es, "p2pT%d" % i, [128, KC, 128], BF16) for i in range(2)]
    pA = [_ps(nc, es, "p2pA%d" % i, [128, BLK], F32) for i in range(2)]
    pC = [_ps(nc, es, "p2pC%d" % i, [128, BLK], F32) for i in range(2)]
    pSm = _ps(nc, es, "p2pS", [128, BLK], F32)
    pK1 = _ps(nc, es, "p2pK1", [128, BLK], F32)
    pK = [pC[1], pC[0]]
    pKkey = [("pC", 1), ("pC", 0)]
    CQB = [(pC[0], ("pC", 0)), (pC[1], ("pC", 1))]
    CKVB = [(pK1, ("pK", 1))]
    junk = sq[:, :, :].rearrange("p a b -> p (a b)")

    wna_v = C["w_na"].rearrange("(k p) f -> p k f", p=128)
    for j in range(3):
        T.dma("pool", Wna[:, :, j * 512:(j + 1) * 512], wna_v[:, :, j * 512:(j + 1) * 512],
              writes=[("Wna", j)], key=("Wgu", 0, j))
    T.dma("pool", Wlat[:, :, :], C["w_lat"].rearrange("(k p) f -> p k f", p=128), writes=["Wlat"], key=("Wgu", 0, 3))

    def norm_stage(b):
        blk = slice(b * B2, (b + 1) * B2)
        T.dma("sp", ropeb[b % 3][64:96, 0, :], C["cosT"][:, blk], writes=[("ropeb", b % 3, 0)])
        T.dma("sp", ropeb[b % 3][64:96, 1, :], C["sinT"][:, blk], writes=[("ropeb", b % 3, 1)])
        for i in range(TP2):
            t = b * TP2 + i
            sl = t % 2
            T.dma("sp", Sx[sl][:], h_scr[t * 128:(t + 1) * 128, :], writes=[("Sx", sl)])
            col = (b % 2) * 8 + i
            T.op("act", lambda e: e.activation(out=junk, in_=Sx[sl][:], func=AF.Square,
                                               scale=float(D ** -0.5), accum_out=st[:, col:col + 1]),
                 reads=[("Sx", sl)], writes=[("sq", 0), ("sq", 1), ("st", col)])
            T.op("act", lambda e: e.activation(out=st[:, col + 4:col + 5], in_=st[:, col:col + 1], func=AF.Sqrt,
                                               bias=eps1[:, 0:1], scale=1.0),
                 reads=[("st", col)], writes=[("st", col + 4)])
            T.op("dve", lambda e: e.reciprocal(out=st[:, col + 4:col + 5], in_=st[:, col + 4:col + 5]),
                 reads=[("st", col + 4)], writes=[("st", col + 4)])
            T.op("dve", lambda e: e.tensor_scalar(out=xn[i][:], in0=Sx[sl][:], scalar1=st[:, col + 4:col + 5],
                                                  scalar2=None, op0=ALU.mult),
                 reads=[("Sx", sl), ("st", col + 4)], writes=[("xn", i)])

    def transpose_stage(b):
        xnT = xnT2[b % 2]
        for i in range(TP2):
            pT = pT2[i % 2]
            for k in range(KC):
                T.op("pe", lambda e: e.transpose(pT[:, k, :], xn[i][:, k * 128:(k + 1) * 128], ident[:]),
                     reads=[("xn", i)], writes=[("pT", i % 2)], sig=(k == KC - 1))
            T.op("dve", lambda e: e.tensor_tensor(
                out=xnT[:, :, i * 128:(i + 1) * 128], in0=pT[:, :, :],
                in1=gcols[:, 8:16].unsqueeze(2).to_broadcast([128, KC, 128]), op=ALU.mult),
                writes=[("pT", i % 2), ("xnT", b % 2, i)])

    nev = [0]

    def evac(out, in_, reads, writes):
        nev[0] += 1
        if nev[0] % 2:
            T.op("act", lambda e: e.copy(out=out, in_=in_), reads=reads, writes=writes)
        else:
            T.op("dve", lambda e: e.tensor_copy(out=out, in_=in_), reads=reads, writes=writes)

    def lat_mm(b, nch, c0, banks):
        xnT = xnT2[b % 2]
        XR = [("xnT", b % 2, i) for i in range(TP2)]
        for ch in range(nch):
            pb, pkey = banks[ch]
            for k in range(KC):
                T.op("pe", lambda e: e.matmul(pb[:, 0:B2], lhsT=Wlat[:, k, c0 + ch * 128:c0 + (ch + 1) * 128],
                                              rhs=xnT[:, k, :], start=(k == 0), stop=(k == KC - 1)),
                     reads=["Wlat"] + XR, writes=[pkey], sig=(k == KC - 1))
            T.op("act", lambda e: e.activation(out=sq[:, ch, 0:B2], in_=pb[:, 0:B2], func=AF.Square),
                 writes=[pkey, ("sq", ch)])

    def lat_sum(b, nch, nfeat):
        for ch in range(nch):
            T.op("pe", lambda e: e.matmul(pSm[:, 0:B2], lhsT=ones_bf[:, :], rhs=sq[:, ch, 0:B2],
                                          start=(ch == 0), stop=(ch == nch - 1)),
                 reads=[("sq", ch)], writes=["pSm"], sig=(ch == nch - 1))
        T.op("act", lambda e: e.activation(out=rs[:], in_=pSm[:, 0:B2], func=AF.Sqrt, bias=eps1[:, 0:1],
                                           scale=1.0 / nfeat), writes=["pSm", "rs"])
        T.op("dve", lambda e: e.reciprocal(out=rs[:], in_=rs[:]), writes=["rs"])

    def lat_scale(b, nch, c0, gcol0, dstf, banks):
        blk = slice(b * B2, (b + 1) * B2)
        for ch in range(nch):
            pb, pkey = banks[ch]
            T.op("dve", lambda e: e.scalar_tensor_tensor(out=dstf(ch)[:, blk], in0=pb[:, 0:B2],
                                                         scalar=gcols[:, gcol0 + ch:gcol0 + ch + 1], in1=rs[:],
                                                         op0=ALU.mult, op1=ALU.mult),
                 reads=["rs"], writes=[pkey, ("lat", c0, ch)])

    def na_qk(b, fcs):
        blk = slice(b * B2, (b + 1) * B2)
        xnT = xnT2[b % 2]
        XR = [("xnT", b % 2, i) for i in range(TP2)]
        for fc in fcs:
            pa = pA[fc % 2]
            for k in range(KC):
                T.op("pe", lambda e: e.matmul(pa[:, 0:B2], lhsT=Wna[:, k, fc * 128:(fc + 1) * 128], rhs=xnT[:, k, :],
                                              start=(k == 0), stop=(k == KC - 1)),
                     reads=[("Wna", fc // 4)] + XR, writes=[("pA", fc % 2)], sig=(k == KC - 1))
            dstT = (qT_na if fc < 4 else kT_na)[fc % 4]
            evac(dstT[:, blk], pa[:, 0:B2], [], [("pA", fc % 2), ("qk", fc)])

    def na_v(b):
        xnT = xnT2[b % 2]
        for i in range(TP2):
            pa = pA[i % 2]
            for k in range(KC):
                T.op("pe", lambda e: e.matmul(pa[:], lhsT=xnT[:, k, i * 128:(i + 1) * 128], rhs=Wna[:, k, 1024:1536],
                                              start=(k == 0), stop=(k == KC - 1)),
                     reads=[("Wna", 2), ("xnT", b % 2, i)], writes=[("pA", i % 2)], sig=(k == KC - 1))
            evac(V_na[:, b * TP2 + i, :, 0:64], pa[:, :].rearrange("p (h d) -> p h d", h=8), [],
                 [("pA", i % 2), ("vna", i)])

    def k_rope(b):
        blk = slice(b * B2, (b + 1) * B2)
        xnT = xnT2[b % 2]
        XR = [("xnT", b % 2, i) for i in range(TP2)]
        for v in range(2):
            for k in range(KC):
                T.op("pe", lambda e: e.matmul(pK[v][0:96, 0:B2], lhsT=Wlat[:, k, 384 + v * 96:480 + v * 96],
                                              rhs=xnT[:, k, :], start=(k == 0), stop=(k == KC - 1)),
                     reads=["Wlat"] + XR, writes=[pKkey[v]], sig=(k == KC - 1))
            T.op("dve", lambda e: e.tensor_tensor(out=tmp[64:96, v, :], in0=pK[v][64:96, 0:B2],
                                                  in1=ropeb[b % 3][64:96, v, :], op=ALU.mult),
                 reads=[("ropeb", b % 3, v)], writes=[pKkey[v], ("tmp", v)])
        T.op("dve", lambda e: e.tensor_tensor(out=krotT[64:96, blk], in0=tmp[64:96, 0, :], in1=tmp[64:96, 1, :],
                                              op=ALU.add),
             reads=[("tmp", 0), ("tmp", 1)], writes=[("krot", b)])

    def proj_block(b):
        cq = lambda ch: cqnT[:, ch, :]
        ckv = lambda ch: ckvnT
        lat_mm(b, 2, 0, CQB)
        na_qk(b, range(0, 4))
        lat_sum(b, 2, 256.0)
        if b + 2 < NB2:
            norm_stage(b + 2)
        lat_mm(b, 1, 256, CKVB)
        na_qk(b, range(4, 6))
        lat_scale(b, 2, 0, 24, cq, CQB)
        na_qk(b, range(6, 8))
        na_v(b)
        lat_sum(b, 1, 128.0)
        k_rope(b)
        lat_scale(b, 1, 256, 26, ckv, CKVB)

    norm_stage(0)
    transpose_stage(0)
    norm_stage(1)
    transpose_stage(1)
    for b in range(NB2):
        proj_block(b)
        if b + 2 < NB2:
            transpose_stage(b + 2)


def p3_na(nc, T, C, es, qT_na, kT_na, V_na):
    identF, oscr, natab = C["identF"], C["oscr"], C["natab"]
    Eb = [_sb(nc, es, "Eb%d" % i, [128, TAB_W], BF16) for i in range(2)]
    stg = [_sb(nc, es, "ebstg%d" % i, [128, TQ], F32) for i in range(2)]
    PT2 = [_sb(nc, es, "p3PT%d" % i, [128, 2 * BLK], BF16) for i in range(3)]
    oT = [_sb(nc, es, "p3oT%d" % i, [65, BLK], F32) for i in range(2)]
    otok = [_sb(nc, es, "p3otok%d" % i, [128, 4, 64], F32) for i in range(2)]
    rc = _sb(nc, es, "p3rc", [128, 8], F32)
    qz = [_sb(nc, es, "p3qz%d" % i, [128, BLK], BF16) for i in range(2)]
    pS2 = [_ps(nc, es, "p3pS%d" % i, [128, 2 * BLK], F32) for i in range(2)]
    pO = [_ps(nc, es, "p3pO%d" % i, [128, BLK], F32) for i in range(2)]

    def q_prep(gi):
        h, g = divmod(gi, 8)
        fc, hp = h // 2, h % 2
        rows = slice(hp * 64, hp * 64 + 64)
        other = slice((1 - hp) * 64, (1 - hp) * 64 + 64)
        par = gi % 2
        if g < 2:
            T.op("pool", lambda e: e.memset(qz[par][other, :], 0.0), writes=[("qz", par)])
        T.op("pool", lambda e: e.tensor_copy(out=qz[par][rows, :], in_=qT_na[fc][rows, g * BLK:(g + 1) * BLK]),
             writes=[("qz", par)])
    pF = _ps(nc, es, "p3pF", [128, 4, 66], F32)

    def eb_dma(h, q):
        T.dma("sp", stg[q % 2][:], natab[h, :, q * TQ:(q + 1) * TQ], writes=[("stg", q % 2)])

    def eb_exp(h, q):
        T.op("act", lambda e: e.activation(out=Eb[h % 2][:, q * TQ:(q + 1) * TQ], in_=stg[q % 2][:], func=AF.Copy,
                                           scale=8.0),
             reads=[("stg", q % 2)], writes=[("Eb", h % 2, q)])

    def load_eb(h):
        for q in range(4):
            eb_dma(h, q)
            eb_exp(h, q)

    items = []
    gi = 0
    for h in range(8):
        fc, hp = h // 2, h % 2
        rows = slice(hp * 64, hp * 64 + 64)
        eb = Eb[h % 2]
        for g in range(8):
            if g == 0:
                chunks = [(2 * j, FIRST0 + j * 512) for j in range(6)]
            elif g == 7:
                chunks = [(52 + 2 * j, LAST0 + j * 512) for j in range(6)]
            else:
                chunks = [(8 * g - 4 + 2 * j, (14 - 2 * j) * 64) for j in range(8)]
            qblk = slice(g * BLK, (g + 1) * BLK)
            chs = []
            for (kr0, o) in chunks:
                ct = kr0 // 2

                def qk(e, ps, stop, c0, c1, ct=ct, fc=fc, par=gi % 2):
                    return e.matmul(ps, lhsT=kT_na[fc][:, ct * 128:(ct + 1) * 128], rhs=qz[par][:, c0:c1],
                                    start=True, stop=stop)
                chd = dict(qk=qk, v=V_na[:, ct, h, 0:65], mask=eb[:, o:o + 512])
                if NA_SUBRANGE and 1 <= g <= 6:
                    j = len(chs)
                    i0, i1 = max(0, 2 * j - 7), min(7, 2 * j + 1)
                    chd["cr"] = (i0 * 64, (i1 + 1) * 64)
                chs.append(chd)
            n_it = len(chs) // 2
            for a in range(n_it):
                items.append(dict(chunks=chs[2 * a:2 * a + 2], first=(a == 0), last=(a == n_it - 1),
                                  pO=pO[gi % 2], pOkey=("pO", gi % 2), qk_reads=[("qz", gi % 2)], v_reads=[], a=a,
                                  mask_reads=[("Eb", h % 2, q) for q in range(4)], h=h, g=g, gi=gi,
                                  newhead=(g == 0 and a == 0)))
            gi += 1

    load_eb(0)
    q_prep(0)

    def after_back(k, it, deferred):
        if it["gi"] == 0 and it["a"] == 0:
            deferred.append(lambda: (eb_dma(1, 0), eb_dma(1, 1)))
        if it["gi"] == 1 and it["a"] == 0:
            deferred.append(lambda: (eb_exp(1, 0), eb_exp(1, 1), eb_dma(1, 2), eb_dma(1, 3)))
        if it["gi"] == 3 and it["a"] == 0:
            deferred.append(lambda: (eb_exp(1, 2), eb_exp(1, 3)))
        if it["a"] == 0 and it["gi"] + 1 < 64:
            q_prep(it["gi"] + 1)
        if it["last"]:
            h, g, gi_ = it["h"], it["g"], it["gi"]
            dst = oscr[g * BLK:(g + 1) * BLK, h * 64:(h + 1) * 64].rearrange("(i p) d -> p i d", p=128)
            attn_finalize(T, it["pO"], it["pOkey"], pF, "pF", oT, otok, rc, identF, gi_ % 2, dst, deferred)
            if g == 5 and h + 2 < 8:
                deferred.append(lambda: (eb_dma(h + 2, 0), eb_dma(h + 2, 1)))
            if g == 0 and 1 <= h and h + 1 < 8:
                deferred.append(lambda: (eb_exp(h + 1, 0), eb_exp(h + 1, 1), eb_dma(h + 1, 2), eb_dma(h + 1, 3)))
            if g == 3 and 1 <= h and h + 1 < 8:
                deferred.append(lambda: (eb_exp(h + 1, 2), eb_exp(h + 1, 3)))

    attn_stream(T, items, pS2, PT2, None, 0.125, after_back, bias_ident=C["ident"])


def p4_mla(nc, T, C, es, cqnT, ckvnT, krotT, rope):
    identF, oscr = C["identF"], C["oscr"]
    Wuq = _sb(nc, es, "Wuq", [128, 2, 768], BF16)
    Wuqs = _sb(nc, es, "Wuqs", [128, 2, 768], BF16)
    Wukv = _sb(nc, es, "Wukv", [128, 1024], BF16)
    V_all = _sb(nc, es, "Vall", [128, 32, 8, 66], BF16)
    kT = [_sb(nc, es, "p4kT%d" % i, [96, S], BF16) for i in range(2)]
    qTb = [_sb(nc, es, "p4qT%d" % i, [96, BLK], BF16) for i in range(2)]
    tmp = _sb(nc, es, "p4tmp", [96, 2, BLK], F32)
    ropeb = [_sb(nc, es, "p4rope%d" % i, [96, 2, BLK], F32) for i in range(2)]
    PT2 = [_sb(nc, es, "p4PT%d" % i, [128, 2 * BLK], BF16) for i in range(3)]
    oT = [_sb(nc, es, "p4oT%d" % i, [65, BLK], F32) for i in range(2)]
    otok = [_sb(nc, es, "p4otok%d" % i, [128, 4, 64], F32) for i in range(2)]
    rc = _sb(nc, es, "p4rc", [128, 8], F32)
    pS2 = [_ps(nc, es, "p4pS%d" % i, [128, 2 * BLK], F32) for i in range(2)]
    pO = [_ps(nc, es, "p4pO%d" % i, [128, BLK], F32) for i in range(2)]
    pX = [_ps(nc, es, "p4pX%d" % i, [128, BLK], F32) for i in range(2)]
    pF = pX[1][:, 0:264].rearrange("p (i d) -> p i d", i=4)

    T.dma("pool", Wuq[:, :, :], C["w_uq"].rearrange("(k p) f -> p k f", p=128), writes=["Wuq"], key=("Wgu", 0, 0))
    T.dma("pool", Wuqs[:, :, :], C["w_uqs"].rearrange("(k p) f -> p k f", p=128), writes=["Wuqs"], key=("Wgu", 0, 1))
    T.dma("pool", Wukv[:, :], C["w_ukv"][:, :], writes=["Wukv"], key=("Wgu", 0, 2))
    T.op("pool", lambda e: e.memset(V_all[:, :, :, 64:66], 1.0), writes=["Vall1"])

    wv = Wukv[:, :].rearrange("p (h t d) -> p h t d", h=8, t=2)
    vbank = [(pX[0], ("pX", 0)), (pX[1], ("pX", 1)), (pO[0], ("pO", 0)), (pO[1], ("pO", 1))]
    for t in range(32):
        px, pkey = vbank[t % 4]
        T.op("pe", lambda e: e.matmul(px[:], lhsT=ckvnT[:, t * 128:(t + 1) * 128], rhs=wv[:, :, 1, :],
                                      start=True, stop=True), reads=["Wukv"], writes=[pkey])
        if t % 2:
            T.op("act", lambda e: e.copy(out=V_all[:, t, :, 0:64], in_=px[:, :].rearrange("p (h d) -> p h d", h=8)),
                 writes=[pkey, ("v_all", 1)])
        else:
            T.op("dve", lambda e: e.tensor_copy(out=V_all[:, t, :, 0:64], in_=px[:, :].rearrange("p (h d) -> p h d", h=8)),
                 writes=[pkey, ("v_all", 0)])

    def k_gen(h, prologue=False):
        kt = kT[h % 2]
        for b in range(NBLK):
            px, pkey = vbank[b % 4] if prologue else (pX[b % 2], ("pX", b % 2))
            blk = slice(b * BLK, (b + 1) * BLK)
            T.op("pe", lambda e: e.matmul(px[0:64, :], lhsT=Wukv[:, h * 128:h * 128 + 64], rhs=ckvnT[:, blk],
                                          start=True, stop=True), reads=["Wukv"], writes=[pkey])
            if prologue and b % 2:
                T.op("act", lambda e: e.copy(out=kt[0:64, blk], in_=px[0:64, :]), writes=[pkey, ("kTn", h % 2)])
            else:
                T.op("dve", lambda e: e.tensor_copy(out=kt[0:64, blk], in_=px[0:64, :]),
                     writes=[pkey, ("kTn", h % 2)])
        T.op("dve", lambda e: e.tensor_copy(out=kt[64:96, :], in_=krotT[64:96, :]), writes=[("kTr", h % 2)])

    def q_gen(h, g, par):
        blk = slice(g * BLK, (g + 1) * BLK)
        T.dma("sp", ropeb[par][64:96, 0, :], C["cosT"][:, blk], writes=[("ropeb", par, 0)])
        T.dma("sp", ropeb[par][64:96, 1, :], C["sinT"][:, blk], writes=[("ropeb", par, 1)])
        for v, W in ((0, Wuq), (1, Wuqs)):
            for kc in range(2):
                T.op("pe", lambda e: e.matmul(pX[v][0:96, :], lhsT=W[:, kc, h * 96:(h + 1) * 96], rhs=cqnT[:, kc, blk],
                                              start=(kc == 0), stop=(kc == 1)),
                     reads=["Wuq", "Wuqs"], writes=[("pX", v)], sig=(kc == 1))
        T.op("dve", lambda e: e.tensor_copy(out=qTb[par][0:64, :], in_=pX[0][0:64, :]),
             writes=[("pX", 0), ("qTbn", par)])
        for v in range(2):
            T.op("dve", lambda e: e.tensor_tensor(out=tmp[64:96, v, :], in0=pX[v][64:96, :], in1=ropeb[par][64:96, v, :],
                                                  op=ALU.mult),
                 reads=[("ropeb", par, v)], writes=[("pX", v), ("tmp", v)])
        T.op("dve", lambda e: e.tensor_tensor(out=qTb[par][64:96, :], in0=tmp[64:96, 0, :], in1=tmp[64:96, 1, :],
                                              op=ALU.add),
             reads=[("tmp", 0), ("tmp", 1)], writes=[("qTb", par)])

    sc = float(96 ** -0.5)
    groups = [(h, g) for h in range(8) for g in range(8)]
    items = []
    for gi, (h, g) in enumerate(groups):
        par = gi % 2
        kt = kT[h % 2]
        for a in range(16):
            chs = []
            for c in range(2):
                j = 2 * a + c

                def qk(e, ps, stop, c0, c1, j=j, kt=kt, par=par):
                    return e.matmul(ps, lhsT=kt[0:96, j * 128:(j + 1) * 128], rhs=qTb[par][0:96, c0:c1],
                                    start=True, stop=stop)
                chs.append(dict(qk=qk, v=V_all[:, j, h, 0:65], mask=None))
            items.append(dict(chunks=chs, first=(a == 0), last=(a == 15), pO=pO[gi % 2], pOkey=("pO", gi % 2),
                              qk_reads=[("kTn", h % 2), ("kTr", h % 2), ("qTb", par), ("qTbn", par)],
                              v_reads=[("v_all", 0), ("v_all", 1), "Vall1"], mask_reads=[], h=h, g=g, gi=gi, a=a))

    k_gen(0, prologue=True)
    q_gen(0, 0, 0)

    def after_back(k, it, deferred):
        gi, a = it["gi"], it["a"]
        if a == 3 and it["g"] == 0 and it["h"] + 1 < 8:
            k_gen(it["h"] + 1)
        if a == 10 and gi + 1 < len(groups):
            nh, ng = groups[gi + 1]
            q_gen(nh, ng, (gi + 1) % 2)
        if it["last"]:
            h, g = it["h"], it["g"]
            dst = oscr[g * BLK:(g + 1) * BLK, 512 + h * 64:512 + (h + 1) * 64].rearrange("(i p) d -> p i d", p=128)
            attn_finalize(T, it["pO"], it["pOkey"], pF, ("pX", 1), oT, otok, rc, identF, gi % 2, dst, deferred)

    attn_stream(T, items, pS2, PT2, None, sc, after_back)


def p5a_wout(nc, T, C, prefetch=None):
    gcols, ident, eps1, h_scr, oscr = C["gcols"], C["ident"], C["eps1"], C["h_scr"], C["oscr"]
    with ExitStack() as es:
        Wout = _sb(nc, es, "Wout", [128, KC, D], BF16)
        Sx = [_sb(nc, es, "p5Sx%d" % i, [128, D], F32) for i in range(4)]
        Rx = [_sb(nc, es, "p5Rx%d" % i, [128, D], F32) for i in range(4)]
        Tt = [_sb(nc, es, "p5Tt%d" % i, [128, D], F32) for i in range(4)]
        xn = [_sb(nc, es, "p5xn%d" % i, [128, D], BF16) for i in range(4)]
        xnT2 = [_sb(nc, es, "p5xnT%d" % i, [128, KC, BLK], BF16) for i in range(2)]
        gpost = _sb(nc, es, "p5gpost", [128, D], F32)
        junk = _sb(nc, es, "p5junk", [128, D], BF16)
        st = _sb(nc, es, "p5st", [128, 64], F32)
        pT2 = [_ps(nc, es, "p5pT%d" % i, [128, KC, 128], BF16) for i in range(2)]
        pD = [_ps(nc, es, "p5pD%d" % i, [128, BLK], F32) for i in range(4)]
        wv = C["w_out"].rearrange("(k p) d -> p k d", p=128)
        for j in range(2):
            T.dma("pool", Wout[:, :, j * 512:(j + 1) * 512], wv[:, :, j * 512:(j + 1) * 512],
                  writes=[("Wout", j)], key=("Wgu", 0, j))
        T.dma("sp", gpost[:], C["grows"][1:2, :].partition_broadcast(128), writes=["gpost"])
        if prefetch is not None:
            prefetch()

        def load_sx(t):
            if t < S // 128:
                T.dma("sp", Sx[t % 4][:], oscr[t * 128:(t + 1) * 128, :], writes=[("Sx", t % 4)])

        def norm_a(b, i):
            t = b * 4 + i
            sl = t % 4
            load_sx(t + 2)
            col = (b % 2) * 16 + i * 4
            for hf in range(2):
                hs = slice(hf * 512, (hf + 1) * 512)
                T.op("act", lambda e: e.activation(out=junk[:, hs], in_=Sx[sl][:, hs], func=AF.Square,
                                                   scale=float(512 ** -0.5), accum_out=st[:, col + hf:col + hf + 1]),
                     reads=[("Sx", sl)], writes=[("junk", hf), ("st", col + hf)])
            T.op("act", lambda e: e.activation(out=st[:, col + 2:col + 4], in_=st[:, col:col + 2], func=AF.Sqrt,
                                               bias=eps1[:, 0:1], scale=1.0),
                 reads=[("st", col), ("st", col + 1)], writes=[("st", col + 2), ("st", col + 3)])
            T.op("dve", lambda e: e.reciprocal(out=st[:, col + 2:col + 4], in_=st[:, col + 2:col + 4]),
                 writes=[("st", col + 2), ("st", col + 3)])

        def norm_b(b, i):
            t = b * 4 + i
            sl = t % 4
            col = (b % 2) * 16 + i * 4
            for hf in range(2):
                hs = slice(hf * 512, (hf + 1) * 512)
                T.op("act", lambda e: e.activation(out=xn[i][:, hs], in_=Sx[sl][:, hs], func=AF.Copy,
                                                   scale=st[:, col + 2 + hf:col + 3 + hf]),
                     reads=[("Sx", sl), ("st", col + 2 + hf)], writes=[("xn", i, hf)])

        def norm_stage(b):
            for i in range(4):
                norm_a(b, i)
                norm_b(b, i)

        def transpose_stage(b):
            for i in range(4):
                pT = pT2[i % 2]
                for k in range(KC):
                    T.op("pe", lambda e: e.transpose(pT[:, k, :], xn[i][:, k * 128:(k + 1) * 128], ident[:]),
                         reads=[("xn", i, k // 4)], writes=[("pT", i % 2)], sig=(k == KC - 1))
                T.op("dve", lambda e: e.tensor_tensor(
                    out=xnT2[b % 2][:, :, i * 128:(i + 1) * 128], in0=pT[:, :, :],
                    in1=gcols[:, 27:35].unsqueeze(2).to_broadcast([128, KC, 128]), op=ALU.mult),
                    writes=[("pT", i % 2), ("xnT", b % 2, i)])

        def out_tile(b, i):
            t = b * 4 + i
            sl = t % 4
            T.dma("sp", Rx[sl][:], h_scr[t * 128:(t + 1) * 128, :], writes=[("Rx", sl)])
            col = 32 + (t % 4) * 4
            for n in range(2):
                pdi = (i % 2) * 2 + n
                pd = pD[pdi]
                for k in range(KC):
                    T.op("pe", lambda e: e.matmul(pd[:], lhsT=xnT2[b % 2][:, k, i * 128:(i + 1) * 128],
                                                  rhs=Wout[:, k, n * 512:(n + 1) * 512],
                                                  start=(k == 0), stop=(k == KC - 1)),
                         reads=[("xnT", b % 2, i), ("Wout", n)], writes=[("pD", pdi)], sig=(k == KC - 1))
                T.op("act", lambda e: e.activation(out=junk[:, 0:512], in_=pd[:], func=AF.Square,
                                                   scale=float(D ** -0.5), accum_out=st[:, col + n:col + n + 1]),
                     writes=[("pD", pdi), ("junk", 0), ("st", col + n)])
                T.op("dve", lambda e: e.tensor_tensor(out=Tt[sl][:, n * 512:(n + 1) * 512], in0=pd[:],
                                                      in1=gpost[:, n * 512:(n + 1) * 512], op=ALU.mult),
                     reads=["gpost"], writes=[("pD", pdi), ("Tt", sl, n)])
            T.op("dve", lambda e: e.tensor_tensor(out=st[:, col + 2:col + 3], in0=st[:, col:col + 1],
                                                  in1=st[:, col + 1:col + 2], op=ALU.add),
                 reads=[("st", col), ("st", col + 1)], writes=[("st", col + 2)])
            T.op("act", lambda e: e.activation(out=st[:, col + 3:col + 4], in_=st[:, col + 2:col + 3], func=AF.Sqrt,
                                               bias=eps1[:, 0:1], scale=1.0),
                 reads=[("st", col + 2)], writes=[("st", col + 3)])
            T.op("dve", lambda e: e.reciprocal(out=st[:, col + 3:col + 4], in_=st[:, col + 3:col + 4]),
                 reads=[("st", col + 3)], writes=[("st", col + 3)])
            T.op("dve", lambda e: e.scalar_tensor_tensor(out=Rx[sl][:], in0=Tt[sl][:], scalar=st[:, col + 3:col + 4],
                                                         in1=Rx[sl][:], op0=ALU.mult, op1=ALU.add),
                 reads=[("Tt", sl, 0), ("Tt", sl, 1), ("st", col + 3), ("Rx", sl)], writes=[("Rx", sl)])
            T.dma("sp", h_scr[t * 128:(t + 1) * 128, :], Rx[sl][:], reads=[("Rx", sl)], key=("Rxo", sl))

        load_sx(0)
        load_sx(1)
        norm_stage(0)
        transpose_stage(0)
        norm_stage(1)
        transpose_stage(1)
        for b in range(NBLK):
            for i in range(4):
                if b + 2 < NBLK:
                    norm_a(b + 2, i)
                out_tile(b, i)
                if b + 2 < NBLK:
                    norm_b(b + 2, i)
            if b + 2 < NBLK:
                transpose_stage(b + 2)
        T.barrier()


PH_ALL = ("f1", "mix", "wout", "f2")


def build_program(phases=PH_ALL, dbg_out=None, dbg_h_from_x=False):
    nc = bass.Bass("TRN2", target_bir_lowering=False)
    dram_in = lambda n, shp, dt=F32: nc.dram_tensor(n, list(shp), dt, kind="ExternalInput").ap()
    C = {}
    x = dram_in("x", [S, D])
    for nm, shp in (("w_gu1", [D, 2 * DFF]), ("w_dn1", [DFF, D]), ("w_gu2", [D, 2 * DFF]), ("w_dn2", [DFF, D]),
                    ("w_na", [D, 1536]), ("w_lat", [D, 576]), ("w_uq", [256, 768]), ("w_uqs", [256, 768]),
                    ("w_ukv", [128, 1024]), ("w_out", [D, D]), ("gcols_d", [128, 40]), ("grows", [4, D]),
                    ("ident_d", [128, 128]), ("cosT", [32, S]), ("sinT", [32, S]), ("natab", [8, 128, TAB_W])):
        C[nm] = dram_in(nm, shp)
    out = nc.dram_tensor("out", [S, D], F32, kind="ExternalOutput").ap()
    if dbg_out == "oscr":
        C["h_scr"] = nc.dram_tensor("h_scr", [S, D], F32).ap()
        C["oscr"] = out
    elif dbg_out == "h_scr":
        C["h_scr"] = out
        C["oscr"] = nc.dram_tensor("oscr", [S, D], F32).ap()
    else:
        C["h_scr"] = nc.dram_tensor("h_scr", [S, D], F32).ap()
        C["oscr"] = nc.dram_tensor("oscr", [S, D], F32).ap()

    if dbg_h_from_x:
        C["h_scr"] = x
    with ExitStack() as es:
        T = Tracker(nc, es)
        gcols = _sb(nc, es, "sb_gcols", [128, 40], F32)
        ident = _sb(nc, es, "sb_ident", [128, 128], BF16)
        identF = _sb(nc, es, "sb_identF", [128, 128], F32)
        ones_bf = _sb(nc, es, "sb_ones", [128, 128], BF16)
        eps1 = _sb(nc, es, "eps1", [128, 1], F32)
        eps4 = _sb(nc, es, "eps4", [128, 1], F32)
        T.dma("sp", gcols[:], C["gcols_d"][:, :], writes=["gcols"])
        T.dma("pool", ident[:], C["ident_d"][:, :], writes=["ident"])
        T.dma("sp", identF[:], C["ident_d"][:, :], writes=["identF"])
        T.op("dve", lambda e: e.memset(eps1[:], EPS), writes=["eps1"])
        T.op("dve", lambda e: e.memset(eps4[:], 4 * EPS), writes=["eps4"])
        T.op("dve", lambda e: e.memset(ones_bf[:], 1.0), writes=["ones"])
        T.barrier()
        C.update(gcols=gcols, ident=ident, identF=identF, ones_bf=ones_bf, eps1=eps1, eps4=eps4)
        h_scr = C["h_scr"]
        if "f1" in phases:
            ffn_phase(nc, T, "f1", x, h_scr, C["w_gu1"], C["w_dn1"], gcols, 0, C["grows"][0:1, :], ident, eps1, eps4)
        if "mix" in phases:
            mixer_phase(nc, T, C)
        if "wout" in phases:
            p5a_wout(nc, T, C)
        if "f2" in phases:
            ffn_phase(nc, T, "f2", h_scr, out, C["w_gu2"], C["w_dn2"], gcols, 16, C["grows"][2:3, :], ident, eps1, eps4)
        T.barrier()
    return nc


def _na_bias_tables(rpb):
    H = rpb.shape[0]
    kl = np.arange(128) // 64
    kc = np.arange(128) % 64
    qc = np.arange(64)
    ws = np.clip(qc - 8, 0, 48)
    colv = (kc[:, None] >= ws[None, :]) & (kc[:, None] < ws[None, :] + 16)
    coff = np.clip(kc[:, None] - qc[None, :] + 15, 0, 30)
    tab = np.full((H, 128, TAB_W), NEG, np.float32)
    for e in range(22):
        dr = (10 - e) + kl
        rowv = (dr >= -4) & (dr <= 3)
        roff = np.clip(dr + 7, 0, 14)
        vals = rpb[:, roff[:, None], coff]
        m = rowv[:, None] & colv
        tab[:, :, e * 64:(e + 1) * 64] = np.where(m[None], vals, np.float32(NEG))
    for base, r0, k0 in ((FIRST0, 0, 0), (LAST0, 56, 52)):
        for j in range(6):
            for i in range(8):
                r = r0 + i
                rs = min(max(r - 4, 0), 56)
                kr = k0 + 2 * j + kl
                rowv = (kr >= rs) & (kr < rs + 8)
                roff = np.clip(kr - r + 7, 0, 14)
                vals = rpb[:, roff[:, None], coff]
                m = rowv[:, None] & colv
                o = base + j * 512 + i * 64
                tab[:, :, o:o + 64] = np.where(m[None], vals, np.float32(NEG))
    return tab


def _rope_tables():
    t = np.arange(S)
    row = (t // 64).astype(np.float64)
    col = (t % 64).astype(np.float64)
    inv = 1.0 / (10000.0 ** (np.arange(8, dtype=np.float64) / 8.0))
    ang = np.concatenate([row[:, None] * inv[None, :], col[:, None] * inv[None, :]], axis=-1)
    cos, sin = np.cos(ang), np.sin(ang)
    cosT = np.concatenate([cos.T, cos.T], axis=0).astype(np.float32)
    sinT = np.concatenate([-sin.T, sin.T], axis=0).astype(np.float32)
    return np.ascontiguousarray(cosT), np.ascontiguousarray(sinT)


def host_prep(inp):
    p = {k: np.asarray(v) for k, v in inp.items()}
    L = 0
    f = np.float32
    gcols = np.zeros((128, 40), f)
    gcols[:, 0:8] = p["ffn1_pre_g"][L].reshape(8, 128).T
    gcols[:, 8:16] = p["mix_pre_g"][L].reshape(8, 128).T
    gcols[:, 16:24] = p["ffn2_pre_g"][L].reshape(8, 128).T
    gcols[:, 24:26] = p["mla_q_norm_g"][L].reshape(2, 128).T
    gcols[:, 26:27] = p["mla_kv_norm_g"][L].reshape(1, 128).T
    gcols[:, 27:35] = np.concatenate([p["na_out_norm_g"][L], p["mla_out_norm_g"][L]]).reshape(8, 128).T
    grows = np.stack([p["ffn1_post_g"][L], p["mix_post_g"][L], p["ffn2_post_g"][L], p["ffn2_post_g"][L]]).astype(f)
    w_in = p["w_in"][L]
    kr = w_in[:, 1920:1952]
    krs = np.concatenate([kr[:, 16:32], kr[:, 0:16]], axis=1)
    z64 = np.zeros((D, 64), f)
    w_lat = np.concatenate([w_in[:, 1536:1920], z64, kr, z64, krs], axis=1)
    w_uq = p["mla_w_uq"][L]
    w_uqs = w_uq.reshape(256, 8, 96).copy()
    w_uqs[:, :, 64:80] = w_uq.reshape(256, 8, 96)[:, :, 80:96]
    w_uqs[:, :, 80:96] = w_uq.reshape(256, 8, 96)[:, :, 64:80]
    cosT, sinT = _rope_tables()
    c = np.ascontiguousarray
    shared = dict(
        w_gu1=c(p["ffn1_w_gu"][L]), w_dn1=c(p["ffn1_w_down"][L]), w_gu2=c(p["ffn2_w_gu"][L]), w_dn2=c(p["ffn2_w_down"][L]),
        w_na=c(w_in[:, 0:1536]), w_lat=c(w_lat.astype(f)), w_uq=c(w_uq), w_uqs=c(w_uqs.reshape(256, 768)),
        w_ukv=c(p["mla_w_ukv"][L]), w_out=c(p["w_out"][L]), gcols_d=gcols, grows=grows,
        ident_d=np.eye(128, dtype=f), cosT=cosT, sinT=sinT, natab=_na_bias_tables(p["na_rpb"][L].astype(f)),
    )
    return p, shared


def kernel(**inputs):
    p, shared = host_prep(inputs)
    nc = build_program()
    n = 8
    in_maps = [dict(shared, x=np.ascontiguousarray(p["x"][c], dtype=np.float32)) for c in range(n)]
    res = run_bass_kernel_spmd(nc, in_maps, core_ids=list(range(n)))
    return np.stack([np.asarray(r["out"]) for r in res.results], axis=0).astype(np.float32)
```

```python
import numpy as np
from contextlib import ExitStack
import concourse.bass as bass
import concourse.mybir as mybir
from concourse.bass_utils import run_bass_kernel_spmd

F32 = mybir.dt.float32
BF16 = mybir.dt.bfloat16
AF = mybir.ActivationFunctionType
ALU = mybir.AluOpType

S = 4096
D = 1024
DFF = 2816
NFC = DFF // 128
KC = D // 128
BLK = 512
NBLK = S // BLK
EPS = 1e-6
NEG = -1e30


class Tracker:
    def __init__(self, nc, es):
        self.nc = nc
        self.es = es
        self.eng = dict(pe=nc.tensor, act=nc.scalar, dve=nc.vector, pool=nc.gpsimd, sp=nc.sync)
        self.sem = {k: es.enter_context(nc.semaphore("s_" + k)) for k in ("pe", "act", "dve", "pool")}
        self.cnt = {k: 0 for k in self.sem}
        self.pending = {k: False for k in self.sem}
        self.seen = {k: {} for k in self.eng}
        self.res = {}
        self.dsem = {}
        self.dcnt = {}
        self.semobj = {}

    def _sid(self, s):
        self.semobj[id(s)] = s
        return id(s)

    def _deps(self, reads, writes):
        deps = []
        for k in reads:
            r = self.res.get(k)
            if r and r[0]:
                deps.append(r[0])
        for k in writes:
            r = self.res.get(k)
            if r:
                if r[0]:
                    deps.append(r[0])
                deps.extend(r[1])
        return deps

    def _wait(self, e, deps):
        best = {}
        for (s, v) in deps:
            sid = self._sid(s)
            if e == "pe" and s is self.sem["pe"]:
                continue
            if v > best.get(sid, 0):
                best[sid] = v
        for sid, v in best.items():
            if self.seen[e].get(sid, 0) >= v:
                continue
            self.eng[e].wait_ge(self.semobj[sid], v)
            self.seen[e][sid] = v

    def _record(self, tok, reads, writes):
        for k in reads:
            r = self.res.setdefault(k, [None, []])
            r[1].append(tok)
            if len(r[1]) > 32:
                r[1] = self._compact(r[1])
        for k in writes:
            self.res[k] = [tok, []]

    @staticmethod
    def _compact(lst):
        best = {}
        for (s, v) in lst:
            if v > best.get(id(s), (None, 0))[1]:
                best[id(s)] = (s, v)
        return list(best.values())

    def op(self, e, fn, reads=(), writes=(), sig=True):
        self._wait(e, self._deps(reads, writes))
        ins = fn(self.eng[e])
        if sig:
            self.cnt[e] += 1
            ins.then_inc(self.sem[e], 1)
            self.pending[e] = False
            tok = (self.sem[e], self.cnt[e])
        else:
            self.pending[e] = True
            tok = (self.sem[e], self.cnt[e] + 1)
        self._record(tok, reads, writes)
        return ins

    def dma(self, q, out, in_, reads=(), writes=(), key=None):
        if key is None:
            key = (tuple(writes) + tuple(reads))[0]
        if key not in self.dsem:
            self.dsem[key] = self.es.enter_context(self.nc.semaphore("d%d" % len(self.dsem)))
            self.dcnt[key] = 0
        self._wait(q, self._deps(reads, writes))
        ins = self.eng[q].dma_start(out=out, in_=in_)
        self.dcnt[key] += 16
        ins.then_inc(self.dsem[key], 16)
        self._record((self.dsem[key], self.dcnt[key]), reads, writes)
        return ins

    def barrier(self, engines=("pe", "act", "dve", "pool", "sp")):
        for e in self.sem:
            assert not self.pending[e], e
        toks = [(self.sem[e], self.cnt[e]) for e in self.sem if self.cnt[e] > 0]
        toks += [(self.dsem[k], self.dcnt[k]) for k in self.dsem]
        for e in engines:
            best = {}
            for (s, v) in toks:
                best[self._sid(s)] = v
            for sid, v in best.items():
                if self.seen[e].get(sid, 0) >= v:
                    continue
                self.eng[e].wait_ge(self.semobj[sid], v)
                self.seen[e][sid] = v
        self.res = {}


def _sb(nc, es, name, shape, dt):
    return es.enter_context(nc.sbuf_tensor(name, list(shape), dt))


def _ps(nc, es, name, shape, dt):
    return es.enter_context(nc.psum_tensor(name, list(shape), dt))


def ffn_alloc_weights(nc, es, ph):
    Wgu = _sb(nc, es, ph + "Wgu", [128, KC, 2 * DFF], BF16)
    Wdn = _sb(nc, es, ph + "Wdn", [128, NFC, D], BF16)
    return Wgu, Wdn


def ffn_load_weights(T, Wgu, Wdn, w_gu, w_dn):
    wgu_v = w_gu.rearrange("(k p) f -> p k f", p=128)
    wdn_v = w_dn.rearrange("(c p) d -> p c d", p=128)
    for j in range(6):
        c0, c1 = j * 512, min((j + 1) * 512, DFF)
        for half in range(2):
            o = half * DFF
            T.dma("pool", Wgu[:, :, o + c0:o + c1], wgu_v[:, :, o + c0:o + c1],
                  writes=[("Wgu", half, j)], key=("Wgu", half, j))
    for j in range(2):
        T.dma("pool", Wdn[:, j * 11:(j + 1) * 11, :], wdn_v[:, j * 11:(j + 1) * 11, :],
              writes=[("Wdn", j)], key=("Wdn", j))


def ffn_phase(nc, T, ph, src, dst, w_gu, w_dn, gcols, gc0, grow, ident, eps1, eps4, W=None):
    with ExitStack() as es:
        if W is None:
            Wgu, Wdn = ffn_alloc_weights(nc, es, ph)
            ffn_load_weights(T, Wgu, Wdn, w_gu, w_dn)
        else:
            Wgu, Wdn = W
        Sx = [_sb(nc, es, ph + "Sx%d" % i, [128, D], F32) for i in range(2)]
        Rx = [_sb(nc, es, ph + "Rx%d" % i, [128, D], F32) for i in range(2)]
        Tt = [_sb(nc, es, ph + "Tt%d" % i, [128, D], F32) for i in range(2)]
        xn = [_sb(nc, es, ph + "xn%d" % i, [128, D], BF16) for i in range(4)]
        xnT = _sb(nc, es, ph + "xnT", [128, KC, BLK], BF16)
        hT = _sb(nc, es, ph + "hT", [128, NFC, BLK], BF16)
        sg = [_sb(nc, es, ph + "sg%d" % i, [128, BLK], BF16) for i in range(2)]
        gpost = _sb(nc, es, ph + "gpost", [128, D], F32)
        junk = _sb(nc, es, ph + "junk", [128, D], BF16)
        st = _sb(nc, es, ph + "st", [128, 64], F32)
        pT2 = [_ps(nc, es, ph + "pT%d" % i, [128, KC, 128], BF16) for i in range(2)]
        pG = [_ps(nc, es, ph + "pG%d" % i, [128, BLK], F32) for i in range(4)]
        pD = [_ps(nc, es, ph + "pD%d" % i, [128, BLK], F32) for i in range(2)]

        T.dma("sp", gpost[:], grow.partition_broadcast(128), writes=["gpost"])

        def wgu_key(c):
            return c * 128 // 512

        def norm_stage(b):
            for i in range(4):
                t = b * 4 + i
                sl = t % 2
                T.dma("sp", Sx[sl][:], src[t * 128:(t + 1) * 128, :], writes=[("Sx", sl)])
                col = (b % 2) * 8 + i
                T.op("act", lambda e: e.activation(out=junk[:], in_=Sx[sl][:], func=AF.Square,
                                                   scale=float(D ** -0.5), accum_out=st[:, col:col + 1]),
                     reads=[("Sx", sl)], writes=["junk", ("st", col)])
                T.op("act", lambda e: e.activation(out=st[:, col + 4:col + 5], in_=st[:, col:col + 1], func=AF.Sqrt,
                                                   bias=eps1[:, 0:1], scale=1.0),
                     reads=[("st", col)], writes=[("st", col + 4)])
                T.op("dve", lambda e: e.reciprocal(out=st[:, col + 4:col + 5], in_=st[:, col + 4:col + 5]),
                     reads=[("st", col + 4)], writes=[("st", col + 4)])
                T.op("dve", lambda e: e.tensor_scalar(out=xn[i][:], in0=Sx[sl][:], scalar1=st[:, col + 4:col + 5],
                                                      scalar2=None, op0=ALU.mult),
                     reads=[("Sx", sl), ("st", col + 4)], writes=[("xn", i)])

        def transpose_stage(b):
            for i in range(4):
                pT = pT2[i % 2]
                for k in range(KC):
                    T.op("pe", lambda e: e.transpose(pT[:, k, :], xn[i][:, k * 128:(k + 1) * 128], ident[:]),
                         reads=[("xn", i)], writes=[("pT", i % 2)], sig=(k == KC - 1))
                T.op("dve", lambda e: e.tensor_tensor(
                    out=xnT[:, :, i * 128:(i + 1) * 128], in0=pT[:, :, :],
                    in1=gcols[:, gc0:gc0 + KC].unsqueeze(2).to_broadcast([128, KC, 128]), op=ALU.mult),
                    writes=[("pT", i % 2), ("xnT", i)])

        def gu_chunk(b, c):
            pg, pu = pG[(c % 2) * 2], pG[(c % 2) * 2 + 1]
            for half, pp in ((0, pg), (1, pu)):
                o = half * DFF + c * 128
                for k in range(KC):
                    T.op("pe", lambda e: e.matmul(pp[:], lhsT=Wgu[:, k, o:o + 128], rhs=xnT[:, k, :],
                                                  start=(k == 0), stop=(k == KC - 1)),
                         reads=[("Wgu", half, wgu_key(c))] + [("xnT", i) for i in range(4)],
                         writes=[("pG", (c % 2) * 2 + half)], sig=(k == KC - 1))
            s = sg[c % 2]
            T.op("act", lambda e: e.activation(out=s[:], in_=pg[:], func=AF.Silu),
                 reads=[("pG", (c % 2) * 2)], writes=[("sg", c % 2)])
            T.op("dve", lambda e: e.tensor_tensor(out=hT[:, c, :], in0=s[:], in1=pu[:], op=ALU.mult),
                 reads=[("sg", c % 2), ("pG", (c % 2) * 2 + 1)], writes=[("hT", c)])

        def down_tile(b, i):
            t = b * 4 + i
            sl = t % 2
            T.dma("sp", Rx[sl][:], src[t * 128:(t + 1) * 128, :], writes=[("Rx", sl)])
            col = 16 + (t % 2) * 8
            for n in range(2):
                pd = pD[n]
                for c in range(NFC):
                    T.op("pe", lambda e: e.matmul(pd[:], lhsT=hT[:, c, i * 128:(i + 1) * 128],
                                                  rhs=Wdn[:, c, n * 512:(n + 1) * 512],
                                                  start=(c == 0), stop=(c == NFC - 1)),
                         reads=[("hT", c), ("Wdn", c // 11)], writes=[("pD", n)], sig=(c == NFC - 1))
                T.op("act", lambda e: e.activation(out=junk[:, 0:512], in_=pd[:], func=AF.Square,
                                                   scale=float(D ** -0.5), accum_out=st[:, col + n:col + n + 1]),
                     writes=[("pD", n), "junk", ("st", col + n)])
                T.op("dve", lambda e: e.tensor_tensor(out=Tt[sl][:, n * 512:(n + 1) * 512], in0=pd[:],
                                                      in1=gpost[:, n * 512:(n + 1) * 512], op=ALU.mult),
                     reads=["gpost"], writes=[("pD", n), ("Tt", sl, n)])
            T.op("dve", lambda e: e.tensor_tensor(out=st[:, col + 2:col + 3], in0=st[:, col:col + 1],
                                                  in1=st[:, col + 1:col + 2], op=ALU.add),
                 reads=[("st", col), ("st", col + 1)], writes=[("st", col + 2)])
            T.op("act", lambda e: e.activation(out=st[:, col + 3:col + 4], in_=st[:, col + 2:col + 3], func=AF.Sqrt,
                                               bias=eps4[:, 0:1], scale=4.0),
                 reads=[("st", col + 2)], writes=[("st", col + 3)])
            T.op("dve", lambda e: e.reciprocal(out=st[:, col + 3:col + 4], in_=st[:, col + 3:col + 4]),
                 reads=[("st", col + 3)], writes=[("st", col + 3)])
            T.op("dve", lambda e: e.scalar_tensor_tensor(out=Rx[sl][:], in0=Tt[sl][:], scalar=st[:, col + 3:col + 4],
                                                         in1=Rx[sl][:], op0=ALU.mult, op1=ALU.add),
                 reads=[("Tt", sl, 0), ("Tt", sl, 1), ("st", col + 3), ("Rx", sl)], writes=[("Rx", sl)])
            T.dma("sp", dst[t * 128:(t + 1) * 128, :], Rx[sl][:], reads=[("Rx", sl)], key=("Rxo", sl))

        norm_stage(0)
        transpose_stage(0)
        for b in range(NBLK):
            for c in range(NFC):
                gu_chunk(b, c)
                if c == 8 and b + 1 < NBLK:
                    norm_stage(b + 1)
            if b + 1 < NBLK:
                transpose_stage(b + 1)
            for i in range(4):
                down_tile(b, i)
        T.barrier()


TAB_W = 22 * 64 + 6 * 512 + 6 * 512
FIRST0 = 0
TAB0 = 6 * 512
LAST0 = TAB0 + 22 * 64
TQ = TAB_W // 4
NA_SUBRANGE = False


def attn_stream(T, items, pS2, PT2, PM2, scale, after_back, bias_ident=None):
    def rng(ch):
        return ch.get("cr", (0, 512))

    def front_pe(k):
        it = items[k]
        s = k % 2
        n = len(it["chunks"])
        for c, ch in enumerate(it["chunks"]):
            c0, c1 = rng(ch)
            ps = pS2[s][:, c * 512 + c0:c * 512 + c1]
            if bias_ident is None:
                T.op("pe", lambda e: ch["qk"](e, ps, True, c0, c1), reads=it["qk_reads"],
                     writes=[("pS", s)], sig=(c == n - 1))
            else:
                T.op("pe", lambda e: ch["qk"](e, ps, False, c0, c1), reads=it["qk_reads"],
                     writes=[("pS", s)], sig=False)
                T.op("pe", lambda e: e.matmul(ps, lhsT=bias_ident[:, :], rhs=ch["mask"][:, c0:c1],
                                              start=False, stop=True),
                     reads=it["mask_reads"], writes=[("pS", s)], sig=(c == n - 1))

    def front_act(k):
        it = items[k]
        s = k % 2
        s3 = k % 3
        n = len(it["chunks"])
        if all(rng(ch) == (0, 512) for ch in it["chunks"]):
            T.op("act", lambda e: e.activation(out=PT2[s3][:, 0:n * 512], in_=pS2[s][:, 0:n * 512], func=AF.Exp,
                                               scale=float(scale)), writes=[("pS", s), ("PT", s3)])
        else:
            for c, ch in enumerate(it["chunks"]):
                c0, c1 = rng(ch)
                T.op("act", lambda e: e.activation(out=PT2[s3][:, c * 512 + c0:c * 512 + c1],
                                                   in_=pS2[s][:, c * 512 + c0:c * 512 + c1], func=AF.Exp,
                                                   scale=float(scale)), writes=[("pS", s), ("PT", s3)])

    for k in range(min(2, len(items))):
        front_pe(k)
        front_act(k)
    deferred = []
    for k in range(len(items)):
        if k + 2 < len(items):
            front_pe(k + 2)
            front_act(k + 2)
        for fn in deferred:
            fn()
        deferred = []
        it = items[k]
        s3 = k % 3
        n = len(it["chunks"])
        for c, ch in enumerate(it["chunks"]):
            c0, c1 = rng(ch)
            T.op("pe", lambda e: e.matmul(it["pO"][0:65, c0:c1], lhsT=ch["v"], rhs=PT2[s3][:, c * 512 + c0:c * 512 + c1],
                                          start=(it["first"] and c == 0), stop=(it["last"] and c == n - 1),
                                          skip_group_check=True),
                 reads=[("PT", s3)] + it["v_reads"], writes=[it["pOkey"]], sig=True)
        after_back(k, it, deferred)
    for fn in deferred:
        fn()


def attn_finalize(T, pO, pOkey, pF, pFkey, oT, otok, rc, identF, par, dst_ap, deferred):
    T.op("dve", lambda e: e.tensor_copy(out=oT[par][0:65, :], in_=pO[0:65, :]), writes=[pOkey, ("oT", par)])

    def late():
        for i in range(4):
            T.op("pe", lambda e: e.transpose(pF[:, i, 0:65], oT[par][0:65, i * 128:(i + 1) * 128], identF[0:65, 0:65]),
                 reads=[("oT", par)], writes=[pFkey], sig=(i == 3))
        rcv = rc[:, par * 4:par * 4 + 4]
        T.op("dve", lambda e: e.reciprocal(out=rcv.unsqueeze(2), in_=pF[:, :, 64:65]), writes=[pFkey, ("rc", par)])
        T.op("dve", lambda e: e.tensor_tensor(out=otok[par][:, :, :], in0=pF[:, :, 0:64],
                                              in1=rcv.unsqueeze(2).to_broadcast([128, 4, 64]), op=ALU.mult),
             reads=[("rc", par)], writes=[pFkey, ("otok", par)])
        T.dma("sp", dst_ap, otok[par][:, :, :], reads=[("otok", par)], key=("otok_o", par))
    deferred.append(late)


def mixer_phase(nc, T, C):
    gcols, ident, identF, eps1, ones_bf = C["gcols"], C["ident"], C["identF"], C["eps1"], C["ones_bf"]
    h_scr, oscr = C["h_scr"], C["oscr"]
    with ExitStack() as es0:
        cqnT = _sb(nc, es0, "cqnT", [128, 2, S], BF16)
        ckvnT = _sb(nc, es0, "ckvnT", [128, S], BF16)
        krotT = _sb(nc, es0, "krotT", [96, S], BF16)
        rope = None
        with ExitStack() as esA:
            qT_na = [_sb(nc, esA, "qTna%d" % i, [128, S], BF16) for i in range(4)]
            kT_na = [_sb(nc, esA, "kTna%d" % i, [128, S], BF16) for i in range(4)]
            V_na = _sb(nc, esA, "Vna", [128, 32, 8, 66], BF16)
            T.op("pool", lambda e: e.memset(V_na[:, :, :, 64:66], 1.0), writes=["Vna1"])
            with ExitStack() as es:
                p2_proj(nc, T, C, es, cqnT, ckvnT, krotT, rope, qT_na, kT_na, V_na)
            T.barrier()
            with ExitStack() as es:
                p3_na(nc, T, C, es, qT_na, kT_na, V_na)
            T.barrier()
        with ExitStack() as es:
            p4_mla(nc, T, C, es, cqnT, ckvnT, krotT, rope)
        T.barrier()


B2 = 256
NB2 = S // B2
TP2 = B2 // 128


def p2_proj(nc, T, C, es, cqnT, ckvnT, krotT, rope, qT_na, kT_na, V_na):
    gcols, ident, eps1, ones_bf, h_scr = C["gcols"], C["ident"], C["eps1"], C["ones_bf"], C["h_scr"]
    Wna = _sb(nc, es, "Wna", [128, KC, 1536], BF16)
    Wlat = _sb(nc, es, "Wlat", [128, KC, 576], BF16)
    Sx = [_sb(nc, es, "p2Sx%d" % i, [128, D], F32) for i in range(2)]
    xn = [_sb(nc, es, "p2xn%d" % i, [128, D], BF16) for i in range(2)]
    xnT2 = [_sb(nc, es, "p2xnT%d" % i, [128, KC, B2], BF16) for i in range(2)]
    st = _sb(nc, es, "p2st", [128, 64], F32)
    sq = _sb(nc, es, "p2sq", [128, 2, 512], BF16)
    rs = _sb(nc, es, "p2rs", [128, B2], F32)
    tmp = _sb(nc, es, "p2tmp", [96, 2, B2], F32)
    ropeb = [_sb(nc, es, "p2rope%d" % i, [96, 2, B2], F32) for i in range(3)]
    pT2 = [_ps(nc, es, "p2pT%d" % i, [128, KC, 128], BF16) for i in range(2)]
    pA = [_ps(nc, es, "p2pA%d" % i, [128, BLK], F32) for i in range(2)]
    pC = [_ps(nc, es, "p2pC%d" % i, [128, BLK], F32) for i in range(2)]
    pSm = _ps(nc, es, "p2pS", [128, BLK], F32)
    pK1 = _ps(nc, es, "p2pK1", [128, BLK], F32)
    pK = [pC[1], pC[0]]
    pKkey = [("pC", 1), ("pC", 0)]
    CQB = [(pC[0], ("pC", 0)), (pC[1], ("pC", 1))]
    CKVB = [(pK1, ("pK", 1))]
    junk = sq[:, :, :].rearrange("p a b -> p (a b)")

    wna_v = C["w_na"].rearrange("(k p) f -> p k f", p=128)
    for j in range(3):
        T.dma("pool", Wna[:, :, j * 512:(j + 1) * 512], wna_v[:, :, j * 512:(j + 1) * 512],
              writes=[("Wna", j)], key=("Wgu", 0, j))
    T.dma("pool", Wlat[:, :, :], C["w_lat"].rearrange("(k p) f -> p k f", p=128), writes=["Wlat"], key=("Wgu", 0, 3))

    def norm_stage(b):
        blk = slice(b * B2, (b + 1) * B2)
        T.dma("sp", ropeb[b % 3][64:96, 0, :], C["cosT"][:, blk], writes=[("ropeb", b % 3, 0)])
        T.dma("sp", ropeb[b % 3][64:96, 1, :], C["sinT"][:, blk], writes=[("ropeb", b % 3, 1)])
        for i in range(TP2):
            t = b * TP2 + i
            sl = t % 2
            T.dma("sp", Sx[sl][:], h_scr[t * 128:(t + 1) * 128, :], writes=[("Sx", sl)])
            col = (b % 2) * 8 + i
            T.op("act", lambda e: e.activation(out=junk, in_=Sx[sl][:], func=AF.Square,
                                               scale=float(D ** -0.5), accum_out=st[:, col:col + 1]),
                 reads=[("Sx", sl)], writes=[("sq", 0), ("sq", 1), ("st", col)])
            T.op("act", lambda e: e.activation(out=st[:, col + 4:col + 5], in_=st[:, col:col + 1], func=AF.Sqrt,
                                               bias=eps1[:, 0:1], scale=1.0),
                 reads=[("st", col)], writes=[("st", col + 4)])
            T.op("dve", lambda e: e.reciprocal(out=st[:, col + 4:col + 5], in_=st[:, col + 4:col + 5]),
                 reads=[("st", col + 4)], writes=[("st", col + 4)])
            T.op("dve", lambda e: e.tensor_scalar(out=xn[i][:], in0=Sx[sl][:], scalar1=st[:, col + 4:col + 5],
                                                  scalar2=None, op0=ALU.mult),
                 reads=[("Sx", sl), ("st", col + 4)], writes=[("xn", i)])

    def transpose_stage(b):
        xnT = xnT2[b % 2]
        for i in range(TP2):
            pT = pT2[i % 2]
            for k in range(KC):
                T.op("pe", lambda e: e.transpose(pT[:, k, :], xn[i][:, k * 128:(k + 1) * 128], ident[:]),
                     reads=[("xn", i)], writes=[("pT", i % 2)], sig=(k == KC - 1))
            T.op("dve", lambda e: e.tensor_tensor(
                out=xnT[:, :, i * 128:(i + 1) * 128], in0=pT[:, :, :],
                in1=gcols[:, 8:16].unsqueeze(2).to_broadcast([128, KC, 128]), op=ALU.mult),
                writes=[("pT", i % 2), ("xnT", b % 2, i)])

    nev = [0]

    def evac(out, in_, reads, writes):
        nev[0] += 1
        if nev[0] % 2:
            T.op("act", lambda e: e.copy(out=out, in_=in_), reads=reads, writes=writes)
        else:
            T.op("dve", lambda e: e.tensor_copy(out=out, in_=in_), reads=reads, writes=writes)

    def lat_mm(b, nch, c0, banks):
        xnT = xnT2[b % 2]
        XR = [("xnT", b % 2, i) for i in range(TP2)]
        for ch in range(nch):
            pb, pkey = banks[ch]
            for k in range(KC):
                T.op("pe", lambda e: e.matmul(pb[:, 0:B2], lhsT=Wlat[:, k, c0 + ch * 128:c0 + (ch + 1) * 128],
                                              rhs=xnT[:, k, :], start=(k == 0), stop=(k == KC - 1)),
                     reads=["Wlat"] + XR, writes=[pkey], sig=(k == KC - 1))
            T.op("act", lambda e: e.activation(out=sq[:, ch, 0:B2], in_=pb[:, 0:B2], func=AF.Square),
                 writes=[pkey, ("sq", ch)])

    def lat_sum(b, nch, nfeat):
        for ch in range(nch):
            T.op("pe", lambda e: e.matmul(pSm[:, 0:B2], lhsT=ones_bf[:, :], rhs=sq[:, ch, 0:B2],
                                          start=(ch == 0), stop=(ch == nch - 1)),
                 reads=[("sq", ch)], writes=["pSm"], sig=(ch == nch - 1))
        T.op("act", lambda e: e.activation(out=rs[:], in_=pSm[:, 0:B2], func=AF.Sqrt, bias=eps1[:, 0:1],
                                           scale=1.0 / nfeat), writes=["pSm", "rs"])
        T.op("dve", lambda e: e.reciprocal(out=rs[:], in_=rs[:]), writes=["rs"])

    def lat_scale(b, nch, c0, gcol0, dstf, banks):
        blk = slice(b * B2, (b + 1) * B2)
        for ch in range(nch):
            pb, pkey = banks[ch]
            T.op("dve", lambda e: e.scalar_tensor_tensor(out=dstf(ch)[:, blk], in0=pb[:, 0:B2],
                                                         scalar=gcols[:, gcol0 + ch:gcol0 + ch + 1], in1=rs[:],
                                                         op0=ALU.mult, op1=ALU.mult),
                 reads=["rs"], writes=[pkey, ("lat", c0, ch)])

    def na_qk(b, fcs):
        blk = slice(b * B2, (b + 1) * B2)
        xnT = xnT2[b % 2]
        XR = [("xnT", b % 2, i) for i in range(TP2)]
        for fc in fcs:
            pa = pA[fc % 2]
            for k in range(KC):
                T.op("pe", lambda e: e.matmul(pa[:, 0:B2], lhsT=Wna[:, k, fc * 128:(fc + 1) * 128], rhs=xnT[:, k, :],
                                              start=(k == 0), stop=(k == KC - 1)),
                     reads=[("Wna", fc // 4)] + XR, writes=[("pA", fc % 2)], sig=(k == KC - 1))
            dstT = (qT_na if fc < 4 else kT_na)[fc % 4]
            evac(dstT[:, blk], pa[:, 0:B2], [], [("pA", fc % 2), ("qk", fc)])

    def na_v(b):
        xnT = xnT2[b % 2]
        for i in range(TP2):
            pa = pA[i % 2]
            for k in range(KC):
                T.op("pe", lambda e: e.matmul(pa[:], lhsT=xnT[:, k, i * 128:(i + 1) * 128], rhs=Wna[:, k, 1024:1536],
                                              start=(k == 0), stop=(k == KC - 1)),
                     reads=[("Wna", 2), ("xnT", b % 2, i)], writes=[("pA", i % 2)], sig=(k == KC - 1))
            evac(V_na[:, b * TP2 + i, :, 0:64], pa[:, :].rearrange("p (h d) -> p h d", h=8), [],
                 [("pA", i % 2), ("vna", i)])

    def k_rope(b):
        blk = slice(b * B2, (b + 1) * B2)
        xnT = xnT2[b % 2]
        XR = [("xnT", b % 2, i) for i in range(TP2)]
        for v in range(2):
            for k in range(KC):
                T.op("pe", lambda e: e.matmul(pK[v][0:96, 0:B2], lhsT=Wlat[:, k, 384 + v * 96:480 + v * 96],
                                              rhs=xnT[:, k, :], start=(k == 0), stop=(k == KC - 1)),
                     reads=["Wlat"] + XR, writes=[pKkey[v]], sig=(k == KC - 1))
            T.op("dve", lambda e: e.tensor_tensor(out=tmp[64:96, v, :], in0=pK[v][64:96, 0:B2],
                                                  in1=ropeb[b % 3][64:96, v, :], op=ALU.mult),
                 reads=[("ropeb", b % 3, v)], writes=[pKkey[v], ("tmp", v)])
        T.op("dve", lambda e: e.tensor_tensor(out=krotT[64:96, blk], in0=tmp[64:96, 0, :], in1=tmp[64:96, 1, :],
                                              op=ALU.add),
             reads=[("tmp", 0), ("tmp", 1)], writes=[("krot", b)])

    def proj_block(b):
        cq = lambda ch: cqnT[:, ch, :]
        ckv = lambda ch: ckvnT
        lat_mm(b, 2, 0, CQB)
        na_qk(b, range(0, 4))
        lat_sum(b, 2, 256.0)
        if b + 2 < NB2:
            norm_stage(b + 2)
        lat_mm(b, 1, 256, CKVB)
        na_qk(b, range(4, 6))
        lat_scale(b, 2, 0, 24, cq, CQB)
        na_qk(b, range(6, 8))
        na_v(b)
        lat_sum(b, 1, 128.0)
        k_rope(b)
        lat_scale(b, 1, 256, 26, ckv, CKVB)

    norm_stage(0)
    transpose_stage(0)
    norm_stage(1)
    transpose_stage(1)
    for b in range(NB2):
        proj_block(b)
        if b + 2 < NB2:
            transpose_stage(b + 2)


def p3_na(nc, T, C, es, qT_na, kT_na, V_na):
    identF, oscr, natab = C["identF"], C["oscr"], C["natab"]
    Eb = [_sb(nc, es, "Eb%d" % i, [128, TAB_W], BF16) for i in range(2)]
    stg = [_sb(nc, es, "ebstg%d" % i, [128, TQ], F32) for i in range(2)]
    PT2 = [_sb(nc, es, "p3PT%d" % i, [128, 2 * BLK], BF16) for i in range(3)]
    oT = [_sb(nc, es, "p3oT%d" % i, [65, BLK], F32) for i in range(2)]
    otok = [_sb(nc, es, "p3otok%d" % i, [128, 4, 64], F32) for i in range(2)]
    rc = _sb(nc, es, "p3rc", [128, 8], F32)
    qz = [_sb(nc, es, "p3qz%d" % i, [128, BLK], BF16) for i in range(2)]
    pS2 = [_ps(nc, es, "p3pS%d" % i, [128, 2 * BLK], F32) for i in range(2)]
    pO = [_ps(nc, es, "p3pO%d" % i, [128, BLK], F32) for i in range(2)]

    def q_prep(gi):
        h, g = divmod(gi, 8)
        fc, hp = h // 2, h % 2
        rows = slice(hp * 64, hp * 64 + 64)
        other = slice((1 - hp) * 64, (1 - hp) * 64 + 64)
        par = gi % 2
        if g < 2:
            T.op("pool", lambda e: e.memset(qz[par][other, :], 0.0), writes=[("qz", par)])
        T.op("pool", lambda e: e.tensor_copy(out=qz[par][rows, :], in_=qT_na[fc][rows, g * BLK:(g + 1) * BLK]),
             writes=[("qz", par)])
    pF = _ps(nc, es, "p3pF", [128, 4, 66], F32)

    def eb_dma(h, q):
        T.dma("sp", stg[q % 2][:], natab[h, :, q * TQ:(q + 1) * TQ], writes=[("stg", q % 2)])

    def eb_exp(h, q):
        T.op("act", lambda e: e.activation(out=Eb[h % 2][:, q * TQ:(q + 1) * TQ], in_=stg[q % 2][:], func=AF.Copy,
                                           scale=8.0),
             reads=[("stg", q % 2)], writes=[("Eb", h % 2, q)])

    def load_eb(h):
        for q in range(4):
            eb_dma(h, q)
            eb_exp(h, q)

    items = []
    gi = 0
    for h in range(8):
        fc, hp = h // 2, h % 2
        rows = slice(hp * 64, hp * 64 + 64)
        eb = Eb[h % 2]
        for g in range(8):
            if g == 0:
                chunks = [(2 * j, FIRST0 + j * 512) for j in range(6)]
            elif g == 7:
                chunks = [(52 + 2 * j, LAST0 + j * 512) for j in range(6)]
            else:
                chunks = [(8 * g - 4 + 2 * j, TAB0 + (14 - 2 * j) * 64) for j in range(8)]
            qblk = slice(g * BLK, (g + 1) * BLK)
            chs = []
            qneed = set()
            for (kr0, o) in chunks:
                ct = kr0 // 2

                def qk(e, ps, stop, c0, c1, ct=ct, fc=fc, par=gi % 2):
                    return e.matmul(ps, lhsT=kT_na[fc][:, ct * 128:(ct + 1) * 128], rhs=qz[par][:, c0:c1],
                                    start=True, stop=stop)
                chd = dict(qk=qk, v=V_na[:, ct, h, 0:65], mask=eb[:, o:o + 512])
                qneed.update(range(o // TQ, (o + 511) // TQ + 1))
                if NA_SUBRANGE and 1 <= g <= 6:
                    j = len(chs)
                    i0, i1 = max(0, 2 * j - 7), min(7, 2 * j + 1)
                    chd["cr"] = (i0 * 64, (i1 + 1) * 64)
                chs.append(chd)
            n_it = len(chs) // 2
            for a in range(n_it):
                items.append(dict(chunks=chs[2 * a:2 * a + 2], first=(a == 0), last=(a == n_it - 1),
                                  pO=pO[gi % 2], pOkey=("pO", gi % 2), qk_reads=[("qz", gi % 2)], v_reads=[], a=a,
                                  mask_reads=[("Eb", h % 2, q) for q in sorted(qneed)], h=h, g=g, gi=gi,
                                  newhead=(g == 0 and a == 0)))
            gi += 1

    load_eb(0)
    q_prep(0)

    def after_back(k, it, deferred):
        if it["gi"] == 0 and it["a"] == 0:
            deferred.append(lambda: (eb_dma(1, 0), eb_dma(1, 1)))
        if it["gi"] == 1 and it["a"] == 0:
            deferred.append(lambda: (eb_exp(1, 0), eb_exp(1, 1), eb_dma(1, 2), eb_dma(1, 3)))
        if it["gi"] == 3 and it["a"] == 0:
            deferred.append(lambda: (eb_exp(1, 2), eb_exp(1, 3)))
        if it["a"] == 0 and it["gi"] + 1 < 64:
            q_prep(it["gi"] + 1)
        if it["last"]:
            h, g, gi_ = it["h"], it["g"], it["gi"]
            dst = oscr[g * BLK:(g + 1) * BLK, h * 64:(h + 1) * 64].rearrange("(i p) d -> p i d", p=128)
            attn_finalize(T, it["pO"], it["pOkey"], pF, "pF", oT, otok, rc, identF, gi_ % 2, dst, deferred)
            if g == 5 and h + 2 < 8:
                deferred.append(lambda: (eb_dma(h + 2, 0), eb_dma(h + 2, 1)))
            if g == 0 and 1 <= h and h + 1 < 8:
                deferred.append(lambda: (eb_exp(h + 1, 0), eb_exp(h + 1, 1), eb_dma(h + 1, 2), eb_dma(h + 1, 3)))
            if g == 3 and 1 <= h and h + 1 < 8:
                deferred.append(lambda: (eb_exp(h + 1, 2), eb_exp(h + 1, 3)))

    attn_stream(T, items, pS2, PT2, None, 0.125, after_back, bias_ident=C["ident"])


def p4_mla(nc, T, C, es, cqnT, ckvnT, krotT, rope):
    identF, oscr = C["identF"], C["oscr"]
    Wuq = _sb(nc, es, "Wuq", [128, 2, 768], BF16)
    Wuqs = _sb(nc, es, "Wuqs", [128, 2, 768], BF16)
    Wukv = _sb(nc, es, "Wukv", [128, 1024], BF16)
    V_all = _sb(nc, es, "Vall", [128, 32, 8, 66], BF16)
    kT = [_sb(nc, es, "p4kT%d" % i, [96, S], BF16) for i in range(2)]
    qTb = [_sb(nc, es, "p4qT%d" % i, [96, BLK], BF16) for i in range(2)]
    tmp = _sb(nc, es, "p4tmp", [96, 2, BLK], F32)
    ropeb = [_sb(nc, es, "p4rope%d" % i, [96, 2, BLK], F32) for i in range(2)]
    PT2 = [_sb(nc, es, "p4PT%d" % i, [128, 2 * BLK], BF16) for i in range(3)]
    oT = [_sb(nc, es, "p4oT%d" % i, [65, BLK], F32) for i in range(2)]
    otok = [_sb(nc, es, "p4otok%d" % i, [128, 4, 64], F32) for i in range(2)]
    rc = _sb(nc, es, "p4rc", [128, 8], F32)
    pS2 = [_ps(nc, es, "p4pS%d" % i, [128, 2 * BLK], F32) for i in range(2)]
    pO = [_ps(nc, es, "p4pO%d" % i, [128, BLK], F32) for i in range(2)]
    pX = [_ps(nc, es, "p4pX%d" % i, [128, BLK], F32) for i in range(2)]
    pF = pX[1][:, 0:264].rearrange("p (i d) -> p i d", i=4)

    T.dma("pool", Wuq[:, :, :], C["w_uq"].rearrange("(k p) f -> p k f", p=128), writes=["Wuq"], key=("Wgu", 0, 0))
    T.dma("pool", Wuqs[:, :, :], C["w_uqs"].rearrange("(k p) f -> p k f", p=128), writes=["Wuqs"], key=("Wgu", 0, 1))
    T.dma("pool", Wukv[:, :], C["w_ukv"][:, :], writes=["Wukv"], key=("Wgu", 0, 2))
    T.op("pool", lambda e: e.memset(V_all[:, :, :, 64:66], 1.0), writes=["Vall1"])

    wv = Wukv[:, :].rearrange("p (h t d) -> p h t d", h=8, t=2)
    vbank = [(pX[0], ("pX", 0)), (pX[1], ("pX", 1)), (pO[0], ("pO", 0)), (pO[1], ("pO", 1))]
    for t in range(32):
        px, pkey = vbank[t % 4]
        T.op("pe", lambda e: e.matmul(px[:], lhsT=ckvnT[:, t * 128:(t + 1) * 128], rhs=wv[:, :, 1, :],
                                      start=True, stop=True), reads=["Wukv"], writes=[pkey])
        if t % 2:
            T.op("act", lambda e: e.copy(out=V_all[:, t, :, 0:64], in_=px[:, :].rearrange("p (h d) -> p h d", h=8)),
                 writes=[pkey, ("v_all", 1)])
        else:
            T.op("dve", lambda e: e.tensor_copy(out=V_all[:, t, :, 0:64], in_=px[:, :].rearrange("p (h d) -> p h d", h=8)),
                 writes=[pkey, ("v_all", 0)])

    def k_gen(h, prologue=False):
        kt = kT[h % 2]
        for b in range(NBLK):
            px, pkey = vbank[b % 4] if prologue else (pX[b % 2], ("pX", b % 2))
            blk = slice(b * BLK, (b + 1) * BLK)
            T.op("pe", lambda e: e.matmul(px[0:64, :], lhsT=Wukv[:, h * 128:h * 128 + 64], rhs=ckvnT[:, blk],
                                          start=True, stop=True), reads=["Wukv"], writes=[pkey])
            if prologue and b % 2:
                T.op("act", lambda e: e.copy(out=kt[0:64, blk], in_=px[0:64, :]), writes=[pkey, ("kTn", h % 2)])
            else:
                T.op("dve", lambda e: e.tensor_copy(out=kt[0:64, blk], in_=px[0:64, :]),
                     writes=[pkey, ("kTn", h % 2)])
        T.op("dve", lambda e: e.tensor_copy(out=kt[64:96, :], in_=krotT[64:96, :]), writes=[("kTr", h % 2)])

    def q_gen(h, g, par):
        blk = slice(g * BLK, (g + 1) * BLK)
        T.dma("sp", ropeb[par][64:96, 0, :], C["cosT"][:, blk], writes=[("ropeb", par, 0)])
        T.dma("sp", ropeb[par][64:96, 1, :], C["sinT"][:, blk], writes=[("ropeb", par, 1)])
        for v, W in ((0, Wuq), (1, Wuqs)):
            for kc in range(2):
                T.op("pe", lambda e: e.matmul(pX[v][0:96, :], lhsT=W[:, kc, h * 96:(h + 1) * 96], rhs=cqnT[:, kc, blk],
                                              start=(kc == 0), stop=(kc == 1)),
                     reads=["Wuq", "Wuqs"], writes=[("pX", v)], sig=(kc == 1))
        T.op("dve", lambda e: e.tensor_copy(out=qTb[par][0:64, :], in_=pX[0][0:64, :]),
             writes=[("pX", 0), ("qTbn", par)])
        for v in range(2):
            T.op("dve", lambda e: e.tensor_tensor(out=tmp[64:96, v, :], in0=pX[v][64:96, :], in1=ropeb[par][64:96, v, :],
                                                  op=ALU.mult),
                 reads=[("ropeb", par, v)], writes=[("pX", v), ("tmp", v)])
        T.op("dve", lambda e: e.tensor_tensor(out=qTb[par][64:96, :], in0=tmp[64:96, 0, :], in1=tmp[64:96, 1, :],
                                              op=ALU.add),
             reads=[("tmp", 0), ("tmp", 1)], writes=[("qTb", par)])

    sc = float(96 ** -0.5)
    groups = [(h, g) for h in range(8) for g in range(8)]
    items = []
    for gi, (h, g) in enumerate(groups):
        par = gi % 2
        kt = kT[h % 2]
        for a in range(16):
            chs = []
            for c in range(2):
                j = 2 * a + c

                def qk(e, ps, stop, c0, c1, j=j, kt=kt, par=par):
                    return e.matmul(ps, lhsT=kt[0:96, j * 128:(j + 1) * 128], rhs=qTb[par][0:96, c0:c1],
                                    start=True, stop=stop)
                chs.append(dict(qk=qk, v=V_all[:, j, h, 0:65], mask=None))
            items.append(dict(chunks=chs, first=(a == 0), last=(a == 15), pO=pO[gi % 2], pOkey=("pO", gi % 2),
                              qk_reads=[("kTn", h % 2), ("kTr", h % 2), ("qTb", par), ("qTbn", par)],
                              v_reads=[("v_all", 0), ("v_all", 1), "Vall1"], mask_reads=[], h=h, g=g, gi=gi, a=a))

    k_gen(0, prologue=True)
    q_gen(0, 0, 0)

    def after_back(k, it, deferred):
        gi, a = it["gi"], it["a"]
        if a == 3 and it["g"] == 0 and it["h"] + 1 < 8:
            k_gen(it["h"] + 1)
        if a == 10 and gi + 1 < len(groups):
            nh, ng = groups[gi + 1]
            q_gen(nh, ng, (gi + 1) % 2)
        if it["last"]:
            h, g = it["h"], it["g"]
            dst = oscr[g * BLK:(g + 1) * BLK, 512 + h * 64:512 + (h + 1) * 64].rearrange("(i p) d -> p i d", p=128)
            attn_finalize(T, it["pO"], it["pOkey"], pF, ("pX", 1), oT, otok, rc, identF, gi % 2, dst, deferred)

    attn_stream(T, items, pS2, PT2, None, sc, after_back)


def p5a_wout(nc, T, C, prefetch=None):
    gcols, ident, eps1, h_scr, oscr = C["gcols"], C["ident"], C["eps1"], C["h_scr"], C["oscr"]
    with ExitStack() as es:
        Wout = _sb(nc, es, "Wout", [128, KC, D], BF16)
        Sx = [_sb(nc, es, "p5Sx%d" % i, [128, D], F32) for i in range(4)]
        Rx = [_sb(nc, es, "p5Rx%d" % i, [128, D], F32) for i in range(4)]
        Tt = [_sb(nc, es, "p5Tt%d" % i, [128, D], F32) for i in range(4)]
        xn = [_sb(nc, es, "p5xn%d" % i, [128, D], BF16) for i in range(4)]
        xnT2 = [_sb(nc, es, "p5xnT%d" % i, [128, KC, BLK], BF16) for i in range(2)]
        gpost = _sb(nc, es, "p5gpost", [128, D], F32)
        junk = _sb(nc, es, "p5junk", [128, D], BF16)
        st = _sb(nc, es, "p5st", [128, 64], F32)
        pT2 = [_ps(nc, es, "p5pT%d" % i, [128, KC, 128], BF16) for i in range(2)]
        pD = [_ps(nc, es, "p5pD%d" % i, [128, BLK], F32) for i in range(4)]
        wv = C["w_out"].rearrange("(k p) d -> p k d", p=128)
        for j in range(2):
            T.dma("pool", Wout[:, :, j * 512:(j + 1) * 512], wv[:, :, j * 512:(j + 1) * 512],
                  writes=[("Wout", j)], key=("Wgu", 0, j))
        T.dma("sp", gpost[:], C["grows"][1:2, :].partition_broadcast(128), writes=["gpost"])
        if prefetch is not None:
            prefetch()

        def load_sx(t):
            if t < S // 128:
                T.dma("sp", Sx[t % 4][:], oscr[t * 128:(t + 1) * 128, :], writes=[("Sx", t % 4)])

        def norm_a(b, i):
            t = b * 4 + i
            sl = t % 4
            load_sx(t + 2)
            col = (b % 2) * 16 + i * 4
            for hf in range(2):
                hs = slice(hf * 512, (hf + 1) * 512)
                T.op("act", lambda e: e.activation(out=junk[:, hs], in_=Sx[sl][:, hs], func=AF.Square,
                                                   scale=float(512 ** -0.5), accum_out=st[:, col + hf:col + hf + 1]),
                     reads=[("Sx", sl)], writes=[("junk", hf), ("st", col + hf)])
            T.op("act", lambda e: e.activation(out=st[:, col + 2:col + 4], in_=st[:, col:col + 2], func=AF.Sqrt,
                                               bias=eps1[:, 0:1], scale=1.0),
                 reads=[("st", col), ("st", col + 1)], writes=[("st", col + 2), ("st", col + 3)])
            T.op("dve", lambda e: e.reciprocal(out=st[:, col + 2:col + 4], in_=st[:, col + 2:col + 4]),
                 writes=[("st", col + 2), ("st", col + 3)])

        def norm_b(b, i):
            t = b * 4 + i
            sl = t % 4
            col = (b % 2) * 16 + i * 4
            for hf in range(2):
                hs = slice(hf * 512, (hf + 1) * 512)
                T.op("act", lambda e: e.activation(out=xn[i][:, hs], in_=Sx[sl][:, hs], func=AF.Copy,
                                                   scale=st[:, col + 2 + hf:col + 3 + hf]),
                     reads=[("Sx", sl), ("st", col + 2 + hf)], writes=[("xn", i, hf)])

        def norm_stage(b):
            for i in range(4):
                norm_a(b, i)
                norm_b(b, i)

        def transpose_stage(b):
            for i in range(4):
                pT = pT2[i % 2]
                for k in range(KC):
                    T.op("pe", lambda e: e.transpose(pT[:, k, :], xn[i][:, k * 128:(k + 1) * 128], ident[:]),
                         reads=[("xn", i, k // 4)], writes=[("pT", i % 2)], sig=(k == KC - 1))
                T.op("dve", lambda e: e.tensor_tensor(
                    out=xnT2[b % 2][:, :, i * 128:(i + 1) * 128], in0=pT[:, :, :],
                    in1=gcols[:, 27:35].unsqueeze(2).to_broadcast([128, KC, 128]), op=ALU.mult),
                    writes=[("pT", i % 2), ("xnT", b % 2, i)])

        def out_tile(b, i):
            t = b * 4 + i
            sl = t % 4
            T.dma("sp", Rx[sl][:], h_scr[t * 128:(t + 1) * 128, :], writes=[("Rx", sl)])
            col = 32 + (t % 4) * 4
            for n in range(2):
                pdi = (i % 2) * 2 + n
                pd = pD[pdi]
                for k in range(KC):
                    T.op("pe", lambda e: e.matmul(pd[:], lhsT=xnT2[b % 2][:, k, i * 128:(i + 1) * 128],
                                                  rhs=Wout[:, k, n * 512:(n + 1) * 512],
                                                  start=(k == 0), stop=(k == KC - 1)),
                         reads=[("xnT", b % 2, i), ("Wout", n)], writes=[("pD", pdi)], sig=(k == KC - 1))
                T.op("act", lambda e: e.activation(out=junk[:, 0:512], in_=pd[:], func=AF.Square,
                                                   scale=float(D ** -0.5), accum_out=st[:, col + n:col + n + 1]),
                     writes=[("pD", pdi), ("junk", 0), ("st", col + n)])
                T.op("dve", lambda e: e.tensor_tensor(out=Tt[sl][:, n * 512:(n + 1) * 512], in0=pd[:],
                                                      in1=gpost[:, n * 512:(n + 1) * 512], op=ALU.mult),
                     reads=["gpost"], writes=[("pD", pdi), ("Tt", sl, n)])
            T.op("dve", lambda e: e.tensor_tensor(out=st[:, col + 2:col + 3], in0=st[:, col:col + 1],
                                                  in1=st[:, col + 1:col + 2], op=ALU.add),
                 reads=[("st", col), ("st", col + 1)], writes=[("st", col + 2)])
            T.op("act", lambda e: e.activation(out=st[:, col + 3:col + 4], in_=st[:, col + 2:col + 3], func=AF.Sqrt,
                                               bias=eps1[:, 0:1], scale=1.0),
                 reads=[("st", col + 2)], writes=[("st", col + 3)])
            T.op("dve", lambda e: e.reciprocal(out=st[:, col + 3:col + 4], in_=st[:, col + 3:col + 4]),
                 reads=[("st", col + 3)], writes=[("st", col + 3)])
            T.op("dve", lambda e: e.scalar_tensor_tensor(out=Rx[sl][:], in0=Tt[sl][:], scalar=st[:, col + 3:col + 4],
                                                         in1=Rx[sl][:], op0=ALU.mult, op1=ALU.add),
                 reads=[("Tt", sl, 0), ("Tt", sl, 1), ("st", col + 3), ("Rx", sl)], writes=[("Rx", sl)])
            T.dma("sp", h_scr[t * 128:(t + 1) * 128, :], Rx[sl][:], reads=[("Rx", sl)], key=("Rxo", sl))

        load_sx(0)
        load_sx(1)
        norm_stage(0)
        transpose_stage(0)
        norm_stage(1)
        transpose_stage(1)
        for b in range(NBLK):
            for i in range(4):
                if b + 2 < NBLK:
                    norm_a(b + 2, i)
                out_tile(b, i)
                if b + 2 < NBLK:
                    norm_b(b + 2, i)
            if b + 2 < NBLK:
                transpose_stage(b + 2)
        T.barrier()


PH_ALL = ("f1", "mix", "wout", "f2")


def build_program(phases=PH_ALL, dbg_out=None, dbg_h_from_x=False):
    nc = bass.Bass("TRN2", target_bir_lowering=False)
    dram_in = lambda n, shp, dt=F32: nc.dram_tensor(n, list(shp), dt, kind="ExternalInput").ap()
    C = {}
    x = dram_in("x", [S, D])
    for nm, shp in (("w_gu1", [D, 2 * DFF]), ("w_dn1", [DFF, D]), ("w_gu2", [D, 2 * DFF]), ("w_dn2", [DFF, D]),
                    ("w_na", [D, 1536]), ("w_lat", [D, 576]), ("w_uq", [256, 768]), ("w_uqs", [256, 768]),
                    ("w_ukv", [128, 1024]), ("w_out", [D, D]), ("gcols_d", [128, 40]), ("grows", [4, D]),
                    ("ident_d", [128, 128]), ("cosT", [32, S]), ("sinT", [32, S]), ("natab", [8, 128, TAB_W])):
        C[nm] = dram_in(nm, shp)
    out = nc.dram_tensor("out", [S, D], F32, kind="ExternalOutput").ap()
    if dbg_out == "oscr":
        C["h_scr"] = nc.dram_tensor("h_scr", [S, D], F32).ap()
        C["oscr"] = out
    elif dbg_out == "h_scr":
        C["h_scr"] = out
        C["oscr"] = nc.dram_tensor("oscr", [S, D], F32).ap()
    else:
        C["h_scr"] = nc.dram_tensor("h_scr", [S, D], F32).ap()
        C["oscr"] = nc.dram_tensor("oscr", [S, D], F32).ap()

    if dbg_h_from_x:
        C["h_scr"] = x
    with ExitStack() as es:
        T = Tracker(nc, es)
        gcols = _sb(nc, es, "sb_gcols", [128, 40], F32)
        ident = _sb(nc, es, "sb_ident", [128, 128], BF16)
        identF = _sb(nc, es, "sb_identF", [128, 128], F32)
        ones_bf = _sb(nc, es, "sb_ones", [128, 128], BF16)
        eps1 = _sb(nc, es, "eps1", [128, 1], F32)
        eps4 = _sb(nc, es, "eps4", [128, 1], F32)
        T.dma("sp", gcols[:], C["gcols_d"][:, :], writes=["gcols"])
        T.dma("pool", ident[:], C["ident_d"][:, :], writes=["ident"])
        T.dma("sp", identF[:], C["ident_d"][:, :], writes=["identF"])
        T.op("dve", lambda e: e.memset(eps1[:], EPS), writes=["eps1"])
        T.op("dve", lambda e: e.memset(eps4[:], 4 * EPS), writes=["eps4"])
        T.op("dve", lambda e: e.memset(ones_bf[:], 1.0), writes=["ones"])
        T.barrier()
        C.update(gcols=gcols, ident=ident, identF=identF, ones_bf=ones_bf, eps1=eps1, eps4=eps4)
        h_scr = C["h_scr"]
        if "f1" in phases:
            ffn_phase(nc, T, "f1", x, h_scr, C["w_gu1"], C["w_dn1"], gcols, 0, C["grows"][0:1, :], ident, eps1, eps4)
        if "mix" in phases:
            mixer_phase(nc, T, C)
        if "wout" in phases:
            p5a_wout(nc, T, C)
        if "f2" in phases:
            ffn_phase(nc, T, "f2", h_scr, out, C["w_gu2"], C["w_dn2"], gcols, 16, C["grows"][2:3, :], ident, eps1, eps4)
        T.barrier()
    return nc


def _na_bias_tables(rpb):
    H = rpb.shape[0]
    kl = np.arange(128) // 64
    kc = np.arange(128) % 64
    qc = np.arange(64)
    ws = np.clip(qc - 8, 0, 48)
    colv = (kc[:, None] >= ws[None, :]) & (kc[:, None] < ws[None, :] + 16)
    coff = np.clip(kc[:, None] - qc[None, :] + 15, 0, 30)
    tab = np.full((H, 128, TAB_W), NEG, np.float32)
    for e in range(22):
        dr = (10 - e) + kl
        rowv = (dr >= -4) & (dr <= 3)
        roff = np.clip(dr + 7, 0, 14)
        vals = rpb[:, roff[:, None], coff]
        m = rowv[:, None] & colv
        tab[:, :, TAB0 + e * 64:TAB0 + (e + 1) * 64] = np.where(m[None], vals, np.float32(NEG))
    for base, r0, k0 in ((FIRST0, 0, 0), (LAST0, 56, 52)):
        for j in range(6):
            for i in range(8):
                r = r0 + i
                rs = min(max(r - 4, 0), 56)
                kr = k0 + 2 * j + kl
                rowv = (kr >= rs) & (kr < rs + 8)
                roff = np.clip(kr - r + 7, 0, 14)
                vals = rpb[:, roff[:, None], coff]
                m = rowv[:, None] & colv
                o = base + j * 512 + i * 64
                tab[:, :, o:o + 64] = np.where(m[None], vals, np.float32(NEG))
    return tab


def _rope_tables():
    t = np.arange(S)
    row = (t // 64).astype(np.float64)
    col = (t % 64).astype(np.float64)
    inv = 1.0 / (10000.0 ** (np.arange(8, dtype=np.float64) / 8.0))
    ang = np.concatenate([row[:, None] * inv[None, :], col[:, None] * inv[None, :]], axis=-1)
    cos, sin = np.cos(ang), np.sin(ang)
    cosT = np.concatenate([cos.T, cos.T], axis=0).astype(np.float32)
    sinT = np.concatenate([-sin.T, sin.T], axis=0).astype(np.float32)
    return np.ascontiguousarray(cosT), np.ascontiguousarray(sinT)


def host_prep(inp):
    p = {k: np.asarray(v) for k, v in inp.items()}
    L = 0
    f = np.float32
    gcols = np.zeros((128, 40), f)
    gcols[:, 0:8] = p["ffn1_pre_g"][L].reshape(8, 128).T
    gcols[:, 8:16] = p["mix_pre_g"][L].reshape(8, 128).T
    gcols[:, 16:24] = p["ffn2_pre_g"][L].reshape(8, 128).T
    gcols[:, 24:26] = p["mla_q_norm_g"][L].reshape(2, 128).T
    gcols[:, 26:27] = p["mla_kv_norm_g"][L].reshape(1, 128).T
    gcols[:, 27:35] = np.concatenate([p["na_out_norm_g"][L], p["mla_out_norm_g"][L]]).reshape(8, 128).T
    grows = np.stack([p["ffn1_post_g"][L], p["mix_post_g"][L], p["ffn2_post_g"][L], p["ffn2_post_g"][L]]).astype(f)
    w_in = p["w_in"][L]
    kr = w_in[:, 1920:1952]
    krs = np.concatenate([kr[:, 16:32], kr[:, 0:16]], axis=1)
    z64 = np.zeros((D, 64), f)
    w_lat = np.concatenate([w_in[:, 1536:1920], z64, kr, z64, krs], axis=1)
    w_uq = p["mla_w_uq"][L]
    w_uqs = w_uq.reshape(256, 8, 96).copy()
    w_uqs[:, :, 64:80] = w_uq.reshape(256, 8, 96)[:, :, 80:96]
    w_uqs[:, :, 80:96] = w_uq.reshape(256, 8, 96)[:, :, 64:80]
    cosT, sinT = _rope_tables()
    c = np.ascontiguousarray
    shared = dict(
        w_gu1=c(p["ffn1_w_gu"][L]), w_dn1=c(p["ffn1_w_down"][L]), w_gu2=c(p["ffn2_w_gu"][L]), w_dn2=c(p["ffn2_w_down"][L]),
        w_na=c(w_in[:, 0:1536]), w_lat=c(w_lat.astype(f)), w_uq=c(w_uq), w_uqs=c(w_uqs.reshape(256, 768)),
        w_ukv=c(p["mla_w_ukv"][L]), w_out=c(p["w_out"][L]), gcols_d=gcols, grows=grows,
        ident_d=np.eye(128, dtype=f), cosT=cosT, sinT=sinT, natab=_na_bias_tables(p["na_rpb"][L].astype(f)),
    )
    return p, shared


def kernel(**inputs):
    p, shared = host_prep(inputs)
    nc = build_program()
    n = 8
    in_maps = [dict(shared, x=np.ascontiguousarray(p["x"][c], dtype=np.float32)) for c in range(n)]
    res = run_bass_kernel_spmd(nc, in_maps, core_ids=list(range(n)))
    return np.stack([np.asarray(r["out"]) for r in res.results], axis=0).astype(np.float32)
```
